# Optimizing a Trainium2 kernel written in Bass

```python
import jax, jax.numpy as jnp
from jax import lax
import numpy as np

D_MODEL = 1024
BATCH = 8
SEQ = 2048
DEPTH = 1

CHUNK = 64
PLE_DIM = 256
EPS = 1e-6
GMLP_GROUPS = 8
GMLP_GROUP_DIM = 128
GMLP_WIDTH = GMLP_GROUPS * GMLP_GROUP_DIM
GMLP_BLOCK = 128
FOX_HEADS = 16
FOX_HEAD_DIM = 64
FOX_WIDTH = FOX_HEADS * FOX_HEAD_DIM
Q_BLOCK = 128
D_FF = 2816
CONV_WIDTH = 3
N_BRANCH = 2
IN_COLS = 2 * GMLP_WIDTH + 3 * FOX_WIDTH + FOX_HEADS + N_BRANCH * D_MODEL

kernel_name = "hybrid_gmlp_fox_convffn_block"


def rmsnorm(x, g):
    x32 = x.astype(jnp.float32)
    y = x32 * lax.rsqrt(jnp.mean(x32 * x32, axis=-1, keepdims=True) + EPS)
    return (y * g.astype(jnp.float32)).astype(x.dtype)


def layernorm(x, g, b):
    x32 = x.astype(jnp.float32)
    mu = jnp.mean(x32, axis=-1, keepdims=True)
    xc = x32 - mu
    y = xc * lax.rsqrt(jnp.mean(xc * xc, axis=-1, keepdims=True) + EPS)
    return (y * g.astype(jnp.float32) + b.astype(jnp.float32)).astype(x.dtype)


def gmlp_spatial_gating(z_u, z_v, ln_g, ln_b, w_s, b_s):
    B, S, _ = z_u.shape
    v = layernorm(z_v, ln_g, ln_b)
    v = v.reshape(B, S // GMLP_BLOCK, GMLP_BLOCK, GMLP_GROUPS, GMLP_GROUP_DIM)
    pos = jnp.arange(GMLP_BLOCK)
    mask = (pos[None, :] // CHUNK) <= (pos[:, None] // CHUNK)
    w = jnp.where(mask[None], w_s, jnp.zeros_like(w_s))
    mixed = jnp.einsum('gts,bnsgc->bntgc', w, v) + b_s.T[None, None, :, :, None]
    return z_u * mixed.reshape(B, S, GMLP_WIDTH)


def forgetting_attention(q, k, v, f_logit, b_f):
    B, S, _ = q.shape
    def heads(t):
        return t.reshape(B, S, FOX_HEADS, FOX_HEAD_DIM).transpose(0, 2, 1, 3)
    q, k, v = heads(q), heads(k), heads(v)
    log_f = jax.nn.log_sigmoid(f_logit.astype(jnp.float32) + b_f.astype(jnp.float32))
    cum = jnp.cumsum(log_f, axis=1).transpose(0, 2, 1)
    scale = FOX_HEAD_DIM ** -0.5
    outs = []
    for i in range(S // Q_BLOCK):
        lo, hi = i * Q_BLOCK, (i + 1) * Q_BLOCK
        s = jnp.einsum('bhqd,bhkd->bhqk', q[:, :, lo:hi], k[:, :, :hi]).astype(jnp.float32) * scale
        s = s + cum[:, :, lo:hi, None] - cum[:, :, None, :hi]
        qpos = jnp.arange(lo, hi)
        kpos = jnp.arange(hi)
        s = jnp.where(kpos[None, :] <= qpos[:, None], s, -1e30)
        prob = jax.nn.softmax(s, axis=-1).astype(v.dtype)
        outs.append(jnp.einsum('bhqk,bhkd->bhqd', prob, v[:, :, :hi]))
    o = jnp.concatenate(outs, axis=2)
    return o.transpose(0, 2, 1, 3).reshape(B, S, FOX_WIDTH)


def causal_depthwise_conv(u, w, b):
    S = u.shape[1]
    up = jnp.pad(u, ((0, 0), (CONV_WIDTH - 1, 0), (0, 0)))
    out = b + w[0] * up[:, 0:S]
    for j in range(1, CONV_WIDTH):
        out = out + w[j] * up[:, j:j + S]
    return out


def setup_inputs(seed: int = 0) -> dict:
    key = jax.random.key(seed)
    ks = jax.random.split(key, 24)
    f32 = jnp.float32
    def nrm(k, shape, scale):
        return jax.random.normal(k, shape, f32) * scale
    L = DEPTH
    return {
        "x": nrm(ks[0], (BATCH, SEQ, D_MODEL), 1.0),
        "p": nrm(ks[1], (DEPTH, BATCH, SEQ, PLE_DIM), 1.0),
        "norm_mix_g": 1.0 + nrm(ks[2], (L, D_MODEL), 0.02),
        "w_in": nrm(ks[3], (L, D_MODEL, IN_COLS), D_MODEL ** -0.5),
        "b_f": 2.0 + nrm(ks[4], (L, FOX_HEADS), 0.5),
        "gmlp_ln_g": 1.0 + nrm(ks[5], (L, GMLP_WIDTH), 0.02),
        "gmlp_ln_b": nrm(ks[6], (L, GMLP_WIDTH), 0.02),
        "gmlp_w_s": nrm(ks[7], (L, GMLP_GROUPS, GMLP_BLOCK, GMLP_BLOCK), GMLP_BLOCK ** -0.5),
        "gmlp_b_s": 1.0 + nrm(ks[8], (L, GMLP_GROUPS, GMLP_BLOCK), 0.1),
        "w_branch_a": nrm(ks[9], (L, GMLP_WIDTH, D_MODEL), GMLP_WIDTH ** -0.5),
        "w_branch_b": nrm(ks[10], (L, FOX_WIDTH, D_MODEL), FOX_WIDTH ** -0.5),
        "w_out": nrm(ks[11], (L, D_MODEL, D_MODEL), D_MODEL ** -0.5),
        "norm_ffn_g": 1.0 + nrm(ks[12], (L, D_MODEL), 0.02),
        "w_up": nrm(ks[13], (L, D_MODEL, 2 * D_FF), D_MODEL ** -0.5),
        "conv_w": nrm(ks[14], (L, CONV_WIDTH, 2 * D_FF), CONV_WIDTH ** -0.5),
        "conv_b": nrm(ks[15], (L, 2 * D_FF), 0.02),
        "w_down": nrm(ks[16], (L, D_FF, D_MODEL), D_FF ** -0.5),
        "norm_ple_g": 1.0 + nrm(ks[17], (L, D_MODEL), 0.02),
        "w_ple": nrm(ks[18], (L, PLE_DIM, D_MODEL), PLE_DIM ** -0.5),
        "w_ple_gate": nrm(ks[19], (L, D_MODEL, D_MODEL), D_MODEL ** -0.5),
        "norm_final_g": 1.0 + nrm(ks[20], (D_MODEL,), 0.02),
    }


def reference(x, p, norm_mix_g, w_in, b_f, gmlp_ln_g, gmlp_ln_b, gmlp_w_s, gmlp_b_s,
              w_branch_a, w_branch_b, w_out, norm_ffn_g, w_up, conv_w, conv_b, w_down,
              norm_ple_g, w_ple, w_ple_gate, norm_final_g):
    o1 = 2 * GMLP_WIDTH
    o2 = o1 + 3 * FOX_WIDTH
    o3 = o2 + FOX_HEADS
    for i in range(DEPTH):
        h = rmsnorm(x, norm_mix_g[i])
        z = jnp.einsum('bsd,dc->bsc', h, w_in[i])
        uv = jax.nn.gelu(z[..., :o1])
        z_u, z_v = uv[..., :GMLP_WIDTH], uv[..., GMLP_WIDTH:]
        q = z[..., o1:o1 + FOX_WIDTH]
        k = z[..., o1 + FOX_WIDTH:o1 + 2 * FOX_WIDTH]
        v = z[..., o1 + 2 * FOX_WIDTH:o2]
        f_logit = z[..., o2:o3]
        gates = jax.nn.sigmoid(z[..., o3:])
        gate_a, gate_b = gates[..., :D_MODEL], gates[..., D_MODEL:]

        a = gmlp_spatial_gating(z_u, z_v, gmlp_ln_g[i], gmlp_ln_b[i], gmlp_w_s[i], gmlp_b_s[i])
        b = forgetting_attention(q, k, v, f_logit, b_f[i])
        y_a = jnp.einsum('bsc,cd->bsd', a, w_branch_a[i])
        y_b = jnp.einsum('bsc,cd->bsd', b, w_branch_b[i])
        merged = gate_a * y_a + gate_b * y_b
        x = x + jnp.einsum('bsd,de->bse', merged, w_out[i])

        h2 = rmsnorm(x, norm_ffn_g[i])
        up = jnp.einsum('bsd,df->bsf', h2, w_up[i])
        up = causal_depthwise_conv(up, conv_w[i], conv_b[i])
        act = jax.nn.gelu(up[..., :D_FF]) * up[..., D_FF:]
        x = x + jnp.einsum('bsf,fd->bsd', act, w_down[i])

        h3 = rmsnorm(x, norm_ple_g[i])
        ple = jnp.einsum('bse,ed->bsd', p[i], w_ple[i])
        x = x + ple * jax.nn.sigmoid(jnp.einsum('bsd,de->bse', h3, w_ple_gate[i]))
    return rmsnorm(x, norm_final_g)
```

```python
import contextlib
import numpy as np
import concourse.bass as bass
import concourse.mybir as mybir
from concourse.bass_utils import run_bass_kernel_spmd

F32 = mybir.dt.float32
BF16 = mybir.dt.bfloat16
AF = mybir.ActivationFunctionType
ALU = mybir.AluOpType

ENGS = ("pe", "act", "dve", "pool", "sp")

S = 2048
D = 1024
NB = 16
NT = 4
DFF = 2816
NFC = 22
IN_COLS = 7184
O_Q = 2048
O_K = 3072
O_V = 4096
O_F = 5120
O_G = 5136
PLE = 256
EPS = 1e-6
KB = 1024
BASE = 16512
import os
_STG = int(os.environ.get('KDBG_STG', '6'))
_NBA = int(os.environ.get('KDBG_NBA', '16'))


class _Op:
    __slots__ = ("fn", "deps", "dma", "flag", "semval", "dsem", "dcount", "waits", "prewait")

    def __init__(self, fn, deps, dma):
        self.fn = fn
        self.deps = deps
        self.dma = dma
        self.flag = False
        self.semval = None
        self.dsem = None
        self.dcount = None
        self.waits = []
        self.prewait = None


class Prog:
    def __init__(self, nc, n_dma_sems=32):
        self.nc = nc
        self.streams = {e: [] for e in ENGS}
        self.last_writer = {}
        self.readers = {}
        self.n_dma_sems = n_dma_sems

    def op(self, eng, fn, reads=(), writes=(), dma=False, extra=None):
        st = self.streams[eng]
        me = (eng, len(st))
        deps = {}
        for t in reads:
            w = self.last_writer.get(t)
            if w is not None:
                deps[w] = True
        for t in writes:
            w = self.last_writer.get(t)
            if w is not None and w not in deps:
                deps[w] = False
            for r in self.readers.get(t, ()):
                if r not in deps:
                    deps[r] = False
        if extra:
            for d in extra:
                deps[d] = True
        deps.pop(me, None)
        for t in writes:
            self.last_writer[t] = me
            self.readers[t] = []
        for t in reads:
            if self.last_writer.get(t) != me:
                self.readers.setdefault(t, []).append(me)
        st.append(_Op(fn, deps, dma))
        return me

    def dma(self, eng, out, in_, reads=(), writes=()):
        return self.op(eng, lambda e: e.dma_start(out=out, in_=in_), reads, writes, dma=True)

    def barrier(self):
        lasts = []
        for e in ENGS:
            st = self.streams[e]
            for i in range(len(st) - 1, -1, -1):
                if not st[i].dma:
                    lasts.append((e, i))
                    break
            n = 0
            for i in range(len(st) - 1, -1, -1):
                if st[i].dma:
                    lasts.append((e, i))
                    n += 1
                    if n >= 16:
                        break
        for e in ENGS:
            self.op(e, lambda g: g.nop(), extra=[d for d in lasts])

    def resolve(self):
        streams = self.streams
        dma_engs = [e for e in ENGS if any(o.dma for o in streams[e])]
        pools = {}
        if dma_engs:
            per = max(2, self.n_dma_sems // len(dma_engs))
            for e in dma_engs:
                pools[e] = min(per, 16)
        self.dma_pool_sizes = pools
        for e in dma_engs:
            k = 0
            counts = [0] * pools[e]
            lastop = [None] * pools[e]
            for i, o in enumerate(streams[e]):
                if not o.dma:
                    continue
                s = k % pools[e]
                k += 1
                if lastop[s] is not None:
                    o.prewait = ((e, s), counts[s])
                counts[s] += 16
                o.dsem = (e, s)
                o.dcount = counts[s]
                lastop[s] = i
        for e in ENGS:
            seen = {f: -1 for f in ENGS}
            seen_d = {}
            for i, o in enumerate(streams[e]):
                if o.prewait is not None:
                    key, cnt = o.prewait
                    if seen_d.get(key, 0) >= cnt:
                        o.prewait = None
                    else:
                        seen_d[key] = cnt
                for (f, n), is_raw in sorted(o.deps.items(), key=lambda kv: (kv[0][0], -kv[0][1])):
                    p = streams[f][n]
                    if p.dma:
                        if seen_d.get(p.dsem, 0) >= p.dcount:
                            continue
                        seen_d[p.dsem] = p.dcount
                        o.waits.append(("d", p.dsem, p.dcount))
                        continue
                    if f == e and not o.dma:
                        if not (is_raw and e != "pe"):
                            continue
                    if n <= seen[f]:
                        continue
                    seen[f] = n
                    p.flag = True
                    o.waits.append(("c", f, n))
        for e in ENGS:
            c = 0
            for o in streams[e]:
                if o.flag:
                    c += 1
                    o.semval = c


def emit(nc, prog):
    prog.resolve()
    with contextlib.ExitStack() as es:
        csem = {e: es.enter_context(nc.semaphore("c_" + e)) for e in ENGS}
        dsem = {}
        for e, n in prog.dma_pool_sizes.items():
            for s in range(n):
                dsem[(e, s)] = es.enter_context(nc.semaphore("d_%s_%d" % (e, s)))
        block = es.enter_context(nc.Block())

        def run(ename, eng):
            for o in prog.streams[ename]:
                if o.prewait is not None:
                    key, cnt = o.prewait
                    eng.wait_ge(dsem[key], cnt)
                for w in o.waits:
                    if w[0] == "d":
                        eng.wait_ge(dsem[w[1]], w[2])
                    else:
                        eng.wait_ge(csem[w[1]], prog.streams[w[1]][w[2]].semval)
                ins = o.fn(eng)
                if o.dma:
                    ins.then_inc(dsem[o.dsem], 16)
                elif o.flag:
                    ins.then_inc(csem[ename], 1)

        @block.tensor
        def _(eng):
            run("pe", eng)

        @block.scalar
        def _(eng):
            run("act", eng)

        @block.vector
        def _(eng):
            run("dve", eng)

        @block.gpsimd
        def _(eng):
            run("pool", eng)

        @block.sync
        def _(eng):
            run("sp", eng)


class _Stop(Exception):
    pass


def build_nc(debug=False, upto=None):
    nc = bass.Bass("TRN2", target_bir_lowering=False)

    def din(name, shape):
        return nc.dram_tensor(name, list(shape), F32, kind="ExternalInput").ap()

    x = din("x", [S, D])
    pin = din("p", [S, PLE])
    norm_mix_g = din("norm_mix_g", [D])
    w_in = din("w_in", [D, IN_COLS])
    b_f = din("b_f", [16])
    ln_g = din("gmlp_ln_g", [D])
    ln_b = din("gmlp_ln_b", [D])
    ws_t = din("ws_t", [128, 8, 128])
    bs_t = din("bs_t", [128, 8])
    w_a = din("w_branch_a", [D, D])
    w_b = din("w_branch_b", [D, D])
    w_out = din("w_out", [D, D])
    norm_ffn_g = din("norm_ffn_g", [D])
    w_up = din("w_up", [D, 2 * DFF])
    cw_t = din("cw_t", [128, 44, 3])
    cb_t = din("cb_t", [128, 44])
    w_down = din("w_down", [DFF, D])
    norm_ple_g = din("norm_ple_g", [D])
    w_ple = din("w_ple", [PLE, D])
    w_pg = din("w_ple_gate", [D, D])
    norm_final_g = din("norm_final_g", [D])
    c_ident = din("c_ident", [128, 128])
    c_tri = din("c_tri", [128, 128])
    c_sel = din("c_sel", [128, 128])
    out = nc.dram_tensor("out", [S, D], F32, kind="ExternalOutput").ap()
    dbg = {}
    if debug:
        for nm in ("hT", "bT", "aT", "mg"):
            dbg[nm] = nc.dram_tensor("dbg_" + nm, [128, 8, S], BF16, kind="ExternalOutput").ap()
        for nm in ("x1", "x2"):
            dbg[nm] = nc.dram_tensor("dbg_" + nm, [128, NB, D], F32, kind="ExternalOutput").ap()

    def dump(nm, src):
        if debug:
            P.dma("sp", dbg[nm], src)
            P.barrier()

    cnt = [0]

    def sb(off, shape, dt):
        cnt[0] += 1
        return nc.alloc_sbuf_tensor_at("t%d" % cnt[0], list(shape), dt, offset=off)

    C0 = BASE
    ident = sb(C0 + 0, [128, 128], BF16)
    onesf = sb(C0 + 256, [128, 128], F32)
    trif = sb(C0 + 768, [128, 128], F32)
    self_ = sb(C0 + 1280, [128, 128], F32)
    maskb = sb(C0 + 1792, [128, 128], BF16)
    cw = sb(C0 + 2048, [128, 44, 3], F32)
    cb = sb(C0 + 2592, [128, 44], F32)
    bs = sb(C0 + 2784, [128, 8], F32)
    bfb = sb(C0 + 2816, [128, 16], F32)
    nh = sb(C0 + 2880, [128, 16], F32)
    stat = sb(C0 + 2944, [128, 4, 3, 16], F32)
    lnst = sb(C0 + 3712, [128, 16, 6], F32)
    G0 = C0 + 4 * KB
    HT = C0 + 36 * KB
    BT = C0 + 68 * KB
    AT = C0 + 100 * KB
    SCR = C0 + 132 * KB
    SCR_END = 229344

    g0 = sb(G0, [128, 8, S], BF16)
    hT = sb(HT, [128, 8, S], BF16)
    bT = sb(BT, [128, 8, S], BF16)
    aT = sb(AT, [128, 8, S], BF16)
    x1 = sb(HT, [128, NB, D], F32)
    h2T = sb(AT, [128, 8, S], BF16)

    es = contextlib.ExitStack()
    ps = [es.enter_context(nc.psum_tensor("ps%d" % i, [128, 512], F32)) for i in range(8)]

    def psb(i):
        return ps[i][:].bitcast(BF16).rearrange("p (c n) -> p c n", c=8)

    P = Prog(nc)

    def mm(o, lhsT, rhs, start, stop, reads, writes, skip=False):
        if skip:
            P.op("pe", lambda e: e.matmul(o, lhsT=lhsT, rhs=rhs, start=start, stop=stop,
                                           skip_group_check=True), reads, writes)
        else:
            P.op("pe", lambda e: e.matmul(o, lhsT=lhsT, rhs=rhs, start=start, stop=stop), reads, writes)

    def wsrc(w, c0, c1):
        return w[:, c0:c1].rearrange("(c p) n -> p c n", p=128)

    try:
        P.dma("pool", ident[:], c_ident, writes=["ident"])
        P.dma("sp", trif[:], c_tri, writes=["trif"])
        P.dma("sp", self_[:], c_sel, writes=["self"])
        P.dma("pool", maskb[:], c_tri, writes=["maskb"])
        P.dma("sp", cw[:], cw_t, writes=["cw"])
        P.dma("sp", cb[:], cb_t, writes=["cb"])
        P.dma("sp", bs[:], bs_t, writes=["bs"])
        P.dma("sp", bfb[:], b_f.partition_broadcast(128), writes=["bfb"])
        P.op("pool", lambda e: e.memset(onesf[:], 1.0), writes=["onesf"])
        P.op("pool", lambda e: e.memset(nh[:], -0.5), writes=["nh"])
        P.op("pool", lambda e: e.memset(stat[:], 0.0), writes=["stat"])
        P.op("pool", lambda e: e.memset(lnst[:], 0.0), writes=["lnst"])

        if upto == 'c':
            raise _Stop()
        def tr8(src, src_tok, bank, dstT, dst_tok, nch=8):
            for c in range(nch):
                bk = bank + c // 4
                P.op("pe", lambda e, c=c, bk=bk: e.matmul(ps[bk][:, (c % 4) * 128:(c % 4 + 1) * 128],
                                                          lhsT=src[:, c * 128:(c + 1) * 128], rhs=ident[:],
                                                          start=True, stop=True),
                     reads=[src_tok, "ident"], writes=[("ps", bk)])
            for hf in range((nch + 3) // 4):
                n = min(4, nch - 4 * hf)
                pvv = ps[bank + hf][:, 0:n * 128].rearrange("p (c n) -> p c n", c=n)
                P.op("act", lambda e, hf=hf, n=n, pvv=pvv: e.copy(out=dstT[:, 4 * hf:4 * hf + n, :], in_=pvv),
                     reads=[("ps", bank + hf)], writes=[dst_tok])

        def norm_block(ni, b, src, gv, hb, sqj, bank, dstT, src_tok, hb_tok, dst_tok, extra_tok=()):
            ssc = stat[:, ni, 0, b:b + 1]
            msc = stat[:, ni, 1, b:b + 1]
            rsc = stat[:, ni, 2, b:b + 1]
            st = ("stat", ni, b)
            P.op("act", lambda e: e.activation(out=sqj, in_=src, func=AF.Square, accum_out=ssc),
                 reads=[src_tok, "stat"] + list(extra_tok), writes=["sqj", st])
            if _STG < 2:
                return
            P.op("dve", lambda e: e.tensor_scalar(out=msc, in0=ssc, scalar1=1.0 / D, scalar2=EPS,
                                                   op0=ALU.mult, op1=ALU.add), reads=[st], writes=[st])
            if _STG < 3:
                return
            P.op("pool", lambda e: e.tensor_tensor(out=rsc, in0=msc, in1=nh[:, 0:1], op=ALU.pow),
                 reads=[st, "nh"], writes=[st])
            if _STG < 4:
                return
            P.op("dve", lambda e: e.scalar_tensor_tensor(out=hb, in0=src, scalar=rsc, in1=gv,
                                                          op0=ALU.mult, op1=ALU.mult),
                 reads=[src_tok, st, "gv"], writes=[hb_tok])
            if _STG < 5:
                return
            tr8(hb, hb_tok, bank, dstT, dst_tok)

        xt = [sb(SCR + 0, [128, D], F32), sb(SCR + 4096, [128, D], F32)]
        hbA = [sb(SCR + 8192, [128, D], BF16), sb(SCR + 10240, [128, D], BF16)]
        gvA = sb(SCR + 12288, [128, D], F32)
        sqjA = sb(SCR + 16384, [128, D], F32)
        P.dma("sp", gvA[:], norm_mix_g.partition_broadcast(128), writes=["gv"])
        Vs = sb(G0, [128, 4, NB, 192], BF16)
        Wv = sb(G0 + 24576, [128, 8, 512], BF16)
        oB = SCR + 20480
        wf = sb(oB, [128, 8, 16], BF16)
        wq = [sb(oB + 29760, [128, 8, 128], BF16), sb(oB + 29760 + 2048, [128, 8, 128], BF16)]
        wk = [sb(oB + 33856, [128, 8, 128], BF16), sb(oB + 33856 + 2048, [128, 8, 128], BF16)]
        P.dma("pool", wf[:], wsrc(w_in, O_F, O_F + 16), writes=["wf"])
        P.dma("pool", Wv[:], wsrc(w_in, O_V, O_V + 512), writes=["Wv"])
        P.dma("pool", wq[0][:], wsrc(w_in, O_Q, O_Q + 128), writes=[("wq", 0)])
        P.dma("pool", wk[0][:], wsrc(w_in, O_K, O_K + 128), writes=[("wk", 0)])
        if upto == 'p':
            raise _Stop()
        for b in range(_NBA):
            P.dma("sp", xt[b % 2][:], x[b * 128:(b + 1) * 128, :], writes=[("xt", b % 2)])
            norm_block(0, b, xt[b % 2][:], gvA[:], hbA[b % 2][:], sqjA[:], 2 * (b % 2),
                       hT[:, :, b * 128:(b + 1) * 128], ("xt", b % 2), ("hbA", b % 2), ("hT", b))

        if upto == 'A':
            raise _Stop()
        o = SCR + 20480
        o += 256
        Lf = sb(o, [128, NB, 16], F32); o += 1024
        Sacc = sb(o, [128, NB + 1, 16], F32); o += 1088
        Cc = sb(o, [128, NB, 16], F32); o += 1024
        Mm = sb(o, [128, NB, 16], F32); o += 1024
        xf = sb(o, [128, 2, 16], F32); o += 128
        ef = sb(o, [128, 2, 16], F32); o += 128
        o = (o + 63) // 64 * 64
        NPAIR = NB * (NB + 1) // 2
        biasT = sb(o, [128, NPAIR, 16], F32); o += NPAIR * 64
        qT = [sb(o, [128, S], BF16), sb(o + 4096, [128, S], BF16)]; o += 8192
        kT = [sb(o, [128, S], BF16), sb(o + 4096, [128, S], BF16)]; o += 8192
        assert o == oB + 29760, (o, oB)
        o += 8192
        NPT = 4
        Pt = [sb(o + i * 1024, [128, 512], BF16) for i in range(NPT)]; o += NPT * 1024
        rl = [sb(o, [128, 512], F32), sb(o + 2048, [128, 512], F32)]; o += 4096
        bc = [sb(o, [128, 512], F32), sb(o + 2048, [128, 512], F32)]; o += 4096
        assert o <= SCR_END, o

        def pidx(qb, kb):
            return qb * (qb + 1) // 2 + kb

        oA = AT
        tri_b = sb(oA, [128, 128], BF16); oA += 256
        sel_b = sb(oA, [128, 128], BF16); oA += 256
        ones_b = sb(oA, [128, 128], BF16); oA += 256
        Lp = [sb(oA + i * 512, [128, NB, 16], BF16) for i in range(3)]; oA += 1536
        Sp = [sb(oA + i * 544, [128, NB + 1, 16], BF16) for i in range(3)]; oA += 1632
        RL = [sb(oA + i * 1024, [128, NB, 16], F32) for i in range(2)]; oA += 2048
        RS = [sb(oA + i * 1088, [128, NB + 1, 16], F32) for i in range(2)]; oA += 2176
        rlp = [[sb(oA + (f_ * 2 + i) * 1024, [128, 512], BF16) for i in range(2)] for f_ in range(2)]; oA += 4096
        rlr = [sb(oA + f_ * 2048, [128, 512], F32) for f_ in range(2)]; oA += 4096
        P.dma("pool", tri_b[:], c_tri, writes=["tri_b"])
        P.dma("pool", sel_b[:], c_sel, writes=["sel_b"])
        P.op("pool", lambda e: e.memset(ones_b[:], 1.0), writes=["ones_b"])
        for i in range(3):
            P.op("pool", lambda e, i=i: e.memset(Sp[i][:, 0, :], 0.0), writes=[("Sp", 0, i)])

        def split(src, src_tok, pieces, tmps, key):
            cur, cur_tok = src, src_tok
            for i, pc in enumerate(pieces):
                P.op("dve", lambda e, pc=pc, cur=cur: e.tensor_copy(out=pc, in_=cur), reads=[cur_tok], writes=[key + (i,)])
                if i < len(pieces) - 1:
                    P.op("dve", lambda e, pc=pc, cur=cur, t=tmps[i]: e.tensor_tensor(out=t, in0=cur, in1=pc, op=ALU.subtract),
                         reads=[cur_tok, key + (i,)], writes=[key + ("r", i)])
                    cur, cur_tok = tmps[i], key + ("r", i)

        if not os.environ.get('KDBG_NOMS'):
            P.op("dve", lambda e: e.memset(Vs[:, :, :, 65:128], 0.0), writes=["Vs_c0"])
            P.op("dve", lambda e: e.memset(Vs[:, :, :, 64:65], 1.0), writes=["Vs_c"])
        P.op("pool", lambda e: e.memset(Sacc[:, 0, :], 0.0), writes=[("Sacc", 0)])

        for b in range(NB):
            for kc in range(8):
                mm(ps[2][:, 0:16], hT[:, kc, b * 128:(b + 1) * 128], wf[:, kc, :], kc == 0, kc == 7,
                   [("hT", b), "wf"], [("ps", 2)])
            P.op("dve", lambda e, b=b: e.tensor_tensor(out=xf[:, b % 2, :], in0=ps[2][:, 0:16], in1=bfb[:], op=ALU.add),
                 reads=[("ps", 2), "bfb"], writes=[("xf", b % 2)])
            P.op("act", lambda e, b=b: e.activation(out=ef[:, b % 2, :], in_=xf[:, b % 2, :], func=AF.Exp, scale=-1.0),
                 reads=[("xf", b % 2)], writes=[("ef", b % 2)])
            P.op("act", lambda e, b=b: e.activation(out=Lf[:, b, :], in_=ef[:, b % 2, :], func=AF.Ln, bias=onesf[:, 0:1]),
                 reads=[("ef", b % 2), "onesf"], writes=[("Lf", b)])
            P.op("dve", lambda e, b=b: e.tensor_tensor(out=Sacc[:, b + 1, :], in0=Sacc[:, b, :], in1=Lf[:, b, :], op=ALU.add),
                 reads=[("Sacc", b), ("Lf", b)], writes=[("Sacc", b + 1)])
            split(Lf[:, b, :], ("Lf", b), [Lp[i][:, b, :] for i in range(3)], [RL[i][:, b, :] for i in range(2)], ("Lp", b))
            split(Sacc[:, b + 1, :], ("Sacc", b + 1), [Sp[i][:, b + 1, :] for i in range(3)],
                  [RS[i][:, b + 1, :] for i in range(2)], ("Sp", b + 1))
        for b in range(NB):
            for (cols, mat, mtok) in ((slice(0, 16), tri_b, "tri_b"), (slice(16, 32), sel_b, "sel_b")):
                for i in range(3):
                    mm(ps[3][:, cols], mat[:], Lp[i][:, b, :], i == 0, False, [mtok, ("Lp", b, i)], [("ps", 3)])
                for i in range(3):
                    mm(ps[3][:, cols], ones_b[:], Sp[i][:, b, :], False, i == 2, ["ones_b", ("Sp", b, i)], [("ps", 3)])
            P.op("dve", lambda e, b=b: e.tensor_copy(out=Cc[:, b, :], in_=ps[3][:, 0:16]), reads=[("ps", 3)], writes=[("Cc", b)])
            P.op("dve", lambda e, b=b: e.tensor_copy(out=Mm[:, b, :], in_=ps[3][:, 16:32]), reads=[("ps", 3)], writes=[("Mm", b)])
        for qb in range(NB):
            for kb in range(qb + 1):
                P.op("pool", lambda e, qb=qb, kb=kb: e.tensor_tensor(out=biasT[:, pidx(qb, kb), :], in0=Cc[:, kb, :],
                                                                     in1=Mm[:, qb, :], op=ALU.subtract),
                     reads=[("Cc", kb), ("Mm", qb)], writes=[("biasT", qb, kb)])

        _KB = int(os.environ.get('KDBG_B', '9'))
        if _KB < 2:
            raise _Stop()
        uid = [0]
        for gh in range(2):
            for b in range(NB):
                bk = int(os.environ.get('KDBG_VB', '4')) + (b % 2)
                for kc in range(8):
                    mm(ps[bk][:], hT[:, kc, b * 128:(b + 1) * 128], Wv[:, kc, :], kc == 0, kc == 7,
                       [("hT", b), "Wv"], [("ps", bk)])
                pv = ps[bk][:].rearrange("p (a n) -> p a n", a=4)
                _vc = int(os.environ.get('KDBG_VC', '2'))
                if _vc >= 1:
                    P.op("act", lambda e, b=b, pv=pv: e.copy(out=Vs[:, :, b, 0:64], in_=pv[:, :, 0:64]),
                         reads=[("ps", bk)], writes=[("Vs", b)])
                if _vc >= 2:
                    P.op("act", lambda e, b=b, pv=pv: e.copy(out=Vs[:, :, b, 128:192], in_=pv[:, :, 64:128]),
                         reads=[("ps", bk)], writes=[("VsB", b)])
            if _KB < 3:
                raise _Stop()
            if gh == 0:
                P.dma("pool", Wv[:], wsrc(w_in, O_V + 512, O_V + 1024), writes=["Wv"])
            for pr in range(4):
                c = gh * 4 + pr
                sl = c % 2
                for j in range(NT):
                    for (wt, dst, tok, bk) in ((wq[sl], qT[sl], "qT", 0), (wk[sl], kT[sl], "kT", 1)):
                        for kc in range(8):
                            mm(ps[bk][:], wt[:, kc, :], hT[:, kc, j * 512:(j + 1) * 512], kc == 0, kc == 7,
                               [("hT", 4 * j), ("hT", 4 * j + 1), ("hT", 4 * j + 2), ("hT", 4 * j + 3),
                                ("wq" if tok == "qT" else "wk", sl)], [("ps", bk)])
                        P.op("dve", lambda e, dst=dst, bk=bk, j=j: e.tensor_copy(out=dst[:, j * 512:(j + 1) * 512], in_=ps[bk][:]),
                             reads=[("ps", bk)], writes=[(tok, sl, j)])
                if _KB < 4:
                    raise _Stop()
                if c + 1 < 8:
                    c1 = c + 1
                    P.dma("pool", wq[c1 % 2][:], wsrc(w_in, O_Q + c1 * 128, O_Q + (c1 + 1) * 128), writes=[("wq", c1 % 2)])
                    P.dma("pool", wk[c1 % 2][:], wsrc(w_in, O_K + c1 * 128, O_K + (c1 + 1) * 128), writes=[("wk", c1 % 2)])
                for hh in range(2):
                    h = 2 * c + hh
                    r0 = 64 * hh
                    lrow = 64 if hh == 0 else 0
                    vsl = slice(0, 65) if hh == 0 else slice(64, 192)
                    for j in range(NT):
                        ob = 5 + (uid[0] % 2)
                        fin = uid[0] % 2
                        uid[0] += 1
                        nk = 4 * j + 4
                        for kb in range(nk):
                            i = kb - 4 * j
                            q0 = max(0, i) * 128
                            sbk = 2 + (kb % 3)
                            pt = Pt[kb % NPT]
                            ptok = ("Pt", kb % NPT)
                            mm(ps[sbk][:, q0:512], kT[sl][r0:r0 + 64, kb * 128:(kb + 1) * 128],
                               qT[sl][r0:r0 + 64, j * 512 + q0:(j + 1) * 512], True, True,
                               [("kT", sl, kb // 4), ("qT", sl, j)], [("ps", sbk)])
                            for sbi in range(q0 // 128, 4):
                                qb = 4 * j + sbi
                                P.op("act", lambda e, sbk=sbk, sbi=sbi, pt=pt, qb=qb, kb=kb, h=h: e.activation(
                                    out=pt[:, sbi * 128:(sbi + 1) * 128], in_=ps[sbk][:, sbi * 128:(sbi + 1) * 128],
                                    func=AF.Exp, bias=biasT[:, pidx(qb, kb), h:h + 1], scale=0.125),
                                    reads=[("ps", sbk), ("biasT", qb, kb)], writes=[ptok])
                            if i >= 0:
                                P.op("dve", lambda e, pt=pt, i=i: e.tensor_tensor(
                                    out=pt[:, i * 128:(i + 1) * 128], in0=pt[:, i * 128:(i + 1) * 128],
                                    in1=maskb[:], op=ALU.mult), reads=[ptok, "maskb"], writes=[ptok])
                            mm(ps[ob][0:(65 if hh == 0 else 128), q0:512], Vs[:, pr, kb, vsl], pt[:, q0:512], kb == 0, kb == nk - 1,
                               [("Vs", kb), ("VsB", kb), "Vs_c", "Vs_c0", ptok], [("ps", ob)], skip=True)
                        P.op("dve", lambda e, ob=ob, fin=fin, lrow=lrow: e.reciprocal(
                            out=rl[fin][lrow:lrow + 1, :], in_=ps[ob][lrow:lrow + 1, :]),
                            reads=[("ps", ob)], writes=[("rl", fin)])
                        split(rl[fin][lrow:lrow + 1, :], ("rl", fin), [rlp[fin][i][lrow:lrow + 1, :] for i in range(2)],
                              [rlr[fin][lrow:lrow + 1, :]], ("rlp", fin))
                        for i in range(2):
                            mm(ps[7][:], ones_b[lrow:lrow + 1, :], rlp[fin][i][lrow:lrow + 1, :], i == 0, i == 1,
                               ["ones_b", ("rlp", fin, i)], [("ps", 7)])
                        P.op("act", lambda e, fin=fin, r0=r0: e.copy(out=bc[fin][r0:r0 + 64, :], in_=ps[7][r0:r0 + 64, :]),
                             reads=[("ps", 7)], writes=[("bc", fin)])
                        P.op("dve", lambda e, ob=ob, fin=fin, r0=r0, c=c, j=j: e.tensor_tensor(
                            out=bT[r0:r0 + 64, c, j * 512:(j + 1) * 512], in0=ps[ob][r0:r0 + 64, :],
                            in1=bc[fin][r0:r0 + 64, :], op=ALU.mult),
                            reads=[("ps", ob), ("bc", fin)], writes=[("bT", c, j, hh)])

        if upto == 'B':
            raise _Stop()
        P.barrier()
        Wuv = sb(G0, [128, 8, 2048], BF16)
        o = SCR
        ub = [sb(o, [128, D], BF16), sb(o + 2048, [128, D], BF16)]; o += 4096
        v32 = [sb(o, [128, D], F32), sb(o + 4096, [128, D], F32)]; o += 8192
        vln = [sb(o, [128, D], BF16), sb(o + 2048, [128, D], BF16)]; o += 4096
        ab = [sb(o, [128, D], BF16), sb(o + 2048, [128, D], BF16)]; o += 4096
        lng = sb(o, [128, D], F32); o += 4096
        lnb = sb(o, [128, D], F32); o += 4096
        WsT = sb(o, [128, 8, 128], BF16); o += 2048
        sqjC = sb(o, [128, D], F32); o += 4096
        for q in range(4):
            P.dma("pool", Wuv[:, :, q * 512:(q + 1) * 512], wsrc(w_in, q * 512, (q + 1) * 512), writes=[("Wuv", q)])
        P.dma("sp", lng[:], ln_g.partition_broadcast(128), writes=["lng"])
        P.dma("sp", lnb[:], ln_b.partition_broadcast(128), writes=["lnb"])
        P.dma("pool", WsT[:], ws_t, writes=["WsT"])
        P.op("pool", lambda e: e.memset(WsT[64:128, :, 0:64], 0.0), reads=["WsT"], writes=["WsT"])

        def C0_(b):
            for ct in range(4):
                bk = ct
                for kc in range(8):
                    mm(ps[bk][:], hT[:, kc, b * 128:(b + 1) * 128], Wuv[:, kc, ct * 512:(ct + 1) * 512], kc == 0, kc == 7,
                       [("hT", b), ("Wuv", ct)], [("ps", bk)])
                if ct < 2:
                    P.op("act", lambda e, b=b, ct=ct, bk=bk: e.activation(
                        out=ub[b % 2][:, ct * 512:(ct + 1) * 512], in_=ps[bk][:], func=AF.Gelu_apprx_tanh),
                        reads=[("ps", bk)], writes=[("ub", b % 2, ct)])
                else:
                    P.op("act", lambda e, b=b, ct=ct, bk=bk: e.activation(
                        out=v32[b % 2][:, (ct - 2) * 512:(ct - 1) * 512], in_=ps[bk][:], func=AF.Gelu_apprx_tanh,
                        accum_out=lnst[:, b, ct - 2:ct - 1]),
                        reads=[("ps", bk), "lnst"], writes=[("v32", b % 2, ct - 2), ("lnst", b, ct - 2)])

        def C1_(b):
            s = b % 2
            L = lambda i: lnst[:, b, i:i + 1]
            lt = ("lnstb", b)
            P.op("act", lambda e: e.activation(out=sqjC[:], in_=v32[s][:], func=AF.Square, accum_out=L(2)),
                 reads=[("v32", s, 0), ("v32", s, 1), "lnst"], writes=["sqjC", lt])
            P.op("dve", lambda e: e.tensor_tensor(out=L(0), in0=L(0), in1=L(1), op=ALU.add),
                 reads=[("lnst", b, 0), ("lnst", b, 1)], writes=[("lnst", b, 0)])
            P.op("dve", lambda e: e.tensor_scalar(out=L(0), in0=L(0), scalar1=1.0 / D, scalar2=0.0, op0=ALU.mult, op1=ALU.add),
                 reads=[("lnst", b, 0)], writes=[("lnst", b, 0)])
            P.op("dve", lambda e: e.tensor_tensor(out=L(1), in0=L(0), in1=L(0), op=ALU.mult),
                 reads=[("lnst", b, 0)], writes=[("lnst", b, 1)])
            P.op("dve", lambda e: e.scalar_tensor_tensor(out=L(3), in0=L(2), scalar=1.0 / D, in1=L(1),
                                                          op0=ALU.mult, op1=ALU.subtract),
                 reads=[lt, ("lnst", b, 1)], writes=[lt])
            P.op("dve", lambda e: e.tensor_scalar(out=L(3), in0=L(3), scalar1=EPS, scalar2=0.0, op0=ALU.add, op1=ALU.add),
                 reads=[lt], writes=[lt])
            P.op("pool", lambda e: e.tensor_tensor(out=L(4), in0=L(3), in1=nh[:, 0:1], op=ALU.pow),
                 reads=[lt, "nh"], writes=[lt])
            P.op("dve", lambda e: e.scalar_tensor_tensor(out=L(5), in0=L(0), scalar=-1.0, in1=L(4),
                                                          op0=ALU.mult, op1=ALU.mult),
                 reads=[lt, ("lnst", b, 0)], writes=[lt])
            P.op("dve", lambda e: e.tensor_scalar(out=v32[s][:], in0=v32[s][:], scalar1=L(4), scalar2=L(5),
                                                   op0=ALU.mult, op1=ALU.add),
                 reads=[lt, ("v32", s, 0), ("v32", s, 1)], writes=[("v32", s, 0), ("v32", s, 1)])
            P.op("dve", lambda e: e.tensor_tensor(out=v32[s][:], in0=v32[s][:], in1=lng[:], op=ALU.mult),
                 reads=[("v32", s, 0), ("v32", s, 1), "lng"], writes=[("v32", s, 0), ("v32", s, 1)])
            P.op("dve", lambda e: e.tensor_tensor(out=vln[s][:], in0=v32[s][:], in1=lnb[:], op=ALU.add),
                 reads=[("v32", s, 0), ("v32", s, 1), "lnb"], writes=[("vln", s)])
            for g in range(8):
                bk = 4 + g // 4
                mm(ps[bk][:, (g % 4) * 128:(g % 4 + 1) * 128], WsT[:, g, :], vln[s][:, g * 128:(g + 1) * 128], True, True,
                   ["WsT", ("vln", s)], [("ps", bk)])

        def C2_(b):
            s = b % 2
            for g in range(8):
                bk = 4 + g // 4
                P.op("dve", lambda e, g=g, bk=bk: e.scalar_tensor_tensor(
                    out=ab[s][:, g * 128:(g + 1) * 128], in0=ps[bk][:, (g % 4) * 128:(g % 4 + 1) * 128],
                    scalar=bs[:, g:g + 1], in1=ub[s][:, g * 128:(g + 1) * 128], op0=ALU.add, op1=ALU.mult),
                    reads=[("ps", bk), "bs", ("ub", s, g // 4)], writes=[("ab", s)])
            tr8(ab[s], ("ab", s), 6, aT[:, :, b * 128:(b + 1) * 128], ("aT", b))

        for t in range(NB + 2):
            if 0 <= t - 2 < NB:
                C2_(t - 2)
            if 0 <= t - 1 < NB:
                C1_(t - 1)
            if t < NB:
                C0_(t)

        if upto == 'C':
            raise _Stop()
        P.barrier()
        dump("hT", hT[:])
        dump("bT", bT[:])
        dump("aT", aT[:])
        o = SCR
        wsm = [[sb(o + (s * 4 + i) * 2048, [128, 8, 128], BF16) for i in range(4)] for s in range(2)]; o += 16384
        gsb = [[sb(o + (s * 2 + i) * 2048, [128, 512], F32) for i in range(2)] for s in range(2)]; o += 8192
        tsb = [[sb(o + (s * 2 + i) * 2048, [128, 512], F32) for i in range(2)] for s in range(2)]; o += 8192
        Wout = sb(o, [128, 8, D], BF16); o += 16384
        oE = o
        P.dma("pool", Wout[:, :, 0:512], wsrc(w_out, 0, 512), writes=[("Wout", 0)])
        P.dma("pool", Wout[:, :, 512:1024], wsrc(w_out, 512, 1024), writes=[("Wout", 1)])
        step = 0

        def loadD(m):
            srcs = (wsrc(w_in, O_G + m * 128, O_G + (m + 1) * 128),
                    wsrc(w_in, O_G + D + m * 128, O_G + D + (m + 1) * 128),
                    wsrc(w_a, m * 128, (m + 1) * 128), wsrc(w_b, m * 128, (m + 1) * 128))
            for i in range(4):
                P.dma("pool", wsm[m % 2][i][:], srcs[i], writes=[("wsm", m % 2, i)])

        loadD(0)
        for m in range(8):
            s = m % 2
            if m + 1 < 8:
                loadD(m + 1)
            for j in range(NT):
                pb = 4 * (step % 2)
                sg = step % 2
                step += 1
                acts = (hT, hT, aT, bT)
                for i in range(4):
                    for kc in range(8):
                        if i < 2:
                            rd = [("hT", 4 * j + u) for u in range(4)]
                        elif i == 2:
                            rd = [("aT", 4 * j + u) for u in range(4)]
                        else:
                            rd = [("bT", kc, j, 0), ("bT", kc, j, 1)]
                        mm(ps[pb + i][:], wsm[s][i][:, kc, :], acts[i][:, kc, j * 512:(j + 1) * 512], kc == 0, kc == 7,
                           rd + [("wsm", s, i)], [("ps", pb + i)])
                for i in range(2):
                    P.op("act", lambda e, i=i, pb=pb, sg=sg: e.activation(out=gsb[sg][i][:], in_=ps[pb + i][:], func=AF.Sigmoid),
                         reads=[("ps", pb + i)], writes=[("gsb", sg, i)])
                for i in range(2):
                    P.op("dve", lambda e, i=i, pb=pb, sg=sg: e.tensor_tensor(out=tsb[sg][i][:], in0=ps[pb + 2 + i][:],
                                                                           in1=gsb[sg][i][:], op=ALU.mult),
                         reads=[("ps", pb + 2 + i), ("gsb", sg, i)], writes=[("tsb", sg, i)])
                P.op("pool", lambda e, sg=sg, m=m, j=j: e.tensor_tensor(out=g0[:, m, j * 512:(j + 1) * 512], in0=tsb[sg][0][:],
                                                                       in1=tsb[sg][1][:], op=ALU.add),
                     reads=[("tsb", sg, 0), ("tsb", sg, 1)], writes=[("mg", m, j)])

        if upto == 'D':
            raise _Stop()
        P.barrier()
        dump("mg", g0[:])
        o = oE
        xr = [sb(o, [128, D], F32), sb(o + 4096, [128, D], F32)]; o += 8192
        hbE = [sb(o, [128, D], BF16), sb(o + 2048, [128, D], BF16)]; o += 4096
        gvE = sb(o, [128, D], F32); o += 4096
        sqjE = sb(o, [128, D], F32); o += 4096
        assert o <= SCR_END
        P.dma("sp", gvE[:], norm_ffn_g.partition_broadcast(128), writes=["gv"])
        for b in range(NB):
            j = b // 4
            P.dma("sp", xr[b % 2][:], x[b * 128:(b + 1) * 128, :], writes=[("xr", b % 2)])
            for ch in range(2):
                bk = 2 * (b % 2) + ch
                for m in range(8):
                    mm(ps[bk][:], g0[:, m, b * 128:(b + 1) * 128], Wout[:, m, ch * 512:(ch + 1) * 512], m == 0, m == 7,
                       [("mg", m, j), ("Wout", ch)], [("ps", bk)])
                P.op("dve", lambda e, b=b, ch=ch, bk=bk: e.tensor_tensor(
                    out=x1[:, b, ch * 512:(ch + 1) * 512], in0=ps[bk][:], in1=xr[b % 2][:, ch * 512:(ch + 1) * 512], op=ALU.add),
                    reads=[("ps", bk), ("xr", b % 2)], writes=[("x1", b, ch)])
            norm_block(1, b, x1[:, b, :], gvE[:], hbE[b % 2][:], sqjE[:], 4 + 2 * (b % 2),
                       h2T[:, :, b * 128:(b + 1) * 128], ("x1", b, 0), ("hbE", b % 2), ("h2T", b),
                       extra_tok=[("x1", b, 1)])

        if upto == 'E':
            raise _Stop()
        P.barrier()
        dump("x1", x1[:])
        NG = 11
        actT = sb(SCR, [128, NG, S], BF16)
        Wd = sb(SCR + 45056, [128, NG, D], BF16)
        o = SCR + 67584
        Ag = [sb(o, [128, 512], F32), sb(o + 2048, [128, 512], F32)]; o += 4096
        Al = [sb(o, [128, 512], F32), sb(o + 2048, [128, 512], F32)]; o += 4096
        assert o <= SCR_END
        o = G0
        NW = 3
        wup = [[sb(o + (s * 2 + i) * 2048, [128, 8, 128], BF16) for i in range(2)] for s in range(NW)]; o += NW * 4096
        NR = 3
        Rg = [sb(o + s * 2080, [128, 514], F32) for s in range(NR)]; o += NR * 2080
        Rl = [sb(o + s * 2080, [128, 514], F32) for s in range(NR)]; o += NR * 2080
        Gg = [sb(o, [128, 512], F32), sb(o + 2048, [128, 512], F32)]; o += 4096
        assert o <= G0 + 32 * KB
        rstep = 0
        _KF = int(os.environ.get('KDBG_F', '9'))

        def loadF(fc):
            P.dma("pool", wup[fc % NW][0][:], wsrc(w_up, fc * 128, (fc + 1) * 128), writes=[("wup", fc % NW, 0)])
            P.dma("pool", wup[fc % NW][1][:], wsrc(w_up, DFF + fc * 128, DFF + (fc + 1) * 128), writes=[("wup", fc % NW, 1)])

        loadF(0)
        loadF(1)
        for grp in range(2):
            for q in range(2):
                P.dma("pool", Wd[:, :, q * 512:(q + 1) * 512],
                      w_down[grp * NG * 128:(grp + 1) * NG * 128, q * 512:(q + 1) * 512].rearrange("(c p) n -> p c n", p=128),
                      writes=[("Wd", q)])
            for fl in range(NG):
                fc = grp * NG + fl
                ws_ = fc % NW
                if fc + 2 < NFC:
                    loadF(fc + 2)
                for j in range(NT):
                    rs = rstep % NR
                    rp = (rstep - 1) % NR
                    sa = rstep % 2
                    pb = 2 * (rstep % 2)
                    rstep += 1
                    for i in range(2):
                        for kc in range(8):
                            mm(ps[pb + i][:], wup[ws_][i][:, kc, :], h2T[:, kc, j * 512:(j + 1) * 512], kc == 0, kc == 7,
                               [("h2T", 4 * j + u) for u in range(4)] + [("wup", ws_, i)], [("ps", pb + i)])
                    for i, (R, A, ci) in enumerate(((Rg, Ag, fc), (Rl, Al, NFC + fc))):
                        rt = ("R", i, rs)
                        P.op("act", lambda e, R=R, i=i, pb=pb, rs=rs: e.copy(out=R[rs][:, 2:514], in_=ps[pb + i][:]),
                             reads=[("ps", pb + i)], writes=[rt])
                        rn = (rs + 1) % NR
                        if j == 0:
                            P.op("dve", lambda e, R=R, rs=rs: e.memset(R[rs][:, 0:2], 0.0), writes=[("Rh", i, rs)])
                        if j < NT - 1:
                            P.op("act", lambda e, R=R, i=i, pb=pb, rn=rn: e.copy(out=R[rn][:, 0:2], in_=ps[pb + i][:, 510:512]),
                                 reads=[("ps", pb + i)], writes=[("Rh", i, rn)])
                        at = ("A", i, sa)
                        if _KF < 2:
                            continue
                        P.op("dve", lambda e, A=A, R=R, rs=rs, sa=sa, ci=ci: e.tensor_scalar(
                            out=A[sa][:], in0=R[rs][:, 2:514], scalar1=cw[:, ci, 2:3], scalar2=cb[:, ci:ci + 1],
                            op0=ALU.mult, op1=ALU.add), reads=[rt, "cw", "cb"], writes=[at])
                        _f2 = os.environ.get('KDBG_F2', '')
                        if _f2 == 'a':
                            continue
                        w1s, w0s = (slice(1, 513), slice(0, 512)) if _f2 != 'al' else (slice(2, 514), slice(2, 514))
                        P.op("dve", lambda e, A=A, R=R, rs=rs, sa=sa, ci=ci, w1s=w1s: e.scalar_tensor_tensor(
                            out=A[sa][:], in0=R[rs][:, w1s], scalar=cw[:, ci, 1:2], in1=A[sa][:],
                            op0=ALU.mult, op1=ALU.add), reads=[rt, ("Rh", i, rs), at, "cw"], writes=[at])
                        P.op("dve", lambda e, A=A, R=R, rs=rs, sa=sa, ci=ci, w0s=w0s: e.scalar_tensor_tensor(
                            out=A[sa][:], in0=R[rs][:, w0s], scalar=cw[:, ci, 0:1], in1=A[sa][:],
                            op0=ALU.mult, op1=ALU.add), reads=[rt, ("Rh", i, rs), at, "cw"], writes=[at])
                    if _KF < 3:
                        continue
                    P.op("act", lambda e, sa=sa: e.activation(out=Gg[sa][:], in_=Ag[sa][:], func=AF.Gelu_apprx_tanh),
                         reads=[("A", 0, sa)], writes=[("Gg", sa)])
                    P.op("pool", lambda e, sa=sa, fl=fl, j=j: e.tensor_tensor(
                        out=actT[:, fl, j * 512:(j + 1) * 512], in0=Gg[sa][:], in1=Al[sa][:], op=ALU.mult),
                        reads=[("Gg", sa), ("A", 1, sa)], writes=[("actT", fl, j)])
            for b in range(NB if _KF >= 4 else 0):
                j = b // 4
                for ch in range(2):
                    bk = 4 + 2 * (b % 2) + ch
                    for fl in range(NG):
                        mm(ps[bk][:], actT[:, fl, b * 128:(b + 1) * 128], Wd[:, fl, ch * 512:(ch + 1) * 512],
                           fl == 0, fl == NG - 1, [("actT", fl, j), ("Wd", ch)], [("ps", bk)])
                    P.op("dve", lambda e, b=b, ch=ch, bk=bk: e.tensor_tensor(
                        out=x1[:, b, ch * 512:(ch + 1) * 512], in0=ps[bk][:], in1=x1[:, b, ch * 512:(ch + 1) * 512], op=ALU.add),
                        reads=[("ps", bk), ("x1", b, ch)], writes=[("x1", b, ch)])

        if upto == 'F':
            raise _Stop()
        P.barrier()
        dump("x2", x1[:])
        o = SCR
        Wg = sb(o, [128, 8, D], BF16); o += 16384
        Wp = sb(o, [128, 2, D], BF16); o += 4096
        pst = [sb(o, [128, PLE], F32), sb(o + 1024, [128, PLE], F32)]; o += 2048
        pb16 = [sb(o, [128, PLE], BF16), sb(o + 512, [128, PLE], BF16)]; o += 1024
        pTb = [sb(o, [128, 2, 128], BF16), sb(o + 512, [128, 2, 128], BF16)]; o += 1024
        hbG = [sb(o, [128, D], BF16), sb(o + 2048, [128, D], BF16)]; o += 4096
        h3Tb = [sb(o, [128, 8, 128], BF16), sb(o + 2048, [128, 8, 128], BF16)]; o += 4096
        sgt = [[sb(o + (s * 2 + i) * 2048, [128, 512], F32) for i in range(2)] for s in range(2)]; o += 8192
        tt = [sb(o, [128, D], F32), sb(o + 4096, [128, D], F32)]; o += 8192
        ot = [sb(o, [128, D], F32), sb(o + 4096, [128, D], F32)]; o += 8192
        gv3 = sb(o, [128, D], F32); o += 4096
        gvf = sb(o, [128, D], F32); o += 4096
        sqjG = sb(o, [128, D], F32); o += 4096
        assert o <= SCR_END
        P.dma("pool", Wg[:, :, 0:512], wsrc(w_pg, 0, 512), writes=[("Wg", 0)])
        P.dma("pool", Wg[:, :, 512:1024], wsrc(w_pg, 512, 1024), writes=[("Wg", 1)])
        P.dma("pool", Wp[:], wsrc(w_ple, 0, D), writes=["Wp"])
        P.dma("sp", gv3[:], norm_ple_g.partition_broadcast(128), writes=["gv"])
        P.dma("sp", gvf[:], norm_final_g.partition_broadcast(128), writes=["gvf"])

        def G0_(b):
            s = b % 2
            P.dma("sp", pst[s][:], pin[b * 128:(b + 1) * 128, :], writes=[("pst", s)])
            P.op("act", lambda e: e.copy(out=pb16[s][:], in_=pst[s][:]), reads=[("pst", s)], writes=[("pb16", s)])
            tr8(pb16[s], ("pb16", s), 4, pTb[s][:], ("pTb", s), nch=2)
            norm_block(2, b, x1[:, b, :], gv3[:], hbG[s][:], sqjG[:], 0, h3Tb[s][:],
                       ("x1", b, 0), ("hbG", s), ("h3Tb", s), extra_tok=[("x1", b, 1)])

        def G1_(b):
            s = b % 2
            for ch in range(2):
                bk = 2 + ch
                for kc in range(8):
                    mm(ps[bk][:], h3Tb[s][:, kc, :], Wg[:, kc, ch * 512:(ch + 1) * 512], kc == 0, kc == 7,
                       [("h3Tb", s), ("Wg", ch)], [("ps", bk)])
                P.op("act", lambda e, ch=ch, bk=bk: e.activation(out=sgt[s][ch][:], in_=ps[bk][:], func=AF.Sigmoid),
                     reads=[("ps", bk)], writes=[("sgt", s, ch)])
                bk2 = 6 + ch
                for kc in range(2):
                    mm(ps[bk2][:], pTb[s][:, kc, :], Wp[:, kc, ch * 512:(ch + 1) * 512], kc == 0, kc == 1,
                       [("pTb", s), "Wp"], [("ps", bk2)])
                P.op("dve", lambda e, ch=ch, bk2=bk2: e.tensor_tensor(
                    out=tt[s][:, ch * 512:(ch + 1) * 512], in0=ps[bk2][:], in1=sgt[s][ch][:], op=ALU.mult),
                    reads=[("ps", bk2), ("sgt", s, ch)], writes=[("tt", s, ch)])
                P.op("dve", lambda e, ch=ch: e.tensor_tensor(
                    out=tt[s][:, ch * 512:(ch + 1) * 512], in0=tt[s][:, ch * 512:(ch + 1) * 512],
                    in1=x1[:, b, ch * 512:(ch + 1) * 512], op=ALU.add),
                    reads=[("tt", s, ch), ("x1", b, ch)], writes=[("tt", s, ch)])
            ssc = stat[:, 3, 0, b:b + 1]
            msc = stat[:, 3, 1, b:b + 1]
            rsc = stat[:, 3, 2, b:b + 1]
            st = ("stat", 3, b)
            P.op("act", lambda e: e.activation(out=sqjG[:], in_=tt[s][:], func=AF.Square, accum_out=ssc),
                 reads=[("tt", s, 0), ("tt", s, 1), "stat"], writes=["sqj", st])
            P.op("dve", lambda e: e.tensor_scalar(out=msc, in0=ssc, scalar1=1.0 / D, scalar2=EPS, op0=ALU.mult, op1=ALU.add),
                 reads=[st], writes=[st])
            P.op("pool", lambda e: e.tensor_tensor(out=rsc, in0=msc, in1=nh[:, 0:1], op=ALU.pow), reads=[st, "nh"], writes=[st])
            P.op("dve", lambda e: e.scalar_tensor_tensor(out=ot[s][:], in0=tt[s][:], scalar=rsc, in1=gvf[:],
                                                          op0=ALU.mult, op1=ALU.mult),
                 reads=[("tt", s, 0), ("tt", s, 1), st, "gvf"], writes=[("ot", s)])
            P.dma("sp", out[b * 128:(b + 1) * 128, :], ot[s][:], reads=[("ot", s)], writes=[("out", b)])

        for t in range(NB + 1):
            if 0 <= t - 1 < NB:
                G1_(t - 1)
            if t < NB:
                G0_(t)
        P.op("sp", lambda e: e.nop(), reads=[("out", b) for b in range(NB)])
        P.barrier()

    except _Stop:
        P.barrier()

    emit(nc, P)
    es.close()
    return nc


_CACHE = {}


def _prep_shared(inp):
    f = np.float32
    d = {}
    d["norm_mix_g"] = np.ascontiguousarray(inp["norm_mix_g"][0], f)
    d["w_in"] = np.ascontiguousarray(inp["w_in"][0], f)
    d["b_f"] = np.ascontiguousarray(inp["b_f"][0], f)
    d["gmlp_ln_g"] = np.ascontiguousarray(inp["gmlp_ln_g"][0], f)
    d["gmlp_ln_b"] = np.ascontiguousarray(inp["gmlp_ln_b"][0], f)
    d["ws_t"] = np.ascontiguousarray(np.transpose(inp["gmlp_w_s"][0], (2, 0, 1)), f)
    d["bs_t"] = np.ascontiguousarray(inp["gmlp_b_s"][0].T, f)
    d["w_branch_a"] = np.ascontiguousarray(inp["w_branch_a"][0], f)
    d["w_branch_b"] = np.ascontiguousarray(inp["w_branch_b"][0], f)
    d["w_out"] = np.ascontiguousarray(inp["w_out"][0], f)
    d["norm_ffn_g"] = np.ascontiguousarray(inp["norm_ffn_g"][0], f)
    d["w_up"] = np.ascontiguousarray(inp["w_up"][0], f)
    cwv = np.asarray(inp["conv_w"][0], f)
    d["cw_t"] = np.ascontiguousarray(cwv.T.reshape(44, 128, 3).transpose(1, 0, 2), f)
    d["cb_t"] = np.ascontiguousarray(np.asarray(inp["conv_b"][0], f).reshape(44, 128).T, f)
    d["w_down"] = np.ascontiguousarray(inp["w_down"][0], f)
    d["norm_ple_g"] = np.ascontiguousarray(inp["norm_ple_g"][0], f)
    d["w_ple"] = np.ascontiguousarray(inp["w_ple"][0], f)
    d["w_ple_gate"] = np.ascontiguousarray(inp["w_ple_gate"][0], f)
    d["norm_final_g"] = np.ascontiguousarray(inp["norm_final_g"], f)
    d["c_ident"] = np.eye(128, dtype=f)
    r = np.arange(128)
    d["c_tri"] = (r[:, None] <= r[None, :]).astype(f)
    d["c_sel"] = np.ascontiguousarray(np.broadcast_to((r[:, None] <= 63), (128, 128))).astype(f)
    return d


def kernel(**inputs):
    inp = {k: np.asarray(v) for k, v in inputs.items()}
    n = 8
    if "nc" not in _CACHE:
        _CACHE["nc"] = build_nc()
    nc = _CACHE["nc"]
    shared = _prep_shared(inp)
    x = np.asarray(inp["x"], np.float32)
    p = np.asarray(inp["p"], np.float32)[0]
    in_maps = []
    for i in range(n):
        m = dict(shared)
        m["x"] = np.ascontiguousarray(x[i])
        m["p"] = np.ascontiguousarray(p[i])
        in_maps.append(m)
    res = run_bass_kernel_spmd(nc, in_maps, core_ids=list(range(n)))
    return np.stack([np.asarray(r["out"], np.float32) for r in res.results], axis=0)
```

```python
import contextlib
import numpy as np
import concourse.bass as bass
import concourse.mybir as mybir
from concourse.bass_utils import run_bass_kernel_spmd

F32 = mybir.dt.float32
BF16 = mybir.dt.bfloat16
AF = mybir.ActivationFunctionType
ALU = mybir.AluOpType

ENGS = ("pe", "act", "dve", "pool", "sp")

S = 2048
D = 1024
NB = 16
NT = 4
DFF = 2816
NFC = 22
IN_COLS = 7184
O_Q = 2048
O_K = 3072
O_V = 4096
O_F = 5120
O_G = 5136
PLE = 256
EPS = 1e-6
KB = 1024
BASE = 16512
import os
_STG = int(os.environ.get('KDBG_STG', '6'))
_NBA = int(os.environ.get('KDBG_NBA', '16'))


class _Op:
    __slots__ = ("fn", "deps", "dma", "flag", "semval", "dsem", "dcount", "waits", "prewait")

    def __init__(self, fn, deps, dma):
        self.fn = fn
        self.deps = deps
        self.dma = dma
        self.flag = False
        self.semval = None
        self.dsem = None
        self.dcount = None
        self.waits = []
        self.prewait = None


class Prog:
    def __init__(self, nc, n_dma_sems=32):
        self.nc = nc
        self.streams = {e: [] for e in ENGS}
        self.last_writer = {}
        self.readers = {}
        self.n_dma_sems = n_dma_sems

    def op(self, eng, fn, reads=(), writes=(), dma=False, extra=None):
        st = self.streams[eng]
        me = (eng, len(st))
        deps = {}
        for t in reads:
            w = self.last_writer.get(t)
            if w is not None:
                deps[w] = True
        for t in writes:
            w = self.last_writer.get(t)
            if w is not None and w not in deps:
                deps[w] = False
            for r in self.readers.get(t, ()):
                if r not in deps:
                    deps[r] = False
        if extra:
            for d in extra:
                deps[d] = True
        deps.pop(me, None)
        for t in writes:
            self.last_writer[t] = me
            self.readers[t] = []
        for t in reads:
            if self.last_writer.get(t) != me:
                self.readers.setdefault(t, []).append(me)
        st.append(_Op(fn, deps, dma))
        return me

    def dma(self, eng, out, in_, reads=(), writes=()):
        return self.op(eng, lambda e: e.dma_start(out=out, in_=in_), reads, writes, dma=True)

    def barrier(self):
        lasts = []
        for e in ENGS:
            st = self.streams[e]
            for i in range(len(st) - 1, -1, -1):
                if not st[i].dma:
                    lasts.append((e, i))
                    break
            n = 0
            for i in range(len(st) - 1, -1, -1):
                if st[i].dma:
                    lasts.append((e, i))
                    n += 1
                    if n >= 16:
                        break
        for e in ENGS:
            self.op(e, lambda g: g.nop(), extra=[d for d in lasts])

    def resolve(self):
        streams = self.streams
        dma_engs = [e for e in ENGS if any(o.dma for o in streams[e])]
        pools = {}
        if dma_engs:
            per = max(2, self.n_dma_sems // len(dma_engs))
            for e in dma_engs:
                pools[e] = min(per, 16)
        self.dma_pool_sizes = pools
        for e in dma_engs:
            k = 0
            counts = [0] * pools[e]
            lastop = [None] * pools[e]
            for i, o in enumerate(streams[e]):
                if not o.dma:
                    continue
                s = k % pools[e]
                k += 1
                if lastop[s] is not None:
                    o.prewait = ((e, s), counts[s])
                counts[s] += 16
                o.dsem = (e, s)
                o.dcount = counts[s]
                lastop[s] = i
        for e in ENGS:
            seen = {f: -1 for f in ENGS}
            seen_d = {}
            for i, o in enumerate(streams[e]):
                if o.prewait is not None:
                    key, cnt = o.prewait
                    if seen_d.get(key, 0) >= cnt:
                        o.prewait = None
                    else:
                        seen_d[key] = cnt
                for (f, n), is_raw in sorted(o.deps.items(), key=lambda kv: (kv[0][0], -kv[0][1])):
                    p = streams[f][n]
                    if p.dma:
                        if seen_d.get(p.dsem, 0) >= p.dcount:
                            continue
                        seen_d[p.dsem] = p.dcount
                        o.waits.append(("d", p.dsem, p.dcount))
                        continue
                    if f == e and not o.dma:
                        if not (is_raw and e != "pe"):
                            continue
                    if n <= seen[f]:
                        continue
                    seen[f] = n
                    p.flag = True
                    o.waits.append(("c", f, n))
        for e in ENGS:
            c = 0
            for o in streams[e]:
                if o.flag:
                    c += 1
                    o.semval = c


def emit(nc, prog):
    prog.resolve()
    with contextlib.ExitStack() as es:
        csem = {e: es.enter_context(nc.semaphore("c_" + e)) for e in ENGS}
        dsem = {}
        for e, n in prog.dma_pool_sizes.items():
            for s in range(n):
                dsem[(e, s)] = es.enter_context(nc.semaphore("d_%s_%d" % (e, s)))
        block = es.enter_context(nc.Block())

        def run(ename, eng):
            for o in prog.streams[ename]:
                if o.prewait is not None:
                    key, cnt = o.prewait
                    eng.wait_ge(dsem[key], cnt)
                for w in o.waits:
                    if w[0] == "d":
                        eng.wait_ge(dsem[w[1]], w[2])
                    else:
                        eng.wait_ge(csem[w[1]], prog.streams[w[1]][w[2]].semval)
                ins = o.fn(eng)
                if o.dma:
                    ins.then_inc(dsem[o.dsem], 16)
                elif o.flag:
                    ins.then_inc(csem[ename], 1)

        @block.tensor
        def _(eng):
            run("pe", eng)

        @block.scalar
        def _(eng):
            run("act", eng)

        @block.vector
        def _(eng):
            run("dve", eng)

        @block.gpsimd
        def _(eng):
            run("pool", eng)

        @block.sync
        def _(eng):
            run("sp", eng)


class _Stop(Exception):
    pass


def build_nc(debug=False, upto=None):
    nc = bass.Bass("TRN2", target_bir_lowering=False)

    def din(name, shape):
        return nc.dram_tensor(name, list(shape), F32, kind="ExternalInput").ap()

    x = din("x", [S, D])
    pin = din("p", [S, PLE])
    norm_mix_g = din("norm_mix_g", [D])
    w_in = din("w_in", [D, IN_COLS])
    b_f = din("b_f", [16])
    ln_g = din("gmlp_ln_g", [D])
    ln_b = din("gmlp_ln_b", [D])
    ws_t = din("ws_t", [128, 8, 128])
    bs_t = din("bs_t", [128, 8])
    w_a = din("w_branch_a", [D, D])
    w_b = din("w_branch_b", [D, D])
    w_out = din("w_out", [D, D])
    norm_ffn_g = din("norm_ffn_g", [D])
    w_up = din("w_up", [D, 2 * DFF])
    cw_t = din("cw_t", [128, 44, 3])
    cb_t = din("cb_t", [128, 44])
    w_down = din("w_down", [DFF, D])
    norm_ple_g = din("norm_ple_g", [D])
    w_ple = din("w_ple", [PLE, D])
    w_pg = din("w_ple_gate", [D, D])
    norm_final_g = din("norm_final_g", [D])
    c_ident = din("c_ident", [128, 128])
    c_tri = din("c_tri", [128, 128])
    c_sel = din("c_sel", [128, 128])
    out = nc.dram_tensor("out", [S, D], F32, kind="ExternalOutput").ap()
    dbg = {}
    if debug:
        for nm in ("hT", "bT", "aT", "mg"):
            dbg[nm] = nc.dram_tensor("dbg_" + nm, [128, 8, S], BF16, kind="ExternalOutput").ap()
        for nm in ("x1", "x2"):
            dbg[nm] = nc.dram_tensor("dbg_" + nm, [128, NB, D], F32, kind="ExternalOutput").ap()

    def dump(nm, src):
        if debug:
            P.dma("sp", dbg[nm], src)
            P.barrier()

    cnt = [0]

    def sb(off, shape, dt):
        cnt[0] += 1
        return nc.alloc_sbuf_tensor_at("t%d" % cnt[0], list(shape), dt, offset=off)

    C0 = BASE
    ident = sb(C0 + 0, [128, 128], BF16)
    onesf = sb(C0 + 256, [128, 128], F32)
    trif = sb(C0 + 768, [128, 128], F32)
    self_ = sb(C0 + 1280, [128, 128], F32)
    maskb = sb(C0 + 1792, [128, 128], BF16)
    cw = sb(C0 + 2048, [128, 44, 3], F32)
    cb = sb(C0 + 2592, [128, 44], F32)
    bs = sb(C0 + 2784, [128, 8], F32)
    bfb = sb(C0 + 2816, [128, 16], F32)
    nh = sb(C0 + 2880, [128, 16], F32)
    stat = sb(C0 + 2944, [128, 4, 3, 16], F32)
    lnst = sb(C0 + 3712, [128, 16, 6], F32)
    G0 = C0 + 4 * KB
    HT = C0 + 36 * KB
    BT = C0 + 68 * KB
    AT = C0 + 100 * KB
    SCR = C0 + 132 * KB
    SCR_END = 229344

    g0 = sb(G0, [128, 8, S], BF16)
    hT = sb(HT, [128, 8, S], BF16)
    bT = sb(BT, [128, 8, S], BF16)
    aT = sb(AT, [128, 8, S], BF16)
    x1 = sb(HT, [128, NB, D], F32)
    h2T = sb(AT, [128, 8, S], BF16)

    es = contextlib.ExitStack()
    ps = [es.enter_context(nc.psum_tensor("ps%d" % i, [128, 512], F32)) for i in range(8)]

    def psb(i):
        return ps[i][:].bitcast(BF16).rearrange("p (c n) -> p c n", c=8)

    P = Prog(nc)

    def mm(o, lhsT, rhs, start, stop, reads, writes, skip=False):
        if skip:
            P.op("pe", lambda e: e.matmul(o, lhsT=lhsT, rhs=rhs, start=start, stop=stop,
                                           skip_group_check=True), reads, writes)
        else:
            P.op("pe", lambda e: e.matmul(o, lhsT=lhsT, rhs=rhs, start=start, stop=stop), reads, writes)

    def wsrc(w, c0, c1):
        return w[:, c0:c1].rearrange("(c p) n -> p c n", p=128)

    try:
        P.dma("pool", ident[:], c_ident, writes=["ident"])
        P.dma("sp", trif[:], c_tri, writes=["trif"])
        P.dma("sp", self_[:], c_sel, writes=["self"])
        P.dma("pool", maskb[:], c_tri, writes=["maskb"])
        P.dma("sp", cw[:], cw_t, writes=["cw"])
        P.dma("sp", cb[:], cb_t, writes=["cb"])
        P.dma("sp", bs[:], bs_t, writes=["bs"])
        P.dma("sp", bfb[:], b_f.partition_broadcast(128), writes=["bfb"])
        P.op("pool", lambda e: e.memset(onesf[:], 1.0), writes=["onesf"])
        P.op("pool", lambda e: e.memset(nh[:], -0.5), writes=["nh"])
        P.op("pool", lambda e: e.memset(stat[:], 0.0), writes=["stat"])
        P.op("pool", lambda e: e.memset(lnst[:], 0.0), writes=["lnst"])

        if upto == 'c':
            raise _Stop()
        def tr8(src, src_tok, bank, dstT, dst_tok, nch=8):
            for c in range(nch):
                bk = bank + c // 4
                P.op("pe", lambda e, c=c, bk=bk: e.matmul(ps[bk][:, (c % 4) * 128:(c % 4 + 1) * 128],
                                                          lhsT=src[:, c * 128:(c + 1) * 128], rhs=ident[:],
                                                          start=True, stop=True),
                     reads=[src_tok, "ident"], writes=[("ps", bk)])
            for hf in range((nch + 3) // 4):
                n = min(4, nch - 4 * hf)
                pvv = ps[bank + hf][:, 0:n * 128].rearrange("p (c n) -> p c n", c=n)
                P.op("act", lambda e, hf=hf, n=n, pvv=pvv: e.copy(out=dstT[:, 4 * hf:4 * hf + n, :], in_=pvv),
                     reads=[("ps", bank + hf)], writes=[dst_tok])

        def norm_block(ni, b, src, gv, hb, sqj, bank, dstT, src_tok, hb_tok, dst_tok, extra_tok=()):
            ssc = stat[:, ni, 0, b:b + 1]
            msc = stat[:, ni, 1, b:b + 1]
            rsc = stat[:, ni, 2, b:b + 1]
            st = ("stat", ni, b)
            P.op("act", lambda e: e.activation(out=sqj, in_=src, func=AF.Square, accum_out=ssc),
                 reads=[src_tok, "stat"] + list(extra_tok), writes=["sqj", st])
            if _STG < 2:
                return
            P.op("dve", lambda e: e.tensor_scalar(out=msc, in0=ssc, scalar1=1.0 / D, scalar2=EPS,
                                                   op0=ALU.mult, op1=ALU.add), reads=[st], writes=[st])
            if _STG < 3:
                return
            P.op("pool", lambda e: e.tensor_tensor(out=rsc, in0=msc, in1=nh[:, 0:1], op=ALU.pow),
                 reads=[st, "nh"], writes=[st])
            if _STG < 4:
                return
            P.op("dve", lambda e: e.scalar_tensor_tensor(out=hb, in0=src, scalar=rsc, in1=gv,
                                                          op0=ALU.mult, op1=ALU.mult),
                 reads=[src_tok, st, "gv"], writes=[hb_tok])
            if _STG < 5:
                return
            tr8(hb, hb_tok, bank, dstT, dst_tok)

        xt = [sb(SCR + 0, [128, D], F32), sb(SCR + 4096, [128, D], F32)]
        hbA = [sb(SCR + 8192, [128, D], BF16), sb(SCR + 10240, [128, D], BF16)]
        gvA = sb(SCR + 12288, [128, D], F32)
        sqjA = sb(SCR + 16384, [128, D], F32)
        P.dma("sp", gvA[:], norm_mix_g.partition_broadcast(128), writes=["gv"])
        Vs = sb(G0, [128, 4, NB, 192], BF16)
        Wv = sb(G0 + 24576, [128, 8, 512], BF16)
        oB = SCR + 20480
        wf = sb(oB, [128, 8, 16], BF16)
        wq = [sb(oB + 29760, [128, 8, 128], BF16), sb(oB + 29760 + 2048, [128, 8, 128], BF16)]
        wk = [sb(oB + 33856, [128, 8, 128], BF16), sb(oB + 33856 + 2048, [128, 8, 128], BF16)]
        P.dma("pool", wf[:], wsrc(w_in, O_F, O_F + 16), writes=["wf"])
        P.dma("pool", Wv[:], wsrc(w_in, O_V, O_V + 512), writes=["Wv"])
        P.dma("pool", wq[0][:], wsrc(w_in, O_Q, O_Q + 128), writes=[("wq", 0)])
        P.dma("pool", wk[0][:], wsrc(w_in, O_K, O_K + 128), writes=[("wk", 0)])
        if upto == 'p':
            raise _Stop()
        for b in range(_NBA):
            P.dma("sp", xt[b % 2][:], x[b * 128:(b + 1) * 128, :], writes=[("xt", b % 2)])
            norm_block(0, b, xt[b % 2][:], gvA[:], hbA[b % 2][:], sqjA[:], 2 * (b % 2),
                       hT[:, :, b * 128:(b + 1) * 128], ("xt", b % 2), ("hbA", b % 2), ("hT", b))

        if upto == 'A':
            raise _Stop()
        o = SCR + 20480
        o += 256
        Lf = sb(o, [128, NB, 16], F32); o += 1024
        Sacc = sb(o, [128, NB + 1, 16], F32); o += 1088
        Cc = sb(o, [128, NB, 16], F32); o += 1024
        Mm = sb(o, [128, NB, 16], F32); o += 1024
        xf = sb(o, [128, 2, 16], F32); o += 128
        ef = sb(o, [128, 2, 16], F32); o += 128
        o = (o + 63) // 64 * 64
        NPAIR = NB * (NB + 1) // 2
        biasT = sb(o, [128, NPAIR, 16], F32); o += NPAIR * 64
        qT = [sb(o, [128, S], BF16), sb(o + 4096, [128, S], BF16)]; o += 8192
        kT = [sb(o, [128, S], BF16), sb(o + 4096, [128, S], BF16)]; o += 8192
        assert o == oB + 29760, (o, oB)
        o += 8192
        NPT = 4
        Pt = [sb(o + i * 1024, [128, 512], BF16) for i in range(NPT)]; o += NPT * 1024
        rl = [sb(o, [128, 512], F32), sb(o + 2048, [128, 512], F32)]; o += 4096
        bc = [sb(o, [128, 512], F32), sb(o + 2048, [128, 512], F32)]; o += 4096
        assert o <= SCR_END, o

        def pidx(qb, kb):
            return qb * (qb + 1) // 2 + kb

        oA = AT
        tri_b = sb(oA, [128, 128], BF16); oA += 256
        sel_b = sb(oA, [128, 128], BF16); oA += 256
        ones_b = sb(oA, [128, 128], BF16); oA += 256
        Lp = [sb(oA + i * 512, [128, NB, 16], BF16) for i in range(3)]; oA += 1536
        Sp = [sb(oA + i * 544, [128, NB + 1, 16], BF16) for i in range(3)]; oA += 1632
        RL = [sb(oA + i * 1024, [128, NB, 16], F32) for i in range(2)]; oA += 2048
        RS = [sb(oA + i * 1088, [128, NB + 1, 16], F32) for i in range(2)]; oA += 2176
        rlp = [[sb(oA + (f_ * 2 + i) * 1024, [128, 512], BF16) for i in range(2)] for f_ in range(2)]; oA += 4096
        rlr = [sb(oA + f_ * 2048, [128, 512], F32) for f_ in range(2)]; oA += 4096
        P.dma("pool", tri_b[:], c_tri, writes=["tri_b"])
        P.dma("pool", sel_b[:], c_sel, writes=["sel_b"])
        P.op("pool", lambda e: e.memset(ones_b[:], 1.0), writes=["ones_b"])
        for i in range(3):
            P.op("pool", lambda e, i=i: e.memset(Sp[i][:, 0, :], 0.0), writes=[("Sp", 0, i)])

        def split(src, src_tok, pieces, tmps, key):
            cur, cur_tok = src, src_tok
            for i, pc in enumerate(pieces):
                P.op("dve", lambda e, pc=pc, cur=cur: e.tensor_copy(out=pc, in_=cur), reads=[cur_tok], writes=[key + (i,)])
                if i < len(pieces) - 1:
                    P.op("dve", lambda e, pc=pc, cur=cur, t=tmps[i]: e.tensor_tensor(out=t, in0=cur, in1=pc, op=ALU.subtract),
                         reads=[cur_tok, key + (i,)], writes=[key + ("r", i)])
                    cur, cur_tok = tmps[i], key + ("r", i)

        if not os.environ.get('KDBG_NOMS'):
            P.op("dve", lambda e: e.memset(Vs[:, :, :, 65:128], 0.0), writes=["Vs_c0"])
            P.op("dve", lambda e: e.memset(Vs[:, :, :, 64:65], 1.0), writes=["Vs_c"])
        P.op("pool", lambda e: e.memset(Sacc[:, 0, :], 0.0), writes=[("Sacc", 0)])

        for b in range(NB):
            for kc in range(8):
                mm(ps[2][:, 0:16], hT[:, kc, b * 128:(b + 1) * 128], wf[:, kc, :], kc == 0, kc == 7,
                   [("hT", b), "wf"], [("ps", 2)])
            P.op("dve", lambda e, b=b: e.tensor_tensor(out=xf[:, b % 2, :], in0=ps[2][:, 0:16], in1=bfb[:], op=ALU.add),
                 reads=[("ps", 2), "bfb"], writes=[("xf", b % 2)])
            P.op("act", lambda e, b=b: e.activation(out=ef[:, b % 2, :], in_=xf[:, b % 2, :], func=AF.Exp, scale=-1.0),
                 reads=[("xf", b % 2)], writes=[("ef", b % 2)])
            P.op("act", lambda e, b=b: e.activation(out=Lf[:, b, :], in_=ef[:, b % 2, :], func=AF.Ln, bias=onesf[:, 0:1]),
                 reads=[("ef", b % 2), "onesf"], writes=[("Lf", b)])
            P.op("dve", lambda e, b=b: e.tensor_tensor(out=Sacc[:, b + 1, :], in0=Sacc[:, b, :], in1=Lf[:, b, :], op=ALU.add),
                 reads=[("Sacc", b), ("Lf", b)], writes=[("Sacc", b + 1)])
            split(Lf[:, b, :], ("Lf", b), [Lp[i][:, b, :] for i in range(3)], [RL[i][:, b, :] for i in range(2)], ("Lp", b))
            split(Sacc[:, b + 1, :], ("Sacc", b + 1), [Sp[i][:, b + 1, :] for i in range(3)],
                  [RS[i][:, b + 1, :] for i in range(2)], ("Sp", b + 1))
        for b in range(NB):
            for (cols, mat, mtok) in ((slice(0, 16), tri_b, "tri_b"), (slice(16, 32), sel_b, "sel_b")):
                for i in range(3):
                    mm(ps[3][:, cols], mat[:], Lp[i][:, b, :], i == 0, False, [mtok, ("Lp", b, i)], [("ps", 3)])
                for i in range(3):
                    mm(ps[3][:, cols], ones_b[:], Sp[i][:, b, :], False, i == 2, ["ones_b", ("Sp", b, i)], [("ps", 3)])
            P.op("dve", lambda e, b=b: e.tensor_copy(out=Cc[:, b, :], in_=ps[3][:, 0:16]), reads=[("ps", 3)], writes=[("Cc", b)])
            P.op("dve", lambda e, b=b: e.tensor_copy(out=Mm[:, b, :], in_=ps[3][:, 16:32]), reads=[("ps", 3)], writes=[("Mm", b)])
        for qb in range(NB):
            for kb in range(qb + 1):
                P.op("pool", lambda e, qb=qb, kb=kb: e.tensor_tensor(out=biasT[:, pidx(qb, kb), :], in0=Cc[:, kb, :],
                                                                     in1=Mm[:, qb, :], op=ALU.subtract),
                     reads=[("Cc", kb), ("Mm", qb)], writes=[("biasT", qb, kb)])

        _KB = int(os.environ.get('KDBG_B', '9'))
        if _KB < 2:
            raise _Stop()
        uid = [0]
        ugc = [0]
        for gh in range(2):
            for b in range(NB):
                bk = int(os.environ.get('KDBG_VB', '4')) + (b % 2)
                for kc in range(8):
                    mm(ps[bk][:], hT[:, kc, b * 128:(b + 1) * 128], Wv[:, kc, :], kc == 0, kc == 7,
                       [("hT", b), "Wv"], [("ps", bk)])
                pv = ps[bk][:].rearrange("p (a n) -> p a n", a=4)
                _vc = int(os.environ.get('KDBG_VC', '2'))
                if _vc >= 1:
                    P.op("act", lambda e, b=b, pv=pv: e.copy(out=Vs[:, :, b, 0:64], in_=pv[:, :, 0:64]),
                         reads=[("ps", bk)], writes=[("Vs", b)])
                if _vc >= 2:
                    P.op("act", lambda e, b=b, pv=pv: e.copy(out=Vs[:, :, b, 128:192], in_=pv[:, :, 64:128]),
                         reads=[("ps", bk)], writes=[("VsB", b)])
            if _KB < 3:
                raise _Stop()
            if gh == 0:
                P.dma("pool", Wv[:], wsrc(w_in, O_V + 512, O_V + 1024), writes=["Wv"])
            for pr in range(4):
                c = gh * 4 + pr
                sl = c % 2
                for j in range(NT):
                    for (wt, dst, tok, bk) in ((wq[sl], qT[sl], "qT", 0), (wk[sl], kT[sl], "kT", 1)):
                        for kc in range(8):
                            mm(ps[bk][:], wt[:, kc, :], hT[:, kc, j * 512:(j + 1) * 512], kc == 0, kc == 7,
                               [("hT", 4 * j), ("hT", 4 * j + 1), ("hT", 4 * j + 2), ("hT", 4 * j + 3),
                                ("wq" if tok == "qT" else "wk", sl)], [("ps", bk)])
                        P.op("dve", lambda e, dst=dst, bk=bk, j=j: e.tensor_copy(out=dst[:, j * 512:(j + 1) * 512], in_=ps[bk][:]),
                             reads=[("ps", bk)], writes=[(tok, sl, j)])
                if _KB < 4:
                    raise _Stop()
                if c + 1 < 8:
                    c1 = c + 1
                    P.dma("pool", wq[c1 % 2][:], wsrc(w_in, O_Q + c1 * 128, O_Q + (c1 + 1) * 128), writes=[("wq", c1 % 2)])
                    P.dma("pool", wk[c1 % 2][:], wsrc(w_in, O_K + c1 * 128, O_K + (c1 + 1) * 128), writes=[("wk", c1 % 2)])
                units = [(hh, j, kb) for hh in range(2) for j in range(NT) for kb in range(4 * j + 4)]
                LA = 2
                tinfo = {}
                pend = []

                def qk_exp(hh, j, kb, ug):
                    h = 2 * c + hh
                    r0 = 64 * hh
                    i = kb - 4 * j
                    q0 = max(0, i) * 128
                    sbk = 2 + (ug % 3)
                    pt = Pt[ug % NPT]
                    ptok = ("Pt", ug % NPT)
                    mm(ps[sbk][:, q0:512], kT[sl][r0:r0 + 64, kb * 128:(kb + 1) * 128],
                       qT[sl][r0:r0 + 64, j * 512 + q0:(j + 1) * 512], True, True,
                       [("kT", sl, kb // 4), ("qT", sl, j)], [("ps", sbk)])
                    for sbi in range(q0 // 128, 4):
                        qb = 4 * j + sbi
                        P.op("act", lambda e, sbk=sbk, sbi=sbi, pt=pt, qb=qb, kb=kb, h=h: e.activation(
                            out=pt[:, sbi * 128:(sbi + 1) * 128], in_=ps[sbk][:, sbi * 128:(sbi + 1) * 128],
                            func=AF.Exp, bias=biasT[:, pidx(qb, kb), h:h + 1], scale=0.125),
                            reads=[("ps", sbk), ("biasT", qb, kb)], writes=[ptok])
                    if i >= 0:
                        P.op("dve", lambda e, pt=pt, i=i: e.tensor_tensor(
                            out=pt[:, i * 128:(i + 1) * 128], in0=pt[:, i * 128:(i + 1) * 128],
                            in1=maskb[:], op=ALU.mult), reads=[ptok, "maskb"], writes=[ptok])

                def pv(hh, j, kb, ug, step):
                    r0 = 64 * hh
                    lrow = 64 if hh == 0 else 0
                    vsl = slice(0, 65) if hh == 0 else slice(64, 192)
                    nk = 4 * j + 4
                    if kb == 0:
                        tinfo[(hh, j)] = (5 + (uid[0] % 2), uid[0] % 2)
                        uid[0] += 1
                    ob, fin = tinfo[(hh, j)]
                    q0 = max(0, kb - 4 * j) * 128
                    pt = Pt[ug % NPT]
                    ptok = ("Pt", ug % NPT)
                    mm(ps[ob][0:(65 if hh == 0 else 128), q0:512], Vs[:, pr, kb, vsl], pt[:, q0:512], kb == 0, kb == nk - 1,
                       [("Vs", kb), ("VsB", kb), "Vs_c", "Vs_c0", ptok], [("ps", ob)], skip=True)
                    if kb != nk - 1:
                        return
                    P.op("dve", lambda e: e.reciprocal(out=rl[fin][lrow:lrow + 1, :], in_=ps[ob][lrow:lrow + 1, :]),
                         reads=[("ps", ob)], writes=[("rl", fin)])
                    split(rl[fin][lrow:lrow + 1, :], ("rl", fin), [rlp[fin][i][lrow:lrow + 1, :] for i in range(2)],
                          [rlr[fin][lrow:lrow + 1, :]], ("rlp", fin))

                    def fin2():
                        for i in range(2):
                            mm(ps[7][:], ones_b[lrow:lrow + 1, :], rlp[fin][i][lrow:lrow + 1, :], i == 0, i == 1,
                               ["ones_b", ("rlp", fin, i)], [("ps", 7)])
                        P.op("act", lambda e: e.copy(out=bc[fin][r0:r0 + 64, :], in_=ps[7][r0:r0 + 64, :]),
                             reads=[("ps", 7)], writes=[("bc", fin)])
                        P.op("dve", lambda e, c=c: e.tensor_tensor(
                            out=bT[r0:r0 + 64, c, j * 512:(j + 1) * 512], in0=ps[ob][r0:r0 + 64, :],
                            in1=bc[fin][r0:r0 + 64, :], op=ALU.mult),
                            reads=[("ps", ob), ("bc", fin)], writes=[("bT", c, j, hh)])
                    pend.append((step + 3, fin2))

                ug0 = ugc[0]
                for t in range(len(units) + LA):
                    if t < len(units):
                        qk_exp(*units[t], ug0 + t)
                    if t - LA >= 0:
                        pv(*units[t - LA], ug0 + t - LA, t)
                    while pend and pend[0][0] <= t:
                        pend.pop(0)[1]()
                while pend:
                    pend.pop(0)[1]()
                ugc[0] += len(units)

        if upto == 'B':
            raise _Stop()
        P.barrier()
        Wuv = sb(G0, [128, 8, 2048], BF16)
        o = SCR
        ub = [sb(o, [128, D], BF16), sb(o + 2048, [128, D], BF16)]; o += 4096
        v32 = [sb(o, [128, D], F32), sb(o + 4096, [128, D], F32)]; o += 8192
        vln = [sb(o, [128, D], BF16), sb(o + 2048, [128, D], BF16)]; o += 4096
        ab = [sb(o, [128, D], BF16), sb(o + 2048, [128, D], BF16)]; o += 4096
        lng = sb(o, [128, D], F32); o += 4096
        lnb = sb(o, [128, D], F32); o += 4096
        WsT = sb(o, [128, 8, 128], BF16); o += 2048
        sqjC = sb(o, [128, D], F32); o += 4096
        for q in range(4):
            P.dma("pool", Wuv[:, :, q * 512:(q + 1) * 512], wsrc(w_in, q * 512, (q + 1) * 512), writes=[("Wuv", q)])
        P.dma("sp", lng[:], ln_g.partition_broadcast(128), writes=["lng"])
        P.dma("sp", lnb[:], ln_b.partition_broadcast(128), writes=["lnb"])
        P.dma("pool", WsT[:], ws_t, writes=["WsT"])
        P.op("pool", lambda e: e.memset(WsT[64:128, :, 0:64], 0.0), reads=["WsT"], writes=["WsT"])

        def C0_(b):
            for ct in range(4):
                bk = ct
                for kc in range(8):
                    mm(ps[bk][:], hT[:, kc, b * 128:(b + 1) * 128], Wuv[:, kc, ct * 512:(ct + 1) * 512], kc == 0, kc == 7,
                       [("hT", b), ("Wuv", ct)], [("ps", bk)])
                if ct < 2:
                    P.op("act", lambda e, b=b, ct=ct, bk=bk: e.activation(
                        out=ub[b % 2][:, ct * 512:(ct + 1) * 512], in_=ps[bk][:], func=AF.Gelu_apprx_tanh),
                        reads=[("ps", bk)], writes=[("ub", b % 2, ct)])
                else:
                    P.op("act", lambda e, b=b, ct=ct, bk=bk: e.activation(
                        out=v32[b % 2][:, (ct - 2) * 512:(ct - 1) * 512], in_=ps[bk][:], func=AF.Gelu_apprx_tanh,
                        accum_out=lnst[:, b, ct - 2:ct - 1]),
                        reads=[("ps", bk), "lnst"], writes=[("v32", b % 2, ct - 2), ("lnst", b, ct - 2)])

        def C1_(b):
            s = b % 2
            L = lambda i: lnst[:, b, i:i + 1]
            lt = ("lnstb", b)
            P.op("act", lambda e: e.activation(out=sqjC[:], in_=v32[s][:], func=AF.Square, accum_out=L(2)),
                 reads=[("v32", s, 0), ("v32", s, 1), "lnst"], writes=["sqjC", lt])
            P.op("dve", lambda e: e.tensor_tensor(out=L(0), in0=L(0), in1=L(1), op=ALU.add),
                 reads=[("lnst", b, 0), ("lnst", b, 1)], writes=[("lnst", b, 0)])
            P.op("dve", lambda e: e.tensor_scalar(out=L(0), in0=L(0), scalar1=1.0 / D, scalar2=0.0, op0=ALU.mult, op1=ALU.add),
                 reads=[("lnst", b, 0)], writes=[("lnst", b, 0)])
            P.op("dve", lambda e: e.tensor_tensor(out=L(1), in0=L(0), in1=L(0), op=ALU.mult),
                 reads=[("lnst", b, 0)], writes=[("lnst", b, 1)])
            P.op("dve", lambda e: e.scalar_tensor_tensor(out=L(3), in0=L(2), scalar=1.0 / D, in1=L(1),
                                                          op0=ALU.mult, op1=ALU.subtract),
                 reads=[lt, ("lnst", b, 1)], writes=[lt])
            P.op("dve", lambda e: e.tensor_scalar(out=L(3), in0=L(3), scalar1=EPS, scalar2=0.0, op0=ALU.add, op1=ALU.add),
                 reads=[lt], writes=[lt])
            P.op("pool", lambda e: e.tensor_tensor(out=L(4), in0=L(3), in1=nh[:, 0:1], op=ALU.pow),
                 reads=[lt, "nh"], writes=[lt])
            P.op("dve", lambda e: e.scalar_tensor_tensor(out=L(5), in0=L(0), scalar=-1.0, in1=L(4),
                                                          op0=ALU.mult, op1=ALU.mult),
                 reads=[lt, ("lnst", b, 0)], writes=[lt])
            P.op("dve", lambda e: e.tensor_scalar(out=v32[s][:], in0=v32[s][:], scalar1=L(4), scalar2=L(5),
                                                   op0=ALU.mult, op1=ALU.add),
                 reads=[lt, ("v32", s, 0), ("v32", s, 1)], writes=[("v32", s, 0), ("v32", s, 1)])
            P.op("dve", lambda e: e.tensor_tensor(out=v32[s][:], in0=v32[s][:], in1=lng[:], op=ALU.mult),
                 reads=[("v32", s, 0), ("v32", s, 1), "lng"], writes=[("v32", s, 0), ("v32", s, 1)])
            P.op("dve", lambda e: e.tensor_tensor(out=vln[s][:], in0=v32[s][:], in1=lnb[:], op=ALU.add),
                 reads=[("v32", s, 0), ("v32", s, 1), "lnb"], writes=[("vln", s)])
            for g in range(8):
                bk = 4 + g // 4
                mm(ps[bk][:, (g % 4) * 128:(g % 4 + 1) * 128], WsT[:, g, :], vln[s][:, g * 128:(g + 1) * 128], True, True,
                   ["WsT", ("vln", s)], [("ps", bk)])

        def C2_(b):
            s = b % 2
            for g in range(8):
                bk = 4 + g // 4
                P.op("dve", lambda e, g=g, bk=bk: e.scalar_tensor_tensor(
                    out=ab[s][:, g * 128:(g + 1) * 128], in0=ps[bk][:, (g % 4) * 128:(g % 4 + 1) * 128],
                    scalar=bs[:, g:g + 1], in1=ub[s][:, g * 128:(g + 1) * 128], op0=ALU.add, op1=ALU.mult),
                    reads=[("ps", bk), "bs", ("ub", s, g // 4)], writes=[("ab", s)])
            tr8(ab[s], ("ab", s), 6, aT[:, :, b * 128:(b + 1) * 128], ("aT", b))

        for t in range(NB + 2):
            if 0 <= t - 2 < NB:
                C2_(t - 2)
            if 0 <= t - 1 < NB:
                C1_(t - 1)
            if t < NB:
                C0_(t)

        if upto == 'C':
            raise _Stop()
        P.barrier()
        dump("hT", hT[:])
        dump("bT", bT[:])
        dump("aT", aT[:])
        o = SCR
        wsm = [[sb(o + (s * 4 + i) * 2048, [128, 8, 128], BF16) for i in range(4)] for s in range(2)]; o += 16384
        gsb = [[sb(o + (s * 2 + i) * 2048, [128, 512], F32) for i in range(2)] for s in range(2)]; o += 8192
        tsb = [[sb(o + (s * 2 + i) * 2048, [128, 512], F32) for i in range(2)] for s in range(2)]; o += 8192
        Wout = sb(o, [128, 8, D], BF16); o += 16384
        oE = o
        P.dma("pool", Wout[:, :, 0:512], wsrc(w_out, 0, 512), writes=[("Wout", 0)])
        P.dma("pool", Wout[:, :, 512:1024], wsrc(w_out, 512, 1024), writes=[("Wout", 1)])
        step = 0

        def loadD(m):
            srcs = (wsrc(w_in, O_G + m * 128, O_G + (m + 1) * 128),
                    wsrc(w_in, O_G + D + m * 128, O_G + D + (m + 1) * 128),
                    wsrc(w_a, m * 128, (m + 1) * 128), wsrc(w_b, m * 128, (m + 1) * 128))
            for i in range(4):
                P.dma("pool", wsm[m % 2][i][:], srcs[i], writes=[("wsm", m % 2, i)])

        loadD(0)
        for m in range(8):
            s = m % 2
            if m + 1 < 8:
                loadD(m + 1)
            for j in range(NT):
                pb = 4 * (step % 2)
                sg = step % 2
                step += 1
                acts = (hT, hT, aT, bT)
                for i in range(4):
                    for kc in range(8):
                        if i < 2:
                            rd = [("hT", 4 * j + u) for u in range(4)]
                        elif i == 2:
                            rd = [("aT", 4 * j + u) for u in range(4)]
                        else:
                            rd = [("bT", kc, j, 0), ("bT", kc, j, 1)]
                        mm(ps[pb + i][:], wsm[s][i][:, kc, :], acts[i][:, kc, j * 512:(j + 1) * 512], kc == 0, kc == 7,
                           rd + [("wsm", s, i)], [("ps", pb + i)])
                for i in range(2):
                    P.op("act", lambda e, i=i, pb=pb, sg=sg: e.activation(out=gsb[sg][i][:], in_=ps[pb + i][:], func=AF.Sigmoid),
                         reads=[("ps", pb + i)], writes=[("gsb", sg, i)])
                for i in range(2):
                    P.op("dve", lambda e, i=i, pb=pb, sg=sg: e.tensor_tensor(out=tsb[sg][i][:], in0=ps[pb + 2 + i][:],
                                                                           in1=gsb[sg][i][:], op=ALU.mult),
                         reads=[("ps", pb + 2 + i), ("gsb", sg, i)], writes=[("tsb", sg, i)])
                P.op("pool", lambda e, sg=sg, m=m, j=j: e.tensor_tensor(out=g0[:, m, j * 512:(j + 1) * 512], in0=tsb[sg][0][:],
                                                                       in1=tsb[sg][1][:], op=ALU.add),
                     reads=[("tsb", sg, 0), ("tsb", sg, 1)], writes=[("mg", m, j)])

        if upto == 'D':
            raise _Stop()
        P.barrier()
        dump("mg", g0[:])
        o = oE
        xr = [sb(o, [128, D], F32), sb(o + 4096, [128, D], F32)]; o += 8192
        hbE = [sb(o, [128, D], BF16), sb(o + 2048, [128, D], BF16)]; o += 4096
        gvE = sb(o, [128, D], F32); o += 4096
        sqjE = sb(o, [128, D], F32); o += 4096
        assert o <= SCR_END
        P.dma("sp", gvE[:], norm_ffn_g.partition_broadcast(128), writes=["gv"])
        for b in range(NB):
            j = b // 4
            P.dma("sp", xr[b % 2][:], x[b * 128:(b + 1) * 128, :], writes=[("xr", b % 2)])
            for ch in range(2):
                bk = 2 * (b % 2) + ch
                for m in range(8):
                    mm(ps[bk][:], g0[:, m, b * 128:(b + 1) * 128], Wout[:, m, ch * 512:(ch + 1) * 512], m == 0, m == 7,
                       [("mg", m, j), ("Wout", ch)], [("ps", bk)])
                P.op("dve", lambda e, b=b, ch=ch, bk=bk: e.tensor_tensor(
                    out=x1[:, b, ch * 512:(ch + 1) * 512], in0=ps[bk][:], in1=xr[b % 2][:, ch * 512:(ch + 1) * 512], op=ALU.add),
                    reads=[("ps", bk), ("xr", b % 2)], writes=[("x1", b, ch)])
            norm_block(1, b, x1[:, b, :], gvE[:], hbE[b % 2][:], sqjE[:], 4 + 2 * (b % 2),
                       h2T[:, :, b * 128:(b + 1) * 128], ("x1", b, 0), ("hbE", b % 2), ("h2T", b),
                       extra_tok=[("x1", b, 1)])

        if upto == 'E':
            raise _Stop()
        P.barrier()
        dump("x1", x1[:])
        NG = 11
        actT = sb(SCR, [128, NG, S], BF16)
        Wd = sb(SCR + 45056, [128, NG, D], BF16)
        o = SCR + 67584
        Ag = [sb(o, [128, 512], F32), sb(o + 2048, [128, 512], F32)]; o += 4096
        Al = [sb(o, [128, 512], F32), sb(o + 2048, [128, 512], F32)]; o += 4096
        assert o <= SCR_END
        o = G0
        NW = 3
        wup = [[sb(o + (s * 2 + i) * 2048, [128, 8, 128], BF16) for i in range(2)] for s in range(NW)]; o += NW * 4096
        NR = 3
        Rg = [sb(o + s * 2080, [128, 514], F32) for s in range(NR)]; o += NR * 2080
        Rl = [sb(o + s * 2080, [128, 514], F32) for s in range(NR)]; o += NR * 2080
        Gg = [sb(o, [128, 512], F32), sb(o + 2048, [128, 512], F32)]; o += 4096
        assert o <= G0 + 32 * KB
        rstep = 0
        _KF = int(os.environ.get('KDBG_F', '9'))

        def loadF(fc):
            P.dma("pool", wup[fc % NW][0][:], wsrc(w_up, fc * 128, (fc + 1) * 128), writes=[("wup", fc % NW, 0)])
            P.dma("pool", wup[fc % NW][1][:], wsrc(w_up, DFF + fc * 128, DFF + (fc + 1) * 128), writes=[("wup", fc % NW, 1)])

        loadF(0)
        loadF(1)
        for grp in range(2):
            for q in range(2):
                P.dma("pool", Wd[:, :, q * 512:(q + 1) * 512],
                      w_down[grp * NG * 128:(grp + 1) * NG * 128, q * 512:(q + 1) * 512].rearrange("(c p) n -> p c n", p=128),
                      writes=[("Wd", q)])
            for fl in range(NG):
                fc = grp * NG + fl
                ws_ = fc % NW
                if fc + 2 < NFC:
                    loadF(fc + 2)
                for j in range(NT):
                    rs = rstep % NR
                    rp = (rstep - 1) % NR
                    sa = rstep % 2
                    pb = 2 * (rstep % 2)
                    rstep += 1
                    for i in range(2):
                        for kc in range(8):
                            mm(ps[pb + i][:], wup[ws_][i][:, kc, :], h2T[:, kc, j * 512:(j + 1) * 512], kc == 0, kc == 7,
                               [("h2T", 4 * j + u) for u in range(4)] + [("wup", ws_, i)], [("ps", pb + i)])
                    for i, (R, A, ci) in enumerate(((Rg, Ag, fc), (Rl, Al, NFC + fc))):
                        rt = ("R", i, rs)
                        P.op("act", lambda e, R=R, i=i, pb=pb, rs=rs: e.copy(out=R[rs][:, 2:514], in_=ps[pb + i][:]),
                             reads=[("ps", pb + i)], writes=[rt])
                        rn = (rs + 1) % NR
                        if j == 0:
                            P.op("dve", lambda e, R=R, rs=rs: e.memset(R[rs][:, 0:2], 0.0), writes=[("Rh", i, rs)])
                        if j < NT - 1:
                            P.op("act", lambda e, R=R, i=i, pb=pb, rn=rn: e.copy(out=R[rn][:, 0:2], in_=ps[pb + i][:, 510:512]),
                                 reads=[("ps", pb + i)], writes=[("Rh", i, rn)])
                        at = ("A", i, sa)
                        if _KF < 2:
                            continue
                        P.op("dve", lambda e, A=A, R=R, rs=rs, sa=sa, ci=ci: e.tensor_scalar(
                            out=A[sa][:], in0=R[rs][:, 2:514], scalar1=cw[:, ci, 2:3], scalar2=cb[:, ci:ci + 1],
                            op0=ALU.mult, op1=ALU.add), reads=[rt, "cw", "cb"], writes=[at])
                        _f2 = os.environ.get('KDBG_F2', '')
                        if _f2 == 'a':
                            continue
                        w1s, w0s = (slice(1, 513), slice(0, 512)) if _f2 != 'al' else (slice(2, 514), slice(2, 514))
                        P.op("dve", lambda e, A=A, R=R, rs=rs, sa=sa, ci=ci, w1s=w1s: e.scalar_tensor_tensor(
                            out=A[sa][:], in0=R[rs][:, w1s], scalar=cw[:, ci, 1:2], in1=A[sa][:],
                            op0=ALU.mult, op1=ALU.add), reads=[rt, ("Rh", i, rs), at, "cw"], writes=[at])
                        P.op("dve", lambda e, A=A, R=R, rs=rs, sa=sa, ci=ci, w0s=w0s: e.scalar_tensor_tensor(
                            out=A[sa][:], in0=R[rs][:, w0s], scalar=cw[:, ci, 0:1], in1=A[sa][:],
                            op0=ALU.mult, op1=ALU.add), reads=[rt, ("Rh", i, rs), at, "cw"], writes=[at])
                    if _KF < 3:
                        continue
                    P.op("act", lambda e, sa=sa: e.activation(out=Gg[sa][:], in_=Ag[sa][:], func=AF.Gelu_apprx_tanh),
                         reads=[("A", 0, sa)], writes=[("Gg", sa)])
                    P.op("pool", lambda e, sa=sa, fl=fl, j=j: e.tensor_tensor(
                        out=actT[:, fl, j * 512:(j + 1) * 512], in0=Gg[sa][:], in1=Al[sa][:], op=ALU.mult),
                        reads=[("Gg", sa), ("A", 1, sa)], writes=[("actT", fl, j)])
            for b in range(NB if _KF >= 4 else 0):
                j = b // 4
                for ch in range(2):
                    bk = 4 + 2 * (b % 2) + ch
                    for fl in range(NG):
                        mm(ps[bk][:], actT[:, fl, b * 128:(b + 1) * 128], Wd[:, fl, ch * 512:(ch + 1) * 512],
                           fl == 0, fl == NG - 1, [("actT", fl, j), ("Wd", ch)], [("ps", bk)])
                    P.op("dve", lambda e, b=b, ch=ch, bk=bk: e.tensor_tensor(
                        out=x1[:, b, ch * 512:(ch + 1) * 512], in0=ps[bk][:], in1=x1[:, b, ch * 512:(ch + 1) * 512], op=ALU.add),
                        reads=[("ps", bk), ("x1", b, ch)], writes=[("x1", b, ch)])

        if upto == 'F':
            raise _Stop()
        P.barrier()
        dump("x2", x1[:])
        o = SCR
        Wg = sb(o, [128, 8, D], BF16); o += 16384
        Wp = sb(o, [128, 2, D], BF16); o += 4096
        pst = [sb(o, [128, PLE], F32), sb(o + 1024, [128, PLE], F32)]; o += 2048
        pb16 = [sb(o, [128, PLE], BF16), sb(o + 512, [128, PLE], BF16)]; o += 1024
        pTb = [sb(o, [128, 2, 128], BF16), sb(o + 512, [128, 2, 128], BF16)]; o += 1024
        hbG = [sb(o, [128, D], BF16), sb(o + 2048, [128, D], BF16)]; o += 4096
        h3Tb = [sb(o, [128, 8, 128], BF16), sb(o + 2048, [128, 8, 128], BF16)]; o += 4096
        sgt = [[sb(o + (s * 2 + i) * 2048, [128, 512], F32) for i in range(2)] for s in range(2)]; o += 8192
        tt = [sb(o, [128, D], F32), sb(o + 4096, [128, D], F32)]; o += 8192
        ot = [sb(o, [128, D], F32), sb(o + 4096, [128, D], F32)]; o += 8192
        gv3 = sb(o, [128, D], F32); o += 4096
        gvf = sb(o, [128, D], F32); o += 4096
        sqjG = sb(o, [128, D], F32); o += 4096
        assert o <= SCR_END
        P.dma("pool", Wg[:, :, 0:512], wsrc(w_pg, 0, 512), writes=[("Wg", 0)])
        P.dma("pool", Wg[:, :, 512:1024], wsrc(w_pg, 512, 1024), writes=[("Wg", 1)])
        P.dma("pool", Wp[:], wsrc(w_ple, 0, D), writes=["Wp"])
        P.dma("sp", gv3[:], norm_ple_g.partition_broadcast(128), writes=["gv"])
        P.dma("sp", gvf[:], norm_final_g.partition_broadcast(128), writes=["gvf"])

        def G0_(b):
            s = b % 2
            P.dma("sp", pst[s][:], pin[b * 128:(b + 1) * 128, :], writes=[("pst", s)])
            P.op("act", lambda e: e.copy(out=pb16[s][:], in_=pst[s][:]), reads=[("pst", s)], writes=[("pb16", s)])
            tr8(pb16[s], ("pb16", s), 4, pTb[s][:], ("pTb", s), nch=2)
            norm_block(2, b, x1[:, b, :], gv3[:], hbG[s][:], sqjG[:], 0, h3Tb[s][:],
                       ("x1", b, 0), ("hbG", s), ("h3Tb", s), extra_tok=[("x1", b, 1)])

        def G1_(b):
            s = b % 2
            for ch in range(2):
                bk = 2 + ch
                for kc in range(8):
                    mm(ps[bk][:], h3Tb[s][:, kc, :], Wg[:, kc, ch * 512:(ch + 1) * 512], kc == 0, kc == 7,
                       [("h3Tb", s), ("Wg", ch)], [("ps", bk)])
                P.op("act", lambda e, ch=ch, bk=bk: e.activation(out=sgt[s][ch][:], in_=ps[bk][:], func=AF.Sigmoid),
                     reads=[("ps", bk)], writes=[("sgt", s, ch)])
                bk2 = 6 + ch
                for kc in range(2):
                    mm(ps[bk2][:], pTb[s][:, kc, :], Wp[:, kc, ch * 512:(ch + 1) * 512], kc == 0, kc == 1,
                       [("pTb", s), "Wp"], [("ps", bk2)])
                P.op("dve", lambda e, ch=ch, bk2=bk2: e.tensor_tensor(
                    out=tt[s][:, ch * 512:(ch + 1) * 512], in0=ps[bk2][:], in1=sgt[s][ch][:], op=ALU.mult),
                    reads=[("ps", bk2), ("sgt", s, ch)], writes=[("tt", s, ch)])
                P.op("dve", lambda e, ch=ch: e.tensor_tensor(
                    out=tt[s][:, ch * 512:(ch + 1) * 512], in0=tt[s][:, ch * 512:(ch + 1) * 512],
                    in1=x1[:, b, ch * 512:(ch + 1) * 512], op=ALU.add),
                    reads=[("tt", s, ch), ("x1", b, ch)], writes=[("tt", s, ch)])
            ssc = stat[:, 3, 0, b:b + 1]
            msc = stat[:, 3, 1, b:b + 1]
            rsc = stat[:, 3, 2, b:b + 1]
            st = ("stat", 3, b)
            P.op("act", lambda e: e.activation(out=sqjG[:], in_=tt[s][:], func=AF.Square, accum_out=ssc),
                 reads=[("tt", s, 0), ("tt", s, 1), "stat"], writes=["sqj", st])
            P.op("dve", lambda e: e.tensor_scalar(out=msc, in0=ssc, scalar1=1.0 / D, scalar2=EPS, op0=ALU.mult, op1=ALU.add),
                 reads=[st], writes=[st])
            P.op("pool", lambda e: e.tensor_tensor(out=rsc, in0=msc, in1=nh[:, 0:1], op=ALU.pow), reads=[st, "nh"], writes=[st])
            P.op("dve", lambda e: e.scalar_tensor_tensor(out=ot[s][:], in0=tt[s][:], scalar=rsc, in1=gvf[:],
                                                          op0=ALU.mult, op1=ALU.mult),
                 reads=[("tt", s, 0), ("tt", s, 1), st, "gvf"], writes=[("ot", s)])
            P.dma("sp", out[b * 128:(b + 1) * 128, :], ot[s][:], reads=[("ot", s)], writes=[("out", b)])

        for t in range(NB + 1):
            if 0 <= t - 1 < NB:
                G1_(t - 1)
            if t < NB:
                G0_(t)
        P.op("sp", lambda e: e.nop(), reads=[("out", b) for b in range(NB)])
        P.barrier()

    except _Stop:
        P.barrier()

    emit(nc, P)
    es.close()
    return nc


_CACHE = {}


def _prep_shared(inp):
    f = np.float32
    d = {}
    d["norm_mix_g"] = np.ascontiguousarray(inp["norm_mix_g"][0], f)
    d["w_in"] = np.ascontiguousarray(inp["w_in"][0], f)
    d["b_f"] = np.ascontiguousarray(inp["b_f"][0], f)
    d["gmlp_ln_g"] = np.ascontiguousarray(inp["gmlp_ln_g"][0], f)
    d["gmlp_ln_b"] = np.ascontiguousarray(inp["gmlp_ln_b"][0], f)
    d["ws_t"] = np.ascontiguousarray(np.transpose(inp["gmlp_w_s"][0], (2, 0, 1)), f)
    d["bs_t"] = np.ascontiguousarray(inp["gmlp_b_s"][0].T, f)
    d["w_branch_a"] = np.ascontiguousarray(inp["w_branch_a"][0], f)
    d["w_branch_b"] = np.ascontiguousarray(inp["w_branch_b"][0], f)
    d["w_out"] = np.ascontiguousarray(inp["w_out"][0], f)
    d["norm_ffn_g"] = np.ascontiguousarray(inp["norm_ffn_g"][0], f)
    d["w_up"] = np.ascontiguousarray(inp["w_up"][0], f)
    cwv = np.asarray(inp["conv_w"][0], f)
    d["cw_t"] = np.ascontiguousarray(cwv.T.reshape(44, 128, 3).transpose(1, 0, 2), f)
    d["cb_t"] = np.ascontiguousarray(np.asarray(inp["conv_b"][0], f).reshape(44, 128).T, f)
    d["w_down"] = np.ascontiguousarray(inp["w_down"][0], f)
    d["norm_ple_g"] = np.ascontiguousarray(inp["norm_ple_g"][0], f)
    d["w_ple"] = np.ascontiguousarray(inp["w_ple"][0], f)
    d["w_ple_gate"] = np.ascontiguousarray(inp["w_ple_gate"][0], f)
    d["norm_final_g"] = np.ascontiguousarray(inp["norm_final_g"], f)
    d["c_ident"] = np.eye(128, dtype=f)
    r = np.arange(128)
    d["c_tri"] = (r[:, None] <= r[None, :]).astype(f)
    d["c_sel"] = np.ascontiguousarray(np.broadcast_to((r[:, None] <= 63), (128, 128))).astype(f)
    return d


def kernel(**inputs):
    inp = {k: np.asarray(v) for k, v in inputs.items()}
    n = 8
    if "nc" not in _CACHE:
        _CACHE["nc"] = build_nc()
    nc = _CACHE["nc"]
    shared = _prep_shared(inp)
    x = np.asarray(inp["x"], np.float32)
    p = np.asarray(inp["p"], np.float32)[0]
    in_maps = []
    for i in range(n):
        m = dict(shared)
        m["x"] = np.ascontiguousarray(x[i])
        m["p"] = np.ascontiguousarray(p[i])
        in_maps.append(m)
    res = run_bass_kernel_spmd(nc, in_maps, core_ids=list(range(n)))
    return np.stack([np.asarray(r["out"], np.float32) for r in res.results], axis=0)
```

```python
import contextlib
import numpy as np
import concourse.bass as bass
import concourse.mybir as mybir
from concourse.bass_utils import run_bass_kernel_spmd

F32 = mybir.dt.float32
BF16 = mybir.dt.bfloat16
AF = mybir.ActivationFunctionType
ALU = mybir.AluOpType

ENGS = ("pe", "act", "dve", "pool", "sp")

S = 2048
D = 1024
NB = 16
NT = 4
DFF = 2816
NFC = 22
IN_COLS = 7184
O_Q = 2048
O_K = 3072
O_V = 4096
O_F = 5120
O_G = 5136
PLE = 256
EPS = 1e-6
KB = 1024
BASE = 16512
import os
_STG = int(os.environ.get('KDBG_STG', '6'))
_NBA = int(os.environ.get('KDBG_NBA', '16'))


class _Op:
    __slots__ = ("fn", "deps", "dma", "flag", "semval", "dsem", "dcount", "waits", "prewait")

    def __init__(self, fn, deps, dma):
        self.fn = fn
        self.deps = deps
        self.dma = dma
        self.flag = False
        self.semval = None
        self.dsem = None
        self.dcount = None
        self.waits = []
        self.prewait = None


class Prog:
    def __init__(self, nc, n_dma_sems=32):
        self.nc = nc
        self.streams = {e: [] for e in ENGS}
        self.last_writer = {}
        self.readers = {}
        self.n_dma_sems = n_dma_sems

    def op(self, eng, fn, reads=(), writes=(), dma=False, extra=None):
        st = self.streams[eng]
        me = (eng, len(st))
        deps = {}
        for t in reads:
            w = self.last_writer.get(t)
            if w is not None:
                deps[w] = True
        for t in writes:
            w = self.last_writer.get(t)
            if w is not None and w not in deps:
                deps[w] = False
            for r in self.readers.get(t, ()):
                if r not in deps:
                    deps[r] = False
        if extra:
            for d in extra:
                deps[d] = True
        deps.pop(me, None)
        for t in writes:
            self.last_writer[t] = me
            self.readers[t] = []
        for t in reads:
            if self.last_writer.get(t) != me:
                self.readers.setdefault(t, []).append(me)
        st.append(_Op(fn, deps, dma))
        return me

    def dma(self, eng, out, in_, reads=(), writes=()):
        return self.op(eng, lambda e: e.dma_start(out=out, in_=in_), reads, writes, dma=True)

    def barrier(self):
        lasts = []
        for e in ENGS:
            st = self.streams[e]
            for i in range(len(st) - 1, -1, -1):
                if not st[i].dma:
                    lasts.append((e, i))
                    break
            n = 0
            for i in range(len(st) - 1, -1, -1):
                if st[i].dma:
                    lasts.append((e, i))
                    n += 1
                    if n >= 16:
                        break
        for e in ENGS:
            self.op(e, lambda g: g.nop(), extra=[d for d in lasts])

    def resolve(self):
        streams = self.streams
        dma_engs = [e for e in ENGS if any(o.dma for o in streams[e])]
        pools = {}
        if dma_engs:
            per = max(2, self.n_dma_sems // len(dma_engs))
            for e in dma_engs:
                pools[e] = min(per, 16)
        self.dma_pool_sizes = pools
        for e in dma_engs:
            k = 0
            counts = [0] * pools[e]
            lastop = [None] * pools[e]
            for i, o in enumerate(streams[e]):
                if not o.dma:
                    continue
                s = k % pools[e]
                k += 1
                if lastop[s] is not None:
                    o.prewait = ((e, s), counts[s])
                counts[s] += 16
                o.dsem = (e, s)
                o.dcount = counts[s]
                lastop[s] = i
        for e in ENGS:
            seen = {f: -1 for f in ENGS}
            seen_d = {}
            for i, o in enumerate(streams[e]):
                if o.prewait is not None:
                    key, cnt = o.prewait
                    if seen_d.get(key, 0) >= cnt:
                        o.prewait = None
                    else:
                        seen_d[key] = cnt
                for (f, n), is_raw in sorted(o.deps.items(), key=lambda kv: (kv[0][0], -kv[0][1])):
                    p = streams[f][n]
                    if p.dma:
                        if seen_d.get(p.dsem, 0) >= p.dcount:
                            continue
                        seen_d[p.dsem] = p.dcount
                        o.waits.append(("d", p.dsem, p.dcount))
                        continue
                    if f == e and not o.dma:
                        if not (is_raw and e != "pe"):
                            continue
                    if n <= seen[f]:
                        continue
                    seen[f] = n
                    p.flag = True
                    o.waits.append(("c", f, n))
        for e in ENGS:
            c = 0
            for o in streams[e]:
                if o.flag:
                    c += 1
                    o.semval = c


def emit(nc, prog):
    prog.resolve()
    with contextlib.ExitStack() as es:
        csem = {e: es.enter_context(nc.semaphore("c_" + e)) for e in ENGS}
        dsem = {}
        for e, n in prog.dma_pool_sizes.items():
            for s in range(n):
                dsem[(e, s)] = es.enter_context(nc.semaphore("d_%s_%d" % (e, s)))
        block = es.enter_context(nc.Block())

        def run(ename, eng):
            for o in prog.streams[ename]:
                if o.prewait is not None:
                    key, cnt = o.prewait
                    eng.wait_ge(dsem[key], cnt)
                for w in o.waits:
                    if w[0] == "d":
                        eng.wait_ge(dsem[w[1]], w[2])
                    else:
                        eng.wait_ge(csem[w[1]], prog.streams[w[1]][w[2]].semval)
                ins = o.fn(eng)
                if o.dma:
                    ins.then_inc(dsem[o.dsem], 16)
                elif o.flag:
                    ins.then_inc(csem[ename], 1)

        @block.tensor
        def _(eng):
            run("pe", eng)

        @block.scalar
        def _(eng):
            run("act", eng)

        @block.vector
        def _(eng):
            run("dve", eng)

        @block.gpsimd
        def _(eng):
            run("pool", eng)

        @block.sync
        def _(eng):
            run("sp", eng)


class _Stop(Exception):
    pass


def build_nc(debug=False, upto=None):
    nc = bass.Bass("TRN2", target_bir_lowering=False)

    def din(name, shape):
        return nc.dram_tensor(name, list(shape), F32, kind="ExternalInput").ap()

    x = din("x", [S, D])
    pin = din("p", [S, PLE])
    norm_mix_g = din("norm_mix_g", [D])
    w_in = din("w_in", [D, IN_COLS])
    b_f = din("b_f", [16])
    ln_g = din("gmlp_ln_g", [D])
    ln_b = din("gmlp_ln_b", [D])
    ws_t = din("ws_t", [128, 8, 128])
    bs_t = din("bs_t", [128, 8])
    w_a = din("w_branch_a", [D, D])
    w_b = din("w_branch_b", [D, D])
    w_out = din("w_out", [D, D])
    norm_ffn_g = din("norm_ffn_g", [D])
    w_up = din("w_up", [D, 2 * DFF])
    cw_t = din("cw_t", [128, 44, 3])
    cb_t = din("cb_t", [128, 44])
    w_down = din("w_down", [DFF, D])
    norm_ple_g = din("norm_ple_g", [D])
    w_ple = din("w_ple", [PLE, D])
    w_pg = din("w_ple_gate", [D, D])
    norm_final_g = din("norm_final_g", [D])
    c_ident = din("c_ident", [128, 128])
    c_tri = din("c_tri", [128, 128])
    c_sel = din("c_sel", [128, 128])
    out = nc.dram_tensor("out", [S, D], F32, kind="ExternalOutput").ap()
    dbg = {}
    if debug:
        for nm in ("hT", "bT", "aT", "mg"):
            dbg[nm] = nc.dram_tensor("dbg_" + nm, [128, 8, S], BF16, kind="ExternalOutput").ap()
        for nm in ("x1", "x2"):
            dbg[nm] = nc.dram_tensor("dbg_" + nm, [128, NB, D], F32, kind="ExternalOutput").ap()

    def dump(nm, src):
        if debug:
            P.dma("sp", dbg[nm], src)
            P.barrier()

    cnt = [0]

    def sb(off, shape, dt):
        cnt[0] += 1
        return nc.alloc_sbuf_tensor_at("t%d" % cnt[0], list(shape), dt, offset=off)

    C0 = BASE
    ident = sb(C0 + 0, [128, 128], BF16)
    onesf = sb(C0 + 256, [128, 128], F32)
    trif = sb(C0 + 768, [128, 128], F32)
    self_ = sb(C0 + 1280, [128, 128], F32)
    maskb = sb(C0 + 1792, [128, 128], BF16)
    cw = sb(C0 + 2048, [128, 44, 3], F32)
    cb = sb(C0 + 2592, [128, 44], F32)
    bs = sb(C0 + 2784, [128, 8], F32)
    bfb = sb(C0 + 2816, [128, 16], F32)
    nh = sb(C0 + 2880, [128, 16], F32)
    stat = sb(C0 + 2944, [128, 4, 3, 16], F32)
    lnst = sb(C0 + 3712, [128, 16, 6], F32)
    G0 = C0 + 4 * KB
    HT = C0 + 36 * KB
    BT = C0 + 68 * KB
    AT = C0 + 100 * KB
    SCR = C0 + 132 * KB
    SCR_END = 229344

    g0 = sb(G0, [128, 8, S], BF16)
    hT = sb(HT, [128, 8, S], BF16)
    bT = sb(BT, [128, 8, S], BF16)
    aT = sb(AT, [128, 8, S], BF16)
    x1 = sb(HT, [128, NB, D], F32)
    h2T = sb(AT, [128, 8, S], BF16)

    es = contextlib.ExitStack()
    ps = [es.enter_context(nc.psum_tensor("ps%d" % i, [128, 512], F32)) for i in range(8)]

    def psb(i):
        return ps[i][:].bitcast(BF16).rearrange("p (c n) -> p c n", c=8)

    P = Prog(nc)

    def mm(o, lhsT, rhs, start, stop, reads, writes, skip=False):
        if skip:
            P.op("pe", lambda e: e.matmul(o, lhsT=lhsT, rhs=rhs, start=start, stop=stop,
                                           skip_group_check=True), reads, writes)
        else:
            P.op("pe", lambda e: e.matmul(o, lhsT=lhsT, rhs=rhs, start=start, stop=stop), reads, writes)

    def wsrc(w, c0, c1):
        return w[:, c0:c1].rearrange("(c p) n -> p c n", p=128)

    try:
        P.dma("pool", ident[:], c_ident, writes=["ident"])
        P.dma("sp", trif[:], c_tri, writes=["trif"])
        P.dma("sp", self_[:], c_sel, writes=["self"])
        P.dma("pool", maskb[:], c_tri, writes=["maskb"])
        P.dma("sp", cw[:], cw_t, writes=["cw"])
        P.dma("sp", cb[:], cb_t, writes=["cb"])
        P.dma("sp", bs[:], bs_t, writes=["bs"])
        P.dma("sp", bfb[:], b_f.partition_broadcast(128), writes=["bfb"])
        P.op("pool", lambda e: e.memset(onesf[:], 1.0), writes=["onesf"])
        P.op("pool", lambda e: e.memset(nh[:], -0.5), writes=["nh"])
        P.op("pool", lambda e: e.memset(stat[:], 0.0), writes=["stat"])
        P.op("pool", lambda e: e.memset(lnst[:], 0.0), writes=["lnst"])

        if upto == 'c':
            raise _Stop()
        def tr8(src, src_tok, bank, dstT, dst_tok, nch=8):
            for c in range(nch):
                bk = bank + c // 4
                P.op("pe", lambda e, c=c, bk=bk: e.matmul(ps[bk][:, (c % 4) * 128:(c % 4 + 1) * 128],
                                                          lhsT=src[:, c * 128:(c + 1) * 128], rhs=ident[:],
                                                          start=True, stop=True),
                     reads=[src_tok, "ident"], writes=[("ps", bk)])
            for hf in range((nch + 3) // 4):
                n = min(4, nch - 4 * hf)
                pvv = ps[bank + hf][:, 0:n * 128].rearrange("p (c n) -> p c n", c=n)
                P.op("act", lambda e, hf=hf, n=n, pvv=pvv: e.copy(out=dstT[:, 4 * hf:4 * hf + n, :], in_=pvv),
                     reads=[("ps", bank + hf)], writes=[dst_tok])

        def norm_block(ni, b, src, gv, hb, sqj, bank, dstT, src_tok, hb_tok, dst_tok, extra_tok=(), defer=False):
            ssc = stat[:, ni, 0, b:b + 1]
            msc = stat[:, ni, 1, b:b + 1]
            rsc = stat[:, ni, 2, b:b + 1]
            st = ("stat", ni, b)
            P.op("act", lambda e: e.activation(out=sqj, in_=src, func=AF.Square, accum_out=ssc),
                 reads=[src_tok, "stat"] + list(extra_tok), writes=["sqj", st])
            if _STG < 2:
                return
            P.op("dve", lambda e: e.tensor_scalar(out=msc, in0=ssc, scalar1=1.0 / D, scalar2=EPS,
                                                   op0=ALU.mult, op1=ALU.add), reads=[st], writes=[st])
            if _STG < 3:
                return
            P.op("pool", lambda e: e.tensor_tensor(out=rsc, in0=msc, in1=nh[:, 0:1], op=ALU.pow),
                 reads=[st, "nh"], writes=[st])
            if _STG < 4:
                return
            P.op("dve", lambda e: e.scalar_tensor_tensor(out=hb, in0=src, scalar=rsc, in1=gv,
                                                          op0=ALU.mult, op1=ALU.mult),
                 reads=[src_tok, st, "gv"], writes=[hb_tok])
            if _STG < 5:
                return
            if defer:
                return lambda: tr8(hb, hb_tok, bank, dstT, dst_tok)
            tr8(hb, hb_tok, bank, dstT, dst_tok)

        xt = [sb(SCR + 0, [128, D], F32), sb(SCR + 4096, [128, D], F32)]
        hbA = [sb(SCR + 8192, [128, D], BF16), sb(SCR + 10240, [128, D], BF16)]
        gvA = sb(SCR + 12288, [128, D], F32)
        sqjA = sb(SCR + 16384, [128, D], F32)
        P.dma("sp", gvA[:], norm_mix_g.partition_broadcast(128), writes=["gv"])
        Vs = sb(G0, [128, 4, NB, 192], BF16)
        Wv = sb(G0 + 24576, [128, 8, 512], BF16)
        oB = SCR + 20480
        wf = sb(oB, [128, 8, 16], BF16)
        wq = [sb(oB + 29760, [128, 8, 128], BF16), sb(oB + 29760 + 2048, [128, 8, 128], BF16)]
        wk = [sb(oB + 33856, [128, 8, 128], BF16), sb(oB + 33856 + 2048, [128, 8, 128], BF16)]
        P.dma("pool", wf[:], wsrc(w_in, O_F, O_F + 16), writes=["wf"])
        P.dma("pool", Wv[:], wsrc(w_in, O_V, O_V + 512), writes=["Wv"])
        P.dma("pool", wq[0][:], wsrc(w_in, O_Q, O_Q + 128), writes=[("wq", 0)])
        P.dma("pool", wk[0][:], wsrc(w_in, O_K, O_K + 128), writes=[("wk", 0)])
        if upto == 'p':
            raise _Stop()
        trA = {}
        for t in range(NB + 1):
            if t >= 1:
                trA[t - 1]()
            if t < NB:
                b = t
                P.dma("sp", xt[b % 2][:], x[b * 128:(b + 1) * 128, :], writes=[("xt", b % 2)])
                trA[b] = norm_block(0, b, xt[b % 2][:], gvA[:], hbA[b % 2][:], sqjA[:], 2 * (b % 2),
                                    hT[:, :, b * 128:(b + 1) * 128], ("xt", b % 2), ("hbA", b % 2), ("hT", b), defer=True)

        if upto == 'A':
            raise _Stop()
        o = SCR + 20480
        o += 256
        Lf = sb(o, [128, NB, 16], F32); o += 1024
        Sacc = sb(o, [128, NB + 1, 16], F32); o += 1088
        Cc = sb(o, [128, NB, 16], F32); o += 1024
        Mm = sb(o, [128, NB, 16], F32); o += 1024
        xf = sb(o, [128, 2, 16], F32); o += 128
        ef = sb(o, [128, 2, 16], F32); o += 128
        o = (o + 63) // 64 * 64
        NPAIR = NB * (NB + 1) // 2
        biasT = sb(o, [128, NPAIR, 16], F32); o += NPAIR * 64
        qT = [sb(o, [128, S], BF16), sb(o + 4096, [128, S], BF16)]; o += 8192
        kT = [sb(o, [128, S], BF16), sb(o + 4096, [128, S], BF16)]; o += 8192
        assert o == oB + 29760, (o, oB)
        o += 8192
        NPT = 4
        Pt = [sb(o + i * 1024, [128, 512], BF16) for i in range(NPT)]; o += NPT * 1024
        rl = [sb(o, [128, 512], F32), sb(o + 2048, [128, 512], F32)]; o += 4096
        bc = [sb(o, [128, 512], F32), sb(o + 2048, [128, 512], F32)]; o += 4096
        assert o <= SCR_END, o

        def pidx(qb, kb):
            return qb * (qb + 1) // 2 + kb

        oA = AT
        tri_b = sb(oA, [128, 128], BF16); oA += 256
        sel_b = sb(oA, [128, 128], BF16); oA += 256
        ones_b = sb(oA, [128, 128], BF16); oA += 256
        Lp = [sb(oA + i * 512, [128, NB, 16], BF16) for i in range(3)]; oA += 1536
        Sp = [sb(oA + i * 544, [128, NB + 1, 16], BF16) for i in range(3)]; oA += 1632
        RL = [sb(oA + i * 1024, [128, NB, 16], F32) for i in range(2)]; oA += 2048
        RS = [sb(oA + i * 1088, [128, NB + 1, 16], F32) for i in range(2)]; oA += 2176
        rlp = [[sb(oA + (f_ * 2 + i) * 1024, [128, 512], BF16) for i in range(2)] for f_ in range(2)]; oA += 4096
        rlr = [sb(oA + f_ * 2048, [128, 512], F32) for f_ in range(2)]; oA += 4096
        P.dma("pool", tri_b[:], c_tri, writes=["tri_b"])
        P.dma("pool", sel_b[:], c_sel, writes=["sel_b"])
        P.op("pool", lambda e: e.memset(ones_b[:], 1.0), writes=["ones_b"])
        for i in range(3):
            P.op("pool", lambda e, i=i: e.memset(Sp[i][:, 0, :], 0.0), writes=[("Sp", 0, i)])

        def split(src, src_tok, pieces, tmps, key):
            cur, cur_tok = src, src_tok
            for i, pc in enumerate(pieces):
                P.op("dve", lambda e, pc=pc, cur=cur: e.tensor_copy(out=pc, in_=cur), reads=[cur_tok], writes=[key + (i,)])
                if i < len(pieces) - 1:
                    P.op("dve", lambda e, pc=pc, cur=cur, t=tmps[i]: e.tensor_tensor(out=t, in0=cur, in1=pc, op=ALU.subtract),
                         reads=[cur_tok, key + (i,)], writes=[key + ("r", i)])
                    cur, cur_tok = tmps[i], key + ("r", i)

        if not os.environ.get('KDBG_NOMS'):
            P.op("dve", lambda e: e.memset(Vs[:, :, :, 65:128], 0.0), writes=["Vs_c0"])
            P.op("dve", lambda e: e.memset(Vs[:, :, :, 64:65], 1.0), writes=["Vs_c"])
        P.op("pool", lambda e: e.memset(Sacc[:, 0, :], 0.0), writes=[("Sacc", 0)])

        for b in range(NB):
            for kc in range(8):
                mm(ps[2][:, 0:16], hT[:, kc, b * 128:(b + 1) * 128], wf[:, kc, :], kc == 0, kc == 7,
                   [("hT", b), "wf"], [("ps", 2)])
            P.op("dve", lambda e, b=b: e.tensor_tensor(out=xf[:, b % 2, :], in0=ps[2][:, 0:16], in1=bfb[:], op=ALU.add),
                 reads=[("ps", 2), "bfb"], writes=[("xf", b % 2)])
            P.op("act", lambda e, b=b: e.activation(out=ef[:, b % 2, :], in_=xf[:, b % 2, :], func=AF.Exp, scale=-1.0),
                 reads=[("xf", b % 2)], writes=[("ef", b % 2)])
            P.op("act", lambda e, b=b: e.activation(out=Lf[:, b, :], in_=ef[:, b % 2, :], func=AF.Ln, bias=onesf[:, 0:1]),
                 reads=[("ef", b % 2), "onesf"], writes=[("Lf", b)])
            P.op("dve", lambda e, b=b: e.tensor_tensor(out=Sacc[:, b + 1, :], in0=Sacc[:, b, :], in1=Lf[:, b, :], op=ALU.add),
                 reads=[("Sacc", b), ("Lf", b)], writes=[("Sacc", b + 1)])
            split(Lf[:, b, :], ("Lf", b), [Lp[i][:, b, :] for i in range(3)], [RL[i][:, b, :] for i in range(2)], ("Lp", b))
            split(Sacc[:, b + 1, :], ("Sacc", b + 1), [Sp[i][:, b + 1, :] for i in range(3)],
                  [RS[i][:, b + 1, :] for i in range(2)], ("Sp", b + 1))
        for b in range(NB):
            for (cols, mat, mtok) in ((slice(0, 16), tri_b, "tri_b"), (slice(16, 32), sel_b, "sel_b")):
                for i in range(3):
                    mm(ps[3][:, cols], mat[:], Lp[i][:, b, :], i == 0, False, [mtok, ("Lp", b, i)], [("ps", 3)])
                for i in range(3):
                    mm(ps[3][:, cols], ones_b[:], Sp[i][:, b, :], False, i == 2, ["ones_b", ("Sp", b, i)], [("ps", 3)])
            P.op("dve", lambda e, b=b: e.tensor_copy(out=Cc[:, b, :], in_=ps[3][:, 0:16]), reads=[("ps", 3)], writes=[("Cc", b)])
            P.op("dve", lambda e, b=b: e.tensor_copy(out=Mm[:, b, :], in_=ps[3][:, 16:32]), reads=[("ps", 3)], writes=[("Mm", b)])
        for qb in range(NB):
            for kb in range(qb + 1):
                P.op("pool", lambda e, qb=qb, kb=kb: e.tensor_tensor(out=biasT[:, pidx(qb, kb), :], in0=Cc[:, kb, :],
                                                                     in1=Mm[:, qb, :], op=ALU.subtract),
                     reads=[("Cc", kb), ("Mm", qb)], writes=[("biasT", qb, kb)])

        _KB = int(os.environ.get('KDBG_B', '9'))
        if _KB < 2:
            raise _Stop()
        uid = [0]
        ugc = [0]
        for gh in range(2):
            for b in range(NB):
                bk = int(os.environ.get('KDBG_VB', '4')) + (b % 2)
                for kc in range(8):
                    mm(ps[bk][:], hT[:, kc, b * 128:(b + 1) * 128], Wv[:, kc, :], kc == 0, kc == 7,
                       [("hT", b), "Wv"], [("ps", bk)])
                pv = ps[bk][:].rearrange("p (a n) -> p a n", a=4)
                _vc = int(os.environ.get('KDBG_VC', '2'))
                if _vc >= 1:
                    P.op("act", lambda e, b=b, pv=pv: e.copy(out=Vs[:, :, b, 0:64], in_=pv[:, :, 0:64]),
                         reads=[("ps", bk)], writes=[("Vs", b)])
                if _vc >= 2:
                    P.op("act", lambda e, b=b, pv=pv: e.copy(out=Vs[:, :, b, 128:192], in_=pv[:, :, 64:128]),
                         reads=[("ps", bk)], writes=[("VsB", b)])
            if _KB < 3:
                raise _Stop()
            if gh == 0:
                P.dma("pool", Wv[:], wsrc(w_in, O_V + 512, O_V + 1024), writes=["Wv"])
            for pr in range(4):
                c = gh * 4 + pr
                sl = c % 2
                for j in range(NT):
                    for (wt, dst, tok, bk) in ((wq[sl], qT[sl], "qT", 0), (wk[sl], kT[sl], "kT", 1)):
                        for kc in range(8):
                            mm(ps[bk][:], wt[:, kc, :], hT[:, kc, j * 512:(j + 1) * 512], kc == 0, kc == 7,
                               [("hT", 4 * j), ("hT", 4 * j + 1), ("hT", 4 * j + 2), ("hT", 4 * j + 3),
                                ("wq" if tok == "qT" else "wk", sl)], [("ps", bk)])
                        P.op("dve", lambda e, dst=dst, bk=bk, j=j: e.tensor_copy(out=dst[:, j * 512:(j + 1) * 512], in_=ps[bk][:]),
                             reads=[("ps", bk)], writes=[(tok, sl, j)])
                if _KB < 4:
                    raise _Stop()
                if c + 1 < 8:
                    c1 = c + 1
                    P.dma("pool", wq[c1 % 2][:], wsrc(w_in, O_Q + c1 * 128, O_Q + (c1 + 1) * 128), writes=[("wq", c1 % 2)])
                    P.dma("pool", wk[c1 % 2][:], wsrc(w_in, O_K + c1 * 128, O_K + (c1 + 1) * 128), writes=[("wk", c1 % 2)])
                units = [(hh, j, kb) for hh in range(2) for j in range(NT) for kb in range(4 * j + 4)]
                LA = 2
                tinfo = {}
                pend = []

                def qk_exp(hh, j, kb, ug):
                    h = 2 * c + hh
                    r0 = 64 * hh
                    i = kb - 4 * j
                    q0 = max(0, i) * 128
                    sbk = 2 + (ug % 3)
                    pt = Pt[ug % NPT]
                    ptok = ("Pt", ug % NPT)
                    mm(ps[sbk][:, q0:512], kT[sl][r0:r0 + 64, kb * 128:(kb + 1) * 128],
                       qT[sl][r0:r0 + 64, j * 512 + q0:(j + 1) * 512], True, True,
                       [("kT", sl, kb // 4), ("qT", sl, j)], [("ps", sbk)])
                    for sbi in range(q0 // 128, 4):
                        qb = 4 * j + sbi
                        P.op("act", lambda e, sbk=sbk, sbi=sbi, pt=pt, qb=qb, kb=kb, h=h: e.activation(
                            out=pt[:, sbi * 128:(sbi + 1) * 128], in_=ps[sbk][:, sbi * 128:(sbi + 1) * 128],
                            func=AF.Exp, bias=biasT[:, pidx(qb, kb), h:h + 1], scale=0.125),
                            reads=[("ps", sbk), ("biasT", qb, kb)], writes=[ptok])
                    if i >= 0:
                        P.op("dve", lambda e, pt=pt, i=i: e.tensor_tensor(
                            out=pt[:, i * 128:(i + 1) * 128], in0=pt[:, i * 128:(i + 1) * 128],
                            in1=maskb[:], op=ALU.mult), reads=[ptok, "maskb"], writes=[ptok])

                def pv(hh, j, kb, ug, step):
                    r0 = 64 * hh
                    lrow = 64 if hh == 0 else 0
                    vsl = slice(0, 65) if hh == 0 else slice(64, 192)
                    nk = 4 * j + 4
                    if kb == 0:
                        tinfo[(hh, j)] = (5 + (uid[0] % 2), uid[0] % 2)
                        uid[0] += 1
                    ob, fin = tinfo[(hh, j)]
                    q0 = max(0, kb - 4 * j) * 128
                    pt = Pt[ug % NPT]
                    ptok = ("Pt", ug % NPT)
                    mm(ps[ob][0:(65 if hh == 0 else 128), q0:512], Vs[:, pr, kb, vsl], pt[:, q0:512], kb == 0, kb == nk - 1,
                       [("Vs", kb), ("VsB", kb), "Vs_c", "Vs_c0", ptok], [("ps", ob)], skip=True)
                    if kb != nk - 1:
                        return
                    P.op("dve", lambda e: e.reciprocal(out=rl[fin][lrow:lrow + 1, :], in_=ps[ob][lrow:lrow + 1, :]),
                         reads=[("ps", ob)], writes=[("rl", fin)])
                    split(rl[fin][lrow:lrow + 1, :], ("rl", fin), [rlp[fin][i][lrow:lrow + 1, :] for i in range(2)],
                          [rlr[fin][lrow:lrow + 1, :]], ("rlp", fin))

                    def fin2():
                        for i in range(2):
                            mm(ps[7][:], ones_b[lrow:lrow + 1, :], rlp[fin][i][lrow:lrow + 1, :], i == 0, i == 1,
                               ["ones_b", ("rlp", fin, i)], [("ps", 7)])
                        P.op("act", lambda e: e.copy(out=bc[fin][r0:r0 + 64, :], in_=ps[7][r0:r0 + 64, :]),
                             reads=[("ps", 7)], writes=[("bc", fin)])
                        P.op("dve", lambda e, c=c: e.tensor_tensor(
                            out=bT[r0:r0 + 64, c, j * 512:(j + 1) * 512], in0=ps[ob][r0:r0 + 64, :],
                            in1=bc[fin][r0:r0 + 64, :], op=ALU.mult),
                            reads=[("ps", ob), ("bc", fin)], writes=[("bT", c, j, hh)])
                    pend.append((step + 3, fin2))

                ug0 = ugc[0]
                for t in range(len(units) + LA):
                    if t < len(units):
                        qk_exp(*units[t], ug0 + t)
                    if t - LA >= 0:
                        pv(*units[t - LA], ug0 + t - LA, t)
                    while pend and pend[0][0] <= t:
                        pend.pop(0)[1]()
                while pend:
                    pend.pop(0)[1]()
                ugc[0] += len(units)

        if upto == 'B':
            raise _Stop()
        P.barrier()
        Wuv = sb(G0, [128, 8, 2048], BF16)
        o = SCR
        ub = [sb(o, [128, D], BF16), sb(o + 2048, [128, D], BF16)]; o += 4096
        v32 = [sb(o, [128, D], F32), sb(o + 4096, [128, D], F32)]; o += 8192
        vln = [sb(o, [128, D], BF16), sb(o + 2048, [128, D], BF16)]; o += 4096
        ab = [sb(o, [128, D], BF16), sb(o + 2048, [128, D], BF16)]; o += 4096
        lng = sb(o, [128, D], F32); o += 4096
        lnb = sb(o, [128, D], F32); o += 4096
        WsT = sb(o, [128, 8, 128], BF16); o += 2048
        sqjC = sb(o, [128, D], F32); o += 4096
        for q in range(4):
            P.dma("pool", Wuv[:, :, q * 512:(q + 1) * 512], wsrc(w_in, q * 512, (q + 1) * 512), writes=[("Wuv", q)])
        P.dma("sp", lng[:], ln_g.partition_broadcast(128), writes=["lng"])
        P.dma("sp", lnb[:], ln_b.partition_broadcast(128), writes=["lnb"])
        P.dma("pool", WsT[:], ws_t, writes=["WsT"])
        P.op("pool", lambda e: e.memset(WsT[64:128, :, 0:64], 0.0), reads=["WsT"], writes=["WsT"])

        def C0_(b):
            for ct in range(4):
                bk = ct
                for kc in range(8):
                    mm(ps[bk][:], hT[:, kc, b * 128:(b + 1) * 128], Wuv[:, kc, ct * 512:(ct + 1) * 512], kc == 0, kc == 7,
                       [("hT", b), ("Wuv", ct)], [("ps", bk)])
                if ct < 2:
                    P.op("act", lambda e, b=b, ct=ct, bk=bk: e.activation(
                        out=ub[b % 2][:, ct * 512:(ct + 1) * 512], in_=ps[bk][:], func=AF.Gelu_apprx_tanh),
                        reads=[("ps", bk)], writes=[("ub", b % 2, ct)])
                else:
                    P.op("act", lambda e, b=b, ct=ct, bk=bk: e.activation(
                        out=v32[b % 2][:, (ct - 2) * 512:(ct - 1) * 512], in_=ps[bk][:], func=AF.Gelu_apprx_tanh,
                        accum_out=lnst[:, b, ct - 2:ct - 1]),
                        reads=[("ps", bk), "lnst"], writes=[("v32", b % 2, ct - 2), ("lnst", b, ct - 2)])

        def C1_(b):
            s = b % 2
            L = lambda i: lnst[:, b, i:i + 1]
            lt = ("lnstb", b)
            P.op("act", lambda e: e.activation(out=sqjC[:], in_=v32[s][:], func=AF.Square, accum_out=L(2)),
                 reads=[("v32", s, 0), ("v32", s, 1), "lnst"], writes=["sqjC", lt])
            P.op("dve", lambda e: e.tensor_tensor(out=L(0), in0=L(0), in1=L(1), op=ALU.add),
                 reads=[("lnst", b, 0), ("lnst", b, 1)], writes=[("lnst", b, 0)])
            P.op("dve", lambda e: e.tensor_scalar(out=L(0), in0=L(0), scalar1=1.0 / D, scalar2=0.0, op0=ALU.mult, op1=ALU.add),
                 reads=[("lnst", b, 0)], writes=[("lnst", b, 0)])
            P.op("dve", lambda e: e.tensor_tensor(out=L(1), in0=L(0), in1=L(0), op=ALU.mult),
                 reads=[("lnst", b, 0)], writes=[("lnst", b, 1)])
            P.op("dve", lambda e: e.scalar_tensor_tensor(out=L(3), in0=L(2), scalar=1.0 / D, in1=L(1),
                                                          op0=ALU.mult, op1=ALU.subtract),
                 reads=[lt, ("lnst", b, 1)], writes=[lt])
            P.op("dve", lambda e: e.tensor_scalar(out=L(3), in0=L(3), scalar1=EPS, scalar2=0.0, op0=ALU.add, op1=ALU.add),
                 reads=[lt], writes=[lt])
            P.op("pool", lambda e: e.tensor_tensor(out=L(4), in0=L(3), in1=nh[:, 0:1], op=ALU.pow),
                 reads=[lt, "nh"], writes=[lt])
            P.op("dve", lambda e: e.scalar_tensor_tensor(out=L(5), in0=L(0), scalar=-1.0, in1=L(4),
                                                          op0=ALU.mult, op1=ALU.mult),
                 reads=[lt, ("lnst", b, 0)], writes=[lt])
            P.op("dve", lambda e: e.tensor_scalar(out=v32[s][:], in0=v32[s][:], scalar1=L(4), scalar2=L(5),
                                                   op0=ALU.mult, op1=ALU.add),
                 reads=[lt, ("v32", s, 0), ("v32", s, 1)], writes=[("v32", s, 0), ("v32", s, 1)])
            P.op("dve", lambda e: e.tensor_tensor(out=v32[s][:], in0=v32[s][:], in1=lng[:], op=ALU.mult),
                 reads=[("v32", s, 0), ("v32", s, 1), "lng"], writes=[("v32", s, 0), ("v32", s, 1)])
            P.op("dve", lambda e: e.tensor_tensor(out=vln[s][:], in0=v32[s][:], in1=lnb[:], op=ALU.add),
                 reads=[("v32", s, 0), ("v32", s, 1), "lnb"], writes=[("vln", s)])
            for g in range(8):
                bk = 4 + g // 4
                mm(ps[bk][:, (g % 4) * 128:(g % 4 + 1) * 128], WsT[:, g, :], vln[s][:, g * 128:(g + 1) * 128], True, True,
                   ["WsT", ("vln", s)], [("ps", bk)])

        def C2_(b):
            s = b % 2
            for g in range(8):
                bk = 4 + g // 4
                P.op("dve", lambda e, g=g, bk=bk: e.scalar_tensor_tensor(
                    out=ab[s][:, g * 128:(g + 1) * 128], in0=ps[bk][:, (g % 4) * 128:(g % 4 + 1) * 128],
                    scalar=bs[:, g:g + 1], in1=ub[s][:, g * 128:(g + 1) * 128], op0=ALU.add, op1=ALU.mult),
                    reads=[("ps", bk), "bs", ("ub", s, g // 4)], writes=[("ab", s)])
            tr8(ab[s], ("ab", s), 6, aT[:, :, b * 128:(b + 1) * 128], ("aT", b))

        for t in range(NB + 2):
            if 0 <= t - 2 < NB:
                C2_(t - 2)
            if 0 <= t - 1 < NB:
                C1_(t - 1)
            if t < NB:
                C0_(t)

        if upto == 'C':
            raise _Stop()
        P.barrier()
        dump("hT", hT[:])
        dump("bT", bT[:])
        dump("aT", aT[:])
        o = SCR
        wsm = [[sb(o + (s * 4 + i) * 2048, [128, 8, 128], BF16) for i in range(4)] for s in range(2)]; o += 16384
        gsb = [[sb(o + (s * 2 + i) * 2048, [128, 512], F32) for i in range(2)] for s in range(2)]; o += 8192
        tsb = [[sb(o + (s * 2 + i) * 2048, [128, 512], F32) for i in range(2)] for s in range(2)]; o += 8192
        Wout = sb(o, [128, 8, D], BF16); o += 16384
        oE = o
        P.dma("pool", Wout[:, :, 0:512], wsrc(w_out, 0, 512), writes=[("Wout", 0)])
        P.dma("pool", Wout[:, :, 512:1024], wsrc(w_out, 512, 1024), writes=[("Wout", 1)])
        step = 0

        def loadD(m):
            srcs = (wsrc(w_in, O_G + m * 128, O_G + (m + 1) * 128),
                    wsrc(w_in, O_G + D + m * 128, O_G + D + (m + 1) * 128),
                    wsrc(w_a, m * 128, (m + 1) * 128), wsrc(w_b, m * 128, (m + 1) * 128))
            for i in range(4):
                P.dma("pool", wsm[m % 2][i][:], srcs[i], writes=[("wsm", m % 2, i)])

        loadD(0)
        for m in range(8):
            s = m % 2
            if m + 1 < 8:
                loadD(m + 1)
            for j in range(NT):
                pb = 4 * (step % 2)
                sg = step % 2
                step += 1
                acts = (hT, hT, aT, bT)
                for i in range(4):
                    for kc in range(8):
                        if i < 2:
                            rd = [("hT", 4 * j + u) for u in range(4)]
                        elif i == 2:
                            rd = [("aT", 4 * j + u) for u in range(4)]
                        else:
                            rd = [("bT", kc, j, 0), ("bT", kc, j, 1)]
                        mm(ps[pb + i][:], wsm[s][i][:, kc, :], acts[i][:, kc, j * 512:(j + 1) * 512], kc == 0, kc == 7,
                           rd + [("wsm", s, i)], [("ps", pb + i)])
                for i in range(2):
                    P.op("act", lambda e, i=i, pb=pb, sg=sg: e.activation(out=gsb[sg][i][:], in_=ps[pb + i][:], func=AF.Sigmoid),
                         reads=[("ps", pb + i)], writes=[("gsb", sg, i)])
                for i in range(2):
                    P.op("dve", lambda e, i=i, pb=pb, sg=sg: e.tensor_tensor(out=tsb[sg][i][:], in0=ps[pb + 2 + i][:],
                                                                           in1=gsb[sg][i][:], op=ALU.mult),
                         reads=[("ps", pb + 2 + i), ("gsb", sg, i)], writes=[("tsb", sg, i)])
                P.op("pool", lambda e, sg=sg, m=m, j=j: e.tensor_tensor(out=g0[:, m, j * 512:(j + 1) * 512], in0=tsb[sg][0][:],
                                                                       in1=tsb[sg][1][:], op=ALU.add),
                     reads=[("tsb", sg, 0), ("tsb", sg, 1)], writes=[("mg", m, j)])

        if upto == 'D':
            raise _Stop()
        P.barrier()
        dump("mg", g0[:])
        o = oE
        xr = [sb(o, [128, D], F32), sb(o + 4096, [128, D], F32)]; o += 8192
        hbE = [sb(o, [128, D], BF16), sb(o + 2048, [128, D], BF16)]; o += 4096
        gvE = sb(o, [128, D], F32); o += 4096
        sqjE = sb(o, [128, D], F32); o += 4096
        assert o <= SCR_END
        P.dma("sp", gvE[:], norm_ffn_g.partition_broadcast(128), writes=["gv"])
        trE = {}
        for t in range(NB + 1):
            if t >= 1:
                trE[t - 1]()
            if t >= NB:
                continue
            b = t
            j = b // 4
            P.dma("sp", xr[b % 2][:], x[b * 128:(b + 1) * 128, :], writes=[("xr", b % 2)])
            for ch in range(2):
                bk = 2 * (b % 2) + ch
                for m in range(8):
                    mm(ps[bk][:], g0[:, m, b * 128:(b + 1) * 128], Wout[:, m, ch * 512:(ch + 1) * 512], m == 0, m == 7,
                       [("mg", m, j), ("Wout", ch)], [("ps", bk)])
                P.op("dve", lambda e, b=b, ch=ch, bk=bk: e.tensor_tensor(
                    out=x1[:, b, ch * 512:(ch + 1) * 512], in0=ps[bk][:], in1=xr[b % 2][:, ch * 512:(ch + 1) * 512], op=ALU.add),
                    reads=[("ps", bk), ("xr", b % 2)], writes=[("x1", b, ch)])
            trE[b] = norm_block(1, b, x1[:, b, :], gvE[:], hbE[b % 2][:], sqjE[:], 4 + 2 * (b % 2),
                                h2T[:, :, b * 128:(b + 1) * 128], ("x1", b, 0), ("hbE", b % 2), ("h2T", b),
                                extra_tok=[("x1", b, 1)], defer=True)

        if upto == 'E':
            raise _Stop()
        P.barrier()
        dump("x1", x1[:])
        NG = 11
        actT = sb(SCR, [128, NG, S], BF16)
        Wd = sb(SCR + 45056, [128, NG, D], BF16)
        o = SCR + 67584
        Ag = [sb(o, [128, 512], F32), sb(o + 2048, [128, 512], F32)]; o += 4096
        Al = [sb(o, [128, 512], F32), sb(o + 2048, [128, 512], F32)]; o += 4096
        assert o <= SCR_END
        o = G0
        NW = 3
        wup = [[sb(o + (s * 2 + i) * 2048, [128, 8, 128], BF16) for i in range(2)] for s in range(NW)]; o += NW * 4096
        NR = 3
        Rg = [sb(o + s * 2080, [128, 514], F32) for s in range(NR)]; o += NR * 2080
        Rl = [sb(o + s * 2080, [128, 514], F32) for s in range(NR)]; o += NR * 2080
        Gg = [sb(o, [128, 512], F32), sb(o + 2048, [128, 512], F32)]; o += 4096
        assert o <= G0 + 32 * KB
        rstep = 0
        _KF = int(os.environ.get('KDBG_F', '9'))

        def loadF(fc):
            P.dma("pool", wup[fc % NW][0][:], wsrc(w_up, fc * 128, (fc + 1) * 128), writes=[("wup", fc % NW, 0)])
            P.dma("pool", wup[fc % NW][1][:], wsrc(w_up, DFF + fc * 128, DFF + (fc + 1) * 128), writes=[("wup", fc % NW, 1)])

        loadF(0)
        loadF(1)
        for grp in range(2):
            for q in range(2):
                P.dma("pool", Wd[:, :, q * 512:(q + 1) * 512],
                      w_down[grp * NG * 128:(grp + 1) * NG * 128, q * 512:(q + 1) * 512].rearrange("(c p) n -> p c n", p=128),
                      writes=[("Wd", q)])
            for fl in range(NG):
                fc = grp * NG + fl
                ws_ = fc % NW
                if fc + 2 < NFC:
                    loadF(fc + 2)
                for j in range(NT):
                    rs = rstep % NR
                    rp = (rstep - 1) % NR
                    sa = rstep % 2
                    pb = 2 * (rstep % 2)
                    rstep += 1
                    for i in range(2):
                        for kc in range(8):
                            mm(ps[pb + i][:], wup[ws_][i][:, kc, :], h2T[:, kc, j * 512:(j + 1) * 512], kc == 0, kc == 7,
                               [("h2T", 4 * j + u) for u in range(4)] + [("wup", ws_, i)], [("ps", pb + i)])
                    for i, (R, A, ci) in enumerate(((Rg, Ag, fc), (Rl, Al, NFC + fc))):
                        rt = ("R", i, rs)
                        P.op("act", lambda e, R=R, i=i, pb=pb, rs=rs: e.copy(out=R[rs][:, 2:514], in_=ps[pb + i][:]),
                             reads=[("ps", pb + i)], writes=[rt])
                        rn = (rs + 1) % NR
                        if j == 0:
                            P.op("dve", lambda e, R=R, rs=rs: e.memset(R[rs][:, 0:2], 0.0), writes=[("Rh", i, rs)])
                        if j < NT - 1:
                            P.op("act", lambda e, R=R, i=i, pb=pb, rn=rn: e.copy(out=R[rn][:, 0:2], in_=ps[pb + i][:, 510:512]),
                                 reads=[("ps", pb + i)], writes=[("Rh", i, rn)])
                        at = ("A", i, sa)
                        if _KF < 2:
                            continue
                        if i == 0:
                            P.op("act", lambda e, A=A, pb=pb, sa=sa, ci=ci: e.activation(
                                out=A[sa][:], in_=ps[pb][:], func=AF.Identity, scale=cw[:, ci, 2:3], bias=cb[:, ci:ci + 1]),
                                reads=[("ps", pb), "cw", "cb"], writes=[at])
                        else:
                            P.op("dve", lambda e, A=A, R=R, rs=rs, sa=sa, ci=ci: e.tensor_scalar(
                                out=A[sa][:], in0=R[rs][:, 2:514], scalar1=cw[:, ci, 2:3], scalar2=cb[:, ci:ci + 1],
                                op0=ALU.mult, op1=ALU.add), reads=[rt, "cw", "cb"], writes=[at])
                        _f2 = os.environ.get('KDBG_F2', '')
                        if _f2 == 'a':
                            continue
                        w1s, w0s = (slice(1, 513), slice(0, 512)) if _f2 != 'al' else (slice(2, 514), slice(2, 514))
                        P.op("dve", lambda e, A=A, R=R, rs=rs, sa=sa, ci=ci, w1s=w1s: e.scalar_tensor_tensor(
                            out=A[sa][:], in0=R[rs][:, w1s], scalar=cw[:, ci, 1:2], in1=A[sa][:],
                            op0=ALU.mult, op1=ALU.add), reads=[rt, ("Rh", i, rs), at, "cw"], writes=[at])
                        P.op("dve", lambda e, A=A, R=R, rs=rs, sa=sa, ci=ci, w0s=w0s: e.scalar_tensor_tensor(
                            out=A[sa][:], in0=R[rs][:, w0s], scalar=cw[:, ci, 0:1], in1=A[sa][:],
                            op0=ALU.mult, op1=ALU.add), reads=[rt, ("Rh", i, rs), at, "cw"], writes=[at])
                    if _KF < 3:
                        continue
                    P.op("act", lambda e, sa=sa: e.activation(out=Gg[sa][:], in_=Ag[sa][:], func=AF.Gelu_apprx_tanh),
                         reads=[("A", 0, sa)], writes=[("Gg", sa)])
                    P.op("pool", lambda e, sa=sa, fl=fl, j=j: e.tensor_tensor(
                        out=actT[:, fl, j * 512:(j + 1) * 512], in0=Gg[sa][:], in1=Al[sa][:], op=ALU.mult),
                        reads=[("Gg", sa), ("A", 1, sa)], writes=[("actT", fl, j)])
            for b in range(NB if _KF >= 4 else 0):
                j = b // 4
                for ch in range(2):
                    bk = 4 + 2 * (b % 2) + ch
                    for fl in range(NG):
                        mm(ps[bk][:], actT[:, fl, b * 128:(b + 1) * 128], Wd[:, fl, ch * 512:(ch + 1) * 512],
                           fl == 0, fl == NG - 1, [("actT", fl, j), ("Wd", ch)], [("ps", bk)])
                    P.op("dve", lambda e, b=b, ch=ch, bk=bk: e.tensor_tensor(
                        out=x1[:, b, ch * 512:(ch + 1) * 512], in0=ps[bk][:], in1=x1[:, b, ch * 512:(ch + 1) * 512], op=ALU.add),
                        reads=[("ps", bk), ("x1", b, ch)], writes=[("x1", b, ch)])

        if upto == 'F':
            raise _Stop()
        P.barrier()
        dump("x2", x1[:])
        o = SCR
        Wg = sb(o, [128, 8, D], BF16); o += 16384
        Wp = sb(o, [128, 2, D], BF16); o += 4096
        pst = [sb(o, [128, PLE], F32), sb(o + 1024, [128, PLE], F32)]; o += 2048
        pb16 = [sb(o, [128, PLE], BF16), sb(o + 512, [128, PLE], BF16)]; o += 1024
        pTb = [sb(o, [128, 2, 128], BF16), sb(o + 512, [128, 2, 128], BF16)]; o += 1024
        hbG = [sb(o, [128, D], BF16), sb(o + 2048, [128, D], BF16)]; o += 4096
        h3Tb = [sb(o, [128, 8, 128], BF16), sb(o + 2048, [128, 8, 128], BF16)]; o += 4096
        sgt = [[sb(o + (s * 2 + i) * 2048, [128, 512], F32) for i in range(2)] for s in range(2)]; o += 8192
        tt = [sb(o, [128, D], F32), sb(o + 4096, [128, D], F32)]; o += 8192
        ot = [sb(o, [128, D], F32), sb(o + 4096, [128, D], F32)]; o += 8192
        gv3 = sb(o, [128, D], F32); o += 4096
        gvf = sb(o, [128, D], F32); o += 4096
        sqjG = sb(o, [128, D], F32); o += 4096
        assert o <= SCR_END
        P.dma("pool", Wg[:, :, 0:512], wsrc(w_pg, 0, 512), writes=[("Wg", 0)])
        P.dma("pool", Wg[:, :, 512:1024], wsrc(w_pg, 512, 1024), writes=[("Wg", 1)])
        P.dma("pool", Wp[:], wsrc(w_ple, 0, D), writes=["Wp"])
        P.dma("sp", gv3[:], norm_ple_g.partition_broadcast(128), writes=["gv"])
        P.dma("sp", gvf[:], norm_final_g.partition_broadcast(128), writes=["gvf"])

        trG = {}

        def G0_(b):
            s = b % 2
            P.dma("sp", pst[s][:], pin[b * 128:(b + 1) * 128, :], writes=[("pst", s)])
            P.op("act", lambda e: e.copy(out=pb16[s][:], in_=pst[s][:]), reads=[("pst", s)], writes=[("pb16", s)])
            trG[b] = norm_block(2, b, x1[:, b, :], gv3[:], hbG[s][:], sqjG[:], 0, h3Tb[s][:],
                                ("x1", b, 0), ("hbG", s), ("h3Tb", s), extra_tok=[("x1", b, 1)], defer=True)

        def G0b_(b):
            s = b % 2
            tr8(pb16[s], ("pb16", s), 4, pTb[s][:], ("pTb", s), nch=2)
            trG[b]()

        def G1_(b):
            s = b % 2
            for ch in range(2):
                bk = 2 + ch
                for kc in range(8):
                    mm(ps[bk][:], h3Tb[s][:, kc, :], Wg[:, kc, ch * 512:(ch + 1) * 512], kc == 0, kc == 7,
                       [("h3Tb", s), ("Wg", ch)], [("ps", bk)])
                P.op("act", lambda e, ch=ch, bk=bk: e.activation(out=sgt[s][ch][:], in_=ps[bk][:], func=AF.Sigmoid),
                     reads=[("ps", bk)], writes=[("sgt", s, ch)])
                bk2 = 6 + ch
                for kc in range(2):
                    mm(ps[bk2][:], pTb[s][:, kc, :], Wp[:, kc, ch * 512:(ch + 1) * 512], kc == 0, kc == 1,
                       [("pTb", s), "Wp"], [("ps", bk2)])
                P.op("dve", lambda e, ch=ch, bk2=bk2: e.tensor_tensor(
                    out=tt[s][:, ch * 512:(ch + 1) * 512], in0=ps[bk2][:], in1=sgt[s][ch][:], op=ALU.mult),
                    reads=[("ps", bk2), ("sgt", s, ch)], writes=[("tt", s, ch)])
                P.op("dve", lambda e, ch=ch: e.tensor_tensor(
                    out=tt[s][:, ch * 512:(ch + 1) * 512], in0=tt[s][:, ch * 512:(ch + 1) * 512],
                    in1=x1[:, b, ch * 512:(ch + 1) * 512], op=ALU.add),
                    reads=[("tt", s, ch), ("x1", b, ch)], writes=[("tt", s, ch)])
            ssc = stat[:, 3, 0, b:b + 1]
            msc = stat[:, 3, 1, b:b + 1]
            rsc = stat[:, 3, 2, b:b + 1]
            st = ("stat", 3, b)
            P.op("act", lambda e: e.activation(out=sqjG[:], in_=tt[s][:], func=AF.Square, accum_out=ssc),
                 reads=[("tt", s, 0), ("tt", s, 1), "stat"], writes=["sqj", st])
            P.op("dve", lambda e: e.tensor_scalar(out=msc, in0=ssc, scalar1=1.0 / D, scalar2=EPS, op0=ALU.mult, op1=ALU.add),
                 reads=[st], writes=[st])
            P.op("pool", lambda e: e.tensor_tensor(out=rsc, in0=msc, in1=nh[:, 0:1], op=ALU.pow), reads=[st, "nh"], writes=[st])
            P.op("dve", lambda e: e.scalar_tensor_tensor(out=ot[s][:], in0=tt[s][:], scalar=rsc, in1=gvf[:],
                                                          op0=ALU.mult, op1=ALU.mult),
                 reads=[("tt", s, 0), ("tt", s, 1), st, "gvf"], writes=[("ot", s)])
            P.dma("sp", out[b * 128:(b + 1) * 128, :], ot[s][:], reads=[("ot", s)], writes=[("out", b)])

        for t in range(NB + 2):
            if 0 <= t - 2 < NB:
                G1_(t - 2)
            if 0 <= t - 1 < NB:
                G0b_(t - 1)
            if t < NB:
                G0_(t)
        P.op("sp", lambda e: e.nop(), reads=[("out", b) for b in range(NB)])
        P.barrier()

    except _Stop:
        P.barrier()

    emit(nc, P)
    es.close()
    return nc


_CACHE = {}


def _prep_shared(inp):
    f = np.float32
    d = {}
    d["norm_mix_g"] = np.ascontiguousarray(inp["norm_mix_g"][0], f)
    d["w_in"] = np.ascontiguousarray(inp["w_in"][0], f)
    d["b_f"] = np.ascontiguousarray(inp["b_f"][0], f)
    d["gmlp_ln_g"] = np.ascontiguousarray(inp["gmlp_ln_g"][0], f)
    d["gmlp_ln_b"] = np.ascontiguousarray(inp["gmlp_ln_b"][0], f)
    d["ws_t"] = np.ascontiguousarray(np.transpose(inp["gmlp_w_s"][0], (2, 0, 1)), f)
    d["bs_t"] = np.ascontiguousarray(inp["gmlp_b_s"][0].T, f)
    d["w_branch_a"] = np.ascontiguousarray(inp["w_branch_a"][0], f)
    d["w_branch_b"] = np.ascontiguousarray(inp["w_branch_b"][0], f)
    d["w_out"] = np.ascontiguousarray(inp["w_out"][0], f)
    d["norm_ffn_g"] = np.ascontiguousarray(inp["norm_ffn_g"][0], f)
    d["w_up"] = np.ascontiguousarray(inp["w_up"][0], f)
    cwv = np.asarray(inp["conv_w"][0], f)
    d["cw_t"] = np.ascontiguousarray(cwv.T.reshape(44, 128, 3).transpose(1, 0, 2), f)
    d["cb_t"] = np.ascontiguousarray(np.asarray(inp["conv_b"][0], f).reshape(44, 128).T, f)
    d["w_down"] = np.ascontiguousarray(inp["w_down"][0], f)
    d["norm_ple_g"] = np.ascontiguousarray(inp["norm_ple_g"][0], f)
    d["w_ple"] = np.ascontiguousarray(inp["w_ple"][0], f)
    d["w_ple_gate"] = np.ascontiguousarray(inp["w_ple_gate"][0], f)
    d["norm_final_g"] = np.ascontiguousarray(inp["norm_final_g"], f)
    d["c_ident"] = np.eye(128, dtype=f)
    r = np.arange(128)
    d["c_tri"] = (r[:, None] <= r[None, :]).astype(f)
    d["c_sel"] = np.ascontiguousarray(np.broadcast_to((r[:, None] <= 63), (128, 128))).astype(f)
    return d


def kernel(**inputs):
    inp = {k: np.asarray(v) for k, v in inputs.items()}
    n = 8
    if "nc" not in _CACHE:
        _CACHE["nc"] = build_nc()
    nc = _CACHE["nc"]
    shared = _prep_shared(inp)
    x = np.asarray(inp["x"], np.float32)
    p = np.asarray(inp["p"], np.float32)[0]
    in_maps = []
    for i in range(n):
        m = dict(shared)
        m["x"] = np.ascontiguousarray(x[i])
        m["p"] = np.ascontiguousarray(p[i])
        in_maps.append(m)
    res = run_bass_kernel_spmd(nc, in_maps, core_ids=list(range(n)))
    return np.stack([np.asarray(r["out"], np.float32) for r in res.results], axis=0)
```

```python
import contextlib
import numpy as np
import concourse.bass as bass
import concourse.mybir as mybir
from concourse.bass_utils import run_bass_kernel_spmd

F32 = mybir.dt.float32
BF16 = mybir.dt.bfloat16
AF = mybir.ActivationFunctionType
ALU = mybir.AluOpType

ENGS = ("pe", "act", "dve", "pool", "sp")

S = 2048
D = 1024
NB = 16
NT = 4
DFF = 2816
NFC = 22
IN_COLS = 7184
O_Q = 2048
O_K = 3072
O_V = 4096
O_F = 5120
O_G = 5136
PLE = 256
EPS = 1e-6
KB = 1024
BASE = 16512
import os
_STG = int(os.environ.get('KDBG_STG', '6'))
_NBA = int(os.environ.get('KDBG_NBA', '16'))


class _Op:
    __slots__ = ("fn", "deps", "dma", "flag", "semval", "dsem", "dcount", "waits", "prewait")

    def __init__(self, fn, deps, dma):
        self.fn = fn
        self.deps = deps
        self.dma = dma
        self.flag = False
        self.semval = None
        self.dsem = None
        self.dcount = None
        self.waits = []
        self.prewait = None


class Prog:
    def __init__(self, nc, n_dma_sems=32):
        self.nc = nc
        self.streams = {e: [] for e in ENGS}
        self.last_writer = {}
        self.readers = {}
        self.n_dma_sems = n_dma_sems

    def op(self, eng, fn, reads=(), writes=(), dma=False, extra=None):
        st = self.streams[eng]
        me = (eng, len(st))
        deps = {}
        for t in reads:
            w = self.last_writer.get(t)
            if w is not None:
                deps[w] = True
        for t in writes:
            w = self.last_writer.get(t)
            if w is not None and w not in deps:
                deps[w] = False
            for r in self.readers.get(t, ()):
                if r not in deps:
                    deps[r] = False
        if extra:
            for d in extra:
                deps[d] = True
        deps.pop(me, None)
        for t in writes:
            self.last_writer[t] = me
            self.readers[t] = []
        for t in reads:
            if self.last_writer.get(t) != me:
                self.readers.setdefault(t, []).append(me)
        st.append(_Op(fn, deps, dma))
        return me

    def dma(self, eng, out, in_, reads=(), writes=()):
        return self.op(eng, lambda e: e.dma_start(out=out, in_=in_), reads, writes, dma=True)

    def barrier(self):
        lasts = []
        for e in ENGS:
            st = self.streams[e]
            for i in range(len(st) - 1, -1, -1):
                if not st[i].dma:
                    lasts.append((e, i))
                    break
            n = 0
            for i in range(len(st) - 1, -1, -1):
                if st[i].dma:
                    lasts.append((e, i))
                    n += 1
                    if n >= 16:
                        break
        for e in ENGS:
            self.op(e, lambda g: g.nop(), extra=[d for d in lasts])

    def resolve(self):
        streams = self.streams
        dma_engs = [e for e in ENGS if any(o.dma for o in streams[e])]
        pools = {}
        if dma_engs:
            per = max(2, self.n_dma_sems // len(dma_engs))
            for e in dma_engs:
                pools[e] = min(per, 16)
        self.dma_pool_sizes = pools
        for e in dma_engs:
            k = 0
            counts = [0] * pools[e]
            lastop = [None] * pools[e]
            for i, o in enumerate(streams[e]):
                if not o.dma:
                    continue
                s = k % pools[e]
                k += 1
                if lastop[s] is not None:
                    o.prewait = ((e, s), counts[s])
                counts[s] += 16
                o.dsem = (e, s)
                o.dcount = counts[s]
                lastop[s] = i
        for e in ENGS:
            seen = {f: -1 for f in ENGS}
            seen_d = {}
            for i, o in enumerate(streams[e]):
                if o.prewait is not None:
                    key, cnt = o.prewait
                    if seen_d.get(key, 0) >= cnt:
                        o.prewait = None
                    else:
                        seen_d[key] = cnt
                for (f, n), is_raw in sorted(o.deps.items(), key=lambda kv: (kv[0][0], -kv[0][1])):
                    p = streams[f][n]
                    if p.dma:
                        if seen_d.get(p.dsem, 0) >= p.dcount:
                            continue
                        seen_d[p.dsem] = p.dcount
                        o.waits.append(("d", p.dsem, p.dcount))
                        continue
                    if f == e and not o.dma:
                        if not (is_raw and e != "pe"):
                            continue
                    if n <= seen[f]:
                        continue
                    seen[f] = n
                    p.flag = True
                    o.waits.append(("c", f, n))
        for e in ENGS:
            c = 0
            for o in streams[e]:
                if o.flag:
                    c += 1
                    o.semval = c


def emit(nc, prog):
    prog.resolve()
    with contextlib.ExitStack() as es:
        csem = {e: es.enter_context(nc.semaphore("c_" + e)) for e in ENGS}
        dsem = {}
        for e, n in prog.dma_pool_sizes.items():
            for s in range(n):
                dsem[(e, s)] = es.enter_context(nc.semaphore("d_%s_%d" % (e, s)))
        block = es.enter_context(nc.Block())

        def run(ename, eng):
            for o in prog.streams[ename]:
                if o.prewait is not None:
                    key, cnt = o.prewait
                    eng.wait_ge(dsem[key], cnt)
                for w in o.waits:
                    if w[0] == "d":
                        eng.wait_ge(dsem[w[1]], w[2])
                    else:
                        eng.wait_ge(csem[w[1]], prog.streams[w[1]][w[2]].semval)
                ins = o.fn(eng)
                if o.dma:
                    ins.then_inc(dsem[o.dsem], 16)
                elif o.flag:
                    ins.then_inc(csem[ename], 1)

        @block.tensor
        def _(eng):
            run("pe", eng)

        @block.scalar
        def _(eng):
            run("act", eng)

        @block.vector
        def _(eng):
            run("dve", eng)

        @block.gpsimd
        def _(eng):
            run("pool", eng)

        @block.sync
        def _(eng):
            run("sp", eng)


class _Stop(Exception):
    pass


def build_nc(debug=False, upto=None):
    nc = bass.Bass("TRN2", target_bir_lowering=False)

    def din(name, shape):
        return nc.dram_tensor(name, list(shape), F32, kind="ExternalInput").ap()

    x = din("x", [S, D])
    pin = din("p", [S, PLE])
    norm_mix_g = din("norm_mix_g", [D])
    w_in = din("w_in", [D, IN_COLS])
    b_f = din("b_f", [16])
    ln_g = din("gmlp_ln_g", [D])
    ln_b = din("gmlp_ln_b", [D])
    ws_t = din("ws_t", [128, 8, 128])
    bs_t = din("bs_t", [128, 8])
    w_a = din("w_branch_a", [D, D])
    w_b = din("w_branch_b", [D, D])
    w_out = din("w_out", [D, D])
    norm_ffn_g = din("norm_ffn_g", [D])
    w_up = din("w_up", [D, 2 * DFF])
    cw_t = din("cw_t", [128, 44, 3])
    cb_t = din("cb_t", [128, 44])
    w_down = din("w_down", [DFF, D])
    norm_ple_g = din("norm_ple_g", [D])
    w_ple = din("w_ple", [PLE, D])
    w_pg = din("w_ple_gate", [D, D])
    norm_final_g = din("norm_final_g", [D])
    c_ident = din("c_ident", [128, 128])
    c_tri = din("c_tri", [128, 128])
    c_sel = din("c_sel", [128, 128])
    out = nc.dram_tensor("out", [S, D], F32, kind="ExternalOutput").ap()
    dbg = {}
    if debug:
        for nm in ("hT", "bT", "aT", "mg"):
            dbg[nm] = nc.dram_tensor("dbg_" + nm, [128, 8, S], BF16, kind="ExternalOutput").ap()
        for nm in ("x1", "x2"):
            dbg[nm] = nc.dram_tensor("dbg_" + nm, [128, NB, D], F32, kind="ExternalOutput").ap()

    def dump(nm, src):
        if debug:
            P.dma("sp", dbg[nm], src)
            P.barrier()

    cnt = [0]

    def sb(off, shape, dt):
        cnt[0] += 1
        return nc.alloc_sbuf_tensor_at("t%d" % cnt[0], list(shape), dt, offset=off)

    C0 = BASE
    ident = sb(C0 + 0, [128, 128], BF16)
    onesf = sb(C0 + 256, [128, 128], F32)
    trif = sb(C0 + 768, [128, 128], F32)
    self_ = sb(C0 + 1280, [128, 128], F32)
    maskb = sb(C0 + 1792, [128, 128], BF16)
    cw = sb(C0 + 2048, [128, 44, 3], F32)
    cb = sb(C0 + 2592, [128, 44], F32)
    bs = sb(C0 + 2784, [128, 8], F32)
    bfb = sb(C0 + 2816, [128, 16], F32)
    nh = sb(C0 + 2880, [128, 16], F32)
    stat = sb(C0 + 2944, [128, 4, 3, 16], F32)
    lnst = sb(C0 + 3712, [128, 16, 6], F32)
    G0 = C0 + 4 * KB
    HT = C0 + 36 * KB
    BT = C0 + 68 * KB
    AT = C0 + 100 * KB
    SCR = C0 + 132 * KB
    SCR_END = 229344

    g0 = sb(G0, [128, 8, S], BF16)
    hT = sb(HT, [128, 8, S], BF16)
    bT = sb(BT, [128, 8, S], BF16)
    aT = sb(AT, [128, 8, S], BF16)
    x1 = sb(HT, [128, NB, D], F32)
    h2T = sb(AT, [128, 8, S], BF16)

    es = contextlib.ExitStack()
    ps = [es.enter_context(nc.psum_tensor("ps%d" % i, [128, 512], F32)) for i in range(8)]

    def psb(i):
        return ps[i][:].bitcast(BF16).rearrange("p (c n) -> p c n", c=8)

    P = Prog(nc)

    def mm(o, lhsT, rhs, start, stop, reads, writes, skip=False):
        if skip:
            P.op("pe", lambda e: e.matmul(o, lhsT=lhsT, rhs=rhs, start=start, stop=stop,
                                           skip_group_check=True), reads, writes)
        else:
            P.op("pe", lambda e: e.matmul(o, lhsT=lhsT, rhs=rhs, start=start, stop=stop), reads, writes)

    def wsrc(w, c0, c1):
        return w[:, c0:c1].rearrange("(c p) n -> p c n", p=128)

    try:
        P.dma("pool", ident[:], c_ident, writes=["ident"])
        P.dma("sp", trif[:], c_tri, writes=["trif"])
        P.dma("sp", self_[:], c_sel, writes=["self"])
        P.dma("pool", maskb[:], c_tri, writes=["maskb"])
        P.dma("sp", cw[:], cw_t, writes=["cw"])
        P.dma("sp", cb[:], cb_t, writes=["cb"])
        P.dma("sp", bs[:], bs_t, writes=["bs"])
        P.dma("sp", bfb[:], b_f.partition_broadcast(128), writes=["bfb"])
        P.op("pool", lambda e: e.memset(onesf[:], 1.0), writes=["onesf"])
        P.op("pool", lambda e: e.memset(nh[:], -0.5), writes=["nh"])
        P.op("pool", lambda e: e.memset(stat[:], 0.0), writes=["stat"])
        P.op("pool", lambda e: e.memset(lnst[:], 0.0), writes=["lnst"])

        if upto == 'c':
            raise _Stop()
        def tr8(src, src_tok, bank, dstT, dst_tok, nch=8):
            for c in range(nch):
                bk = bank + c // 4
                P.op("pe", lambda e, c=c, bk=bk: e.matmul(ps[bk][:, (c % 4) * 128:(c % 4 + 1) * 128],
                                                          lhsT=src[:, c * 128:(c + 1) * 128], rhs=ident[:],
                                                          start=True, stop=True),
                     reads=[src_tok, "ident"], writes=[("ps", bk)])
            for hf in range((nch + 3) // 4):
                n = min(4, nch - 4 * hf)
                pvv = ps[bank + hf][:, 0:n * 128].rearrange("p (c n) -> p c n", c=n)
                P.op("act", lambda e, hf=hf, n=n, pvv=pvv: e.copy(out=dstT[:, 4 * hf:4 * hf + n, :], in_=pvv),
                     reads=[("ps", bank + hf)], writes=[dst_tok])

        def norm_block(ni, b, src, gv, hb, sqj, bank, dstT, src_tok, hb_tok, dst_tok, extra_tok=(), defer=False):
            ssc = stat[:, ni, 0, b:b + 1]
            msc = stat[:, ni, 1, b:b + 1]
            rsc = stat[:, ni, 2, b:b + 1]
            st = ("stat", ni, b)
            P.op("act", lambda e: e.activation(out=sqj, in_=src, func=AF.Square, accum_out=ssc),
                 reads=[src_tok, "stat"] + list(extra_tok), writes=["sqj", st])
            if _STG < 2:
                return
            P.op("dve", lambda e: e.tensor_scalar(out=msc, in0=ssc, scalar1=1.0 / D, scalar2=EPS,
                                                   op0=ALU.mult, op1=ALU.add), reads=[st], writes=[st])
            if _STG < 3:
                return
            P.op("pool", lambda e: e.tensor_tensor(out=rsc, in0=msc, in1=nh[:, 0:1], op=ALU.pow),
                 reads=[st, "nh"], writes=[st])
            if _STG < 4:
                return
            P.op("dve", lambda e: e.scalar_tensor_tensor(out=hb, in0=src, scalar=rsc, in1=gv,
                                                          op0=ALU.mult, op1=ALU.mult),
                 reads=[src_tok, st, "gv"], writes=[hb_tok])
            if _STG < 5:
                return
            if defer:
                return lambda: tr8(hb, hb_tok, bank, dstT, dst_tok)
            tr8(hb, hb_tok, bank, dstT, dst_tok)

        xt = [sb(SCR + 0, [128, D], F32), sb(SCR + 4096, [128, D], F32)]
        hbA = [sb(SCR + 8192, [128, D], BF16), sb(SCR + 10240, [128, D], BF16)]
        gvA = sb(SCR + 12288, [128, D], F32)
        sqjA = sb(SCR + 16384, [128, D], F32)
        P.dma("sp", gvA[:], norm_mix_g.partition_broadcast(128), writes=["gv"])
        Vs = sb(G0, [128, 4, NB, 192], BF16)
        Wv = sb(G0 + 24576, [128, 8, 512], BF16)
        oB = SCR + 20480
        wf = sb(oB, [128, 8, 16], BF16)
        wq = [sb(oB + 29760, [128, 8, 128], BF16), sb(oB + 29760 + 2048, [128, 8, 128], BF16)]
        wk = [sb(oB + 33856, [128, 8, 128], BF16), sb(oB + 33856 + 2048, [128, 8, 128], BF16)]
        P.dma("pool", wf[:], wsrc(w_in, O_F, O_F + 16), writes=["wf"])
        P.dma("pool", Wv[:], wsrc(w_in, O_V, O_V + 512), writes=["Wv"])
        P.dma("pool", wq[0][:], wsrc(w_in, O_Q, O_Q + 128), writes=[("wq", 0)])
        P.dma("pool", wk[0][:], wsrc(w_in, O_K, O_K + 128), writes=[("wk", 0)])
        if upto == 'p':
            raise _Stop()
        trA = {}
        for t in range(NB + 1):
            if t >= 1:
                trA[t - 1]()
            if t < NB:
                b = t
                P.dma("sp", xt[b % 2][:], x[b * 128:(b + 1) * 128, :], writes=[("xt", b % 2)])
                trA[b] = norm_block(0, b, xt[b % 2][:], gvA[:], hbA[b % 2][:], sqjA[:], 2 * (b % 2),
                                    hT[:, :, b * 128:(b + 1) * 128], ("xt", b % 2), ("hbA", b % 2), ("hT", b), defer=True)

        if upto == 'A':
            raise _Stop()
        o = SCR + 20480
        o += 256
        Lf = sb(o, [128, NB, 16], F32); o += 1024
        Sacc = sb(o, [128, NB + 1, 16], F32); o += 1088
        Cc = sb(o, [128, NB, 16], F32); o += 1024
        Mm = sb(o, [128, NB, 16], F32); o += 1024
        xf = sb(o, [128, 2, 16], F32); o += 128
        ef = sb(o, [128, 2, 16], F32); o += 128
        o = (o + 63) // 64 * 64
        NPAIR = NB * (NB + 1) // 2
        biasT = sb(o, [128, NPAIR, 16], F32); o += NPAIR * 64
        qT = [sb(o, [128, S], BF16), sb(o + 4096, [128, S], BF16)]; o += 8192
        kT = [sb(o, [128, S], BF16), sb(o + 4096, [128, S], BF16)]; o += 8192
        assert o == oB + 29760, (o, oB)
        o += 8192
        NPT = 4
        Pt = [sb(o + i * 1024, [128, 512], BF16) for i in range(NPT)]; o += NPT * 1024
        rl = [sb(o, [128, 512], F32), sb(o + 2048, [128, 512], F32)]; o += 4096
        bc = [sb(o, [128, 512], F32), sb(o + 2048, [128, 512], F32)]; o += 4096
        assert o <= SCR_END, o

        def pidx(u, kb):
            return u * (u + 1) + kb

        oA = AT
        tri_b = sb(oA, [128, 128], BF16); oA += 256
        sel_b = sb(oA, [128, 128], BF16); oA += 256
        ones_b = sb(oA, [128, 128], BF16); oA += 256
        Lp = [sb(oA + i * 512, [128, NB, 16], BF16) for i in range(3)]; oA += 1536
        Sp = [sb(oA + i * 544, [128, NB + 1, 16], BF16) for i in range(3)]; oA += 1632
        RL = [sb(oA + i * 1024, [128, NB, 16], F32) for i in range(2)]; oA += 2048
        RS = [sb(oA + i * 1088, [128, NB + 1, 16], F32) for i in range(2)]; oA += 2176
        rlp = [[sb(oA + (f_ * 2 + i) * 1024, [128, 512], BF16) for i in range(2)] for f_ in range(2)]; oA += 4096
        rlr = [sb(oA + f_ * 2048, [128, 512], F32) for f_ in range(2)]; oA += 4096
        P.dma("pool", tri_b[:], c_tri, writes=["tri_b"])
        P.dma("pool", sel_b[:], c_sel, writes=["sel_b"])
        P.op("pool", lambda e: e.memset(ones_b[:], 1.0), writes=["ones_b"])
        for i in range(3):
            P.op("pool", lambda e, i=i: e.memset(Sp[i][:, 0, :], 0.0), writes=[("Sp", 0, i)])

        def split(src, src_tok, pieces, tmps, key):
            cur, cur_tok = src, src_tok
            for i, pc in enumerate(pieces):
                P.op("dve", lambda e, pc=pc, cur=cur: e.tensor_copy(out=pc, in_=cur), reads=[cur_tok], writes=[key + (i,)])
                if i < len(pieces) - 1:
                    P.op("dve", lambda e, pc=pc, cur=cur, t=tmps[i]: e.tensor_tensor(out=t, in0=cur, in1=pc, op=ALU.subtract),
                         reads=[cur_tok, key + (i,)], writes=[key + ("r", i)])
                    cur, cur_tok = tmps[i], key + ("r", i)

        if not os.environ.get('KDBG_NOMS'):
            P.op("dve", lambda e: e.memset(Vs[:, :, :, 65:128], 0.0), writes=["Vs_c0"])
            P.op("dve", lambda e: e.memset(Vs[:, :, :, 64:65], 1.0), writes=["Vs_c"])
        P.op("pool", lambda e: e.memset(Sacc[:, 0, :], 0.0), writes=[("Sacc", 0)])

        for b in range(NB):
            for kc in range(8):
                mm(ps[2][:, 0:16], hT[:, kc, b * 128:(b + 1) * 128], wf[:, kc, :], kc == 0, kc == 7,
                   [("hT", b), "wf"], [("ps", 2)])
            P.op("dve", lambda e, b=b: e.tensor_tensor(out=xf[:, b % 2, :], in0=ps[2][:, 0:16], in1=bfb[:], op=ALU.add),
                 reads=[("ps", 2), "bfb"], writes=[("xf", b % 2)])
            P.op("act", lambda e, b=b: e.activation(out=ef[:, b % 2, :], in_=xf[:, b % 2, :], func=AF.Exp, scale=-1.0),
                 reads=[("xf", b % 2)], writes=[("ef", b % 2)])
            P.op("act", lambda e, b=b: e.activation(out=Lf[:, b, :], in_=ef[:, b % 2, :], func=AF.Ln, bias=onesf[:, 0:1]),
                 reads=[("ef", b % 2), "onesf"], writes=[("Lf", b)])
            P.op("dve", lambda e, b=b: e.tensor_tensor(out=Sacc[:, b + 1, :], in0=Sacc[:, b, :], in1=Lf[:, b, :], op=ALU.add),
                 reads=[("Sacc", b), ("Lf", b)], writes=[("Sacc", b + 1)])
            split(Lf[:, b, :], ("Lf", b), [Lp[i][:, b, :] for i in range(3)], [RL[i][:, b, :] for i in range(2)], ("Lp", b))
            split(Sacc[:, b + 1, :], ("Sacc", b + 1), [Sp[i][:, b + 1, :] for i in range(3)],
                  [RS[i][:, b + 1, :] for i in range(2)], ("Sp", b + 1))
        for b in range(NB):
            for (cols, mat, mtok) in ((slice(0, 16), tri_b, "tri_b"), (slice(16, 32), sel_b, "sel_b")):
                for i in range(3):
                    mm(ps[3][:, cols], mat[:], Lp[i][:, b, :], i == 0, False, [mtok, ("Lp", b, i)], [("ps", 3)])
                for i in range(3):
                    mm(ps[3][:, cols], ones_b[:], Sp[i][:, b, :], False, i == 2, ["ones_b", ("Sp", b, i)], [("ps", 3)])
            P.op("dve", lambda e, b=b: e.tensor_copy(out=Cc[:, b, :], in_=ps[3][:, 0:16]), reads=[("ps", 3)], writes=[("Cc", b)])
            P.op("dve", lambda e, b=b: e.tensor_copy(out=Mm[:, b, :], in_=ps[3][:, 16:32]), reads=[("ps", 3)], writes=[("Mm", b)])
        for u in range(NB // 2):
            for kb in range(2 * u + 2):
                P.op("pool", lambda e, u=u, kb=kb: e.tensor_tensor(out=biasT[:, pidx(u, kb), :], in0=Cc[:, kb, :],
                                                                   in1=Mm[:, 2 * u, :], op=ALU.subtract),
                     reads=[("Cc", kb), ("Mm", 2 * u)], writes=[("biasT", u, kb)])

        _KB = int(os.environ.get('KDBG_B', '9'))
        if _KB < 2:
            raise _Stop()
        uid = [0]
        ugc = [0]
        for gh in range(2):
            for b in range(NB):
                bk = int(os.environ.get('KDBG_VB', '4')) + (b % 2)
                for kc in range(8):
                    mm(ps[bk][:], hT[:, kc, b * 128:(b + 1) * 128], Wv[:, kc, :], kc == 0, kc == 7,
                       [("hT", b), "Wv"], [("ps", bk)])
                pv = ps[bk][:].rearrange("p (a n) -> p a n", a=4)
                _vc = int(os.environ.get('KDBG_VC', '2'))
                if _vc >= 1:
                    P.op("act", lambda e, b=b, pv=pv: e.copy(out=Vs[:, :, b, 0:64], in_=pv[:, :, 0:64]),
                         reads=[("ps", bk)], writes=[("Vs", b)])
                if _vc >= 2:
                    P.op("act", lambda e, b=b, pv=pv: e.copy(out=Vs[:, :, b, 128:192], in_=pv[:, :, 64:128]),
                         reads=[("ps", bk)], writes=[("VsB", b)])
            if _KB < 3:
                raise _Stop()
            if gh == 0:
                P.dma("pool", Wv[:], wsrc(w_in, O_V + 512, O_V + 1024), writes=["Wv"])
            for pr in range(4):
                c = gh * 4 + pr
                sl = c % 2
                for j in range(NT):
                    for (wt, dst, tok, bk) in ((wq[sl], qT[sl], "qT", 0), (wk[sl], kT[sl], "kT", 1)):
                        for kc in range(8):
                            mm(ps[bk][:], wt[:, kc, :], hT[:, kc, j * 512:(j + 1) * 512], kc == 0, kc == 7,
                               [("hT", 4 * j), ("hT", 4 * j + 1), ("hT", 4 * j + 2), ("hT", 4 * j + 3),
                                ("wq" if tok == "qT" else "wk", sl)], [("ps", bk)])
                        P.op("dve", lambda e, dst=dst, bk=bk, j=j: e.tensor_copy(out=dst[:, j * 512:(j + 1) * 512], in_=ps[bk][:]),
                             reads=[("ps", bk)], writes=[(tok, sl, j)])
                if _KB < 4:
                    raise _Stop()
                if c + 1 < 8:
                    c1 = c + 1
                    P.dma("pool", wq[c1 % 2][:], wsrc(w_in, O_Q + c1 * 128, O_Q + (c1 + 1) * 128), writes=[("wq", c1 % 2)])
                    P.dma("pool", wk[c1 % 2][:], wsrc(w_in, O_K + c1 * 128, O_K + (c1 + 1) * 128), writes=[("wk", c1 % 2)])
                units = [(hh, j, kb) for hh in range(2) for j in range(NT) for kb in range(4 * j + 4)]
                LA = 2
                tinfo = {}
                pend = []

                def qk_exp(hh, j, kb, ug):
                    h = 2 * c + hh
                    r0 = 64 * hh
                    i = kb - 4 * j
                    q0 = max(0, i) * 128
                    sbk = 2 + (ug % 3)
                    pt = Pt[ug % NPT]
                    ptok = ("Pt", ug % NPT)
                    mm(ps[sbk][:, q0:512], kT[sl][r0:r0 + 64, kb * 128:(kb + 1) * 128],
                       qT[sl][r0:r0 + 64, j * 512 + q0:(j + 1) * 512], True, True,
                       [("kT", sl, kb // 4), ("qT", sl, j)], [("ps", sbk)])
                    for g in range(q0 // 256, 2):
                        u = 2 * j + g
                        ca, cz = max(q0, 256 * g), 256 * (g + 1)
                        P.op("act", lambda e, sbk=sbk, ca=ca, cz=cz, pt=pt, u=u, kb=kb, h=h: e.activation(
                            out=pt[:, ca:cz], in_=ps[sbk][:, ca:cz],
                            func=AF.Exp, bias=biasT[:, pidx(u, kb), h:h + 1], scale=0.125),
                            reads=[("ps", sbk), ("biasT", u, kb)], writes=[ptok])
                    if i >= 0:
                        P.op("dve", lambda e, pt=pt, i=i: e.tensor_tensor(
                            out=pt[:, i * 128:(i + 1) * 128], in0=pt[:, i * 128:(i + 1) * 128],
                            in1=maskb[:], op=ALU.mult), reads=[ptok, "maskb"], writes=[ptok])

                def pv(hh, j, kb, ug, step):
                    r0 = 64 * hh
                    lrow = 64 if hh == 0 else 0
                    vsl = slice(0, 65) if hh == 0 else slice(64, 192)
                    nk = 4 * j + 4
                    if kb == 0:
                        tinfo[(hh, j)] = (5 + (uid[0] % 2), uid[0] % 2)
                        uid[0] += 1
                    ob, fin = tinfo[(hh, j)]
                    q0 = max(0, kb - 4 * j) * 128
                    pt = Pt[ug % NPT]
                    ptok = ("Pt", ug % NPT)
                    mm(ps[ob][0:(65 if hh == 0 else 128), q0:512], Vs[:, pr, kb, vsl], pt[:, q0:512], kb == 0, kb == nk - 1,
                       [("Vs", kb), ("VsB", kb), "Vs_c", "Vs_c0", ptok], [("ps", ob)], skip=True)
                    if kb != nk - 1:
                        return
                    P.op("dve", lambda e: e.reciprocal(out=rl[fin][lrow:lrow + 1, :], in_=ps[ob][lrow:lrow + 1, :]),
                         reads=[("ps", ob)], writes=[("rl", fin)])
                    split(rl[fin][lrow:lrow + 1, :], ("rl", fin), [rlp[fin][i][lrow:lrow + 1, :] for i in range(2)],
                          [rlr[fin][lrow:lrow + 1, :]], ("rlp", fin))

                    def fin2():
                        for i in range(2):
                            mm(ps[7][:], ones_b[lrow:lrow + 1, :], rlp[fin][i][lrow:lrow + 1, :], i == 0, i == 1,
                               ["ones_b", ("rlp", fin, i)], [("ps", 7)])
                        P.op("act", lambda e: e.copy(out=bc[fin][r0:r0 + 64, :], in_=ps[7][r0:r0 + 64, :]),
                             reads=[("ps", 7)], writes=[("bc", fin)])
                        P.op("dve", lambda e, c=c: e.tensor_tensor(
                            out=bT[r0:r0 + 64, c, j * 512:(j + 1) * 512], in0=ps[ob][r0:r0 + 64, :],
                            in1=bc[fin][r0:r0 + 64, :], op=ALU.mult),
                            reads=[("ps", ob), ("bc", fin)], writes=[("bT", c, j, hh)])
                    nxt = 4 * (j + 1) + 4 if j + 1 < NT else (4 if hh == 0 else 0)
                    pend.append((step + min(7, max(nxt, 1)), fin2))

                ug0 = ugc[0]
                for t in range(len(units) + LA):
                    if t < len(units):
                        qk_exp(*units[t], ug0 + t)
                    if t - LA >= 0:
                        pv(*units[t - LA], ug0 + t - LA, t)
                    while pend and pend[0][0] <= t:
                        pend.pop(0)[1]()
                while pend:
                    pend.pop(0)[1]()
                ugc[0] += len(units)

        if upto == 'B':
            raise _Stop()
        P.barrier()
        Wuv = sb(G0, [128, 8, 2048], BF16)
        o = SCR
        ub = [sb(o, [128, D], BF16), sb(o + 2048, [128, D], BF16)]; o += 4096
        v32 = [sb(o, [128, D], F32), sb(o + 4096, [128, D], F32)]; o += 8192
        vln = [sb(o, [128, D], BF16), sb(o + 2048, [128, D], BF16)]; o += 4096
        ab = [sb(o, [128, D], BF16), sb(o + 2048, [128, D], BF16)]; o += 4096
        lng = sb(o, [128, D], F32); o += 4096
        lnb = sb(o, [128, D], F32); o += 4096
        WsT = sb(o, [128, 8, 128], BF16); o += 2048
        sqjC = sb(o, [128, D], F32); o += 4096
        for q in range(4):
            P.dma("pool", Wuv[:, :, q * 512:(q + 1) * 512], wsrc(w_in, q * 512, (q + 1) * 512), writes=[("Wuv", q)])
        P.dma("sp", lng[:], ln_g.partition_broadcast(128), writes=["lng"])
        P.dma("sp", lnb[:], ln_b.partition_broadcast(128), writes=["lnb"])
        P.dma("pool", WsT[:], ws_t, writes=["WsT"])
        P.op("pool", lambda e: e.memset(WsT[64:128, :, 0:64], 0.0), reads=["WsT"], writes=["WsT"])

        def C0_(b):
            for ct in range(4):
                bk = ct
                for kc in range(8):
                    mm(ps[bk][:], hT[:, kc, b * 128:(b + 1) * 128], Wuv[:, kc, ct * 512:(ct + 1) * 512], kc == 0, kc == 7,
                       [("hT", b), ("Wuv", ct)], [("ps", bk)])
                if ct < 2:
                    P.op("act", lambda e, b=b, ct=ct, bk=bk: e.activation(
                        out=ub[b % 2][:, ct * 512:(ct + 1) * 512], in_=ps[bk][:], func=AF.Gelu_apprx_tanh),
                        reads=[("ps", bk)], writes=[("ub", b % 2, ct)])
                else:
                    P.op("act", lambda e, b=b, ct=ct, bk=bk: e.activation(
                        out=v32[b % 2][:, (ct - 2) * 512:(ct - 1) * 512], in_=ps[bk][:], func=AF.Gelu_apprx_tanh,
                        accum_out=lnst[:, b, ct - 2:ct - 1]),
                        reads=[("ps", bk), "lnst"], writes=[("v32", b % 2, ct - 2), ("lnst", b, ct - 2)])

        def C1_(b):
            s = b % 2
            L = lambda i: lnst[:, b, i:i + 1]
            lt = ("lnstb", b)
            P.op("act", lambda e: e.activation(out=sqjC[:], in_=v32[s][:], func=AF.Square, accum_out=L(2)),
                 reads=[("v32", s, 0), ("v32", s, 1), "lnst"], writes=["sqjC", lt])
            P.op("dve", lambda e: e.tensor_tensor(out=L(0), in0=L(0), in1=L(1), op=ALU.add),
                 reads=[("lnst", b, 0), ("lnst", b, 1)], writes=[("lnst", b, 0)])
            P.op("dve", lambda e: e.tensor_scalar(out=L(0), in0=L(0), scalar1=1.0 / D, scalar2=0.0, op0=ALU.mult, op1=ALU.add),
                 reads=[("lnst", b, 0)], writes=[("lnst", b, 0)])
            P.op("dve", lambda e: e.tensor_tensor(out=L(1), in0=L(0), in1=L(0), op=ALU.mult),
                 reads=[("lnst", b, 0)], writes=[("lnst", b, 1)])
            P.op("dve", lambda e: e.scalar_tensor_tensor(out=L(3), in0=L(2), scalar=1.0 / D, in1=L(1),
                                                          op0=ALU.mult, op1=ALU.subtract),
                 reads=[lt, ("lnst", b, 1)], writes=[lt])
            P.op("dve", lambda e: e.tensor_scalar(out=L(3), in0=L(3), scalar1=EPS, scalar2=0.0, op0=ALU.add, op1=ALU.add),
                 reads=[lt], writes=[lt])
            P.op("pool", lambda e: e.tensor_tensor(out=L(4), in0=L(3), in1=nh[:, 0:1], op=ALU.pow),
                 reads=[lt, "nh"], writes=[lt])
            P.op("dve", lambda e: e.scalar_tensor_tensor(out=L(5), in0=L(0), scalar=-1.0, in1=L(4),
                                                          op0=ALU.mult, op1=ALU.mult),
                 reads=[lt, ("lnst", b, 0)], writes=[lt])
            P.op("dve", lambda e: e.tensor_scalar(out=v32[s][:], in0=v32[s][:], scalar1=L(4), scalar2=L(5),
                                                   op0=ALU.mult, op1=ALU.add),
                 reads=[lt, ("v32", s, 0), ("v32", s, 1)], writes=[("v32", s, 0), ("v32", s, 1)])
            P.op("dve", lambda e: e.tensor_tensor(out=v32[s][:], in0=v32[s][:], in1=lng[:], op=ALU.mult),
                 reads=[("v32", s, 0), ("v32", s, 1), "lng"], writes=[("v32", s, 0), ("v32", s, 1)])
            P.op("dve", lambda e: e.tensor_tensor(out=vln[s][:], in0=v32[s][:], in1=lnb[:], op=ALU.add),
                 reads=[("v32", s, 0), ("v32", s, 1), "lnb"], writes=[("vln", s)])
            for g in range(8):
                bk = 4 + g // 4
                mm(ps[bk][:, (g % 4) * 128:(g % 4 + 1) * 128], WsT[:, g, :], vln[s][:, g * 128:(g + 1) * 128], True, True,
                   ["WsT", ("vln", s)], [("ps", bk)])

        def C2_(b):
            s = b % 2
            for g in range(8):
                bk = 4 + g // 4
                P.op("dve", lambda e, g=g, bk=bk: e.scalar_tensor_tensor(
                    out=ab[s][:, g * 128:(g + 1) * 128], in0=ps[bk][:, (g % 4) * 128:(g % 4 + 1) * 128],
                    scalar=bs[:, g:g + 1], in1=ub[s][:, g * 128:(g + 1) * 128], op0=ALU.add, op1=ALU.mult),
                    reads=[("ps", bk), "bs", ("ub", s, g // 4)], writes=[("ab", s)])
            tr8(ab[s], ("ab", s), 6, aT[:, :, b * 128:(b + 1) * 128], ("aT", b))

        for t in range(NB + 2):
            if 0 <= t - 2 < NB:
                C2_(t - 2)
            if 0 <= t - 1 < NB:
                C1_(t - 1)
            if t < NB:
                C0_(t)

        if upto == 'C':
            raise _Stop()
        P.barrier()
        dump("hT", hT[:])
        dump("bT", bT[:])
        dump("aT", aT[:])
        o = SCR
        wsm = [[sb(o + (s * 4 + i) * 2048, [128, 8, 128], BF16) for i in range(4)] for s in range(2)]; o += 16384
        gsb = [[sb(o + (s * 2 + i) * 2048, [128, 512], F32) for i in range(2)] for s in range(2)]; o += 8192
        tsb = [[sb(o + (s * 2 + i) * 2048, [128, 512], F32) for i in range(2)] for s in range(2)]; o += 8192
        Wout = sb(o, [128, 8, D], BF16); o += 16384
        oE = o
        P.dma("pool", Wout[:, :, 0:512], wsrc(w_out, 0, 512), writes=[("Wout", 0)])
        P.dma("pool", Wout[:, :, 512:1024], wsrc(w_out, 512, 1024), writes=[("Wout", 1)])
        step = 0

        def loadD(m):
            srcs = (wsrc(w_in, O_G + m * 128, O_G + (m + 1) * 128),
                    wsrc(w_in, O_G + D + m * 128, O_G + D + (m + 1) * 128),
                    wsrc(w_a, m * 128, (m + 1) * 128), wsrc(w_b, m * 128, (m + 1) * 128))
            for i in range(4):
                P.dma("pool", wsm[m % 2][i][:], srcs[i], writes=[("wsm", m % 2, i)])

        loadD(0)
        for m in range(8):
            s = m % 2
            if m + 1 < 8:
                loadD(m + 1)
            for j in range(NT):
                pb = 4 * (step % 2)
                sg = step % 2
                step += 1
                acts = (hT, hT, aT, bT)
                for i in range(4):
                    for kc in range(8):
                        if i < 2:
                            rd = [("hT", 4 * j + u) for u in range(4)]
                        elif i == 2:
                            rd = [("aT", 4 * j + u) for u in range(4)]
                        else:
                            rd = [("bT", kc, j, 0), ("bT", kc, j, 1)]
                        mm(ps[pb + i][:], wsm[s][i][:, kc, :], acts[i][:, kc, j * 512:(j + 1) * 512], kc == 0, kc == 7,
                           rd + [("wsm", s, i)], [("ps", pb + i)])
                for i in range(2):
                    P.op("act", lambda e, i=i, pb=pb, sg=sg: e.activation(out=gsb[sg][i][:], in_=ps[pb + i][:], func=AF.Sigmoid),
                         reads=[("ps", pb + i)], writes=[("gsb", sg, i)])
                for i in range(2):
                    P.op("dve", lambda e, i=i, pb=pb, sg=sg: e.tensor_tensor(out=tsb[sg][i][:], in0=ps[pb + 2 + i][:],
                                                                           in1=gsb[sg][i][:], op=ALU.mult),
                         reads=[("ps", pb + 2 + i), ("gsb", sg, i)], writes=[("tsb", sg, i)])
                P.op("pool", lambda e, sg=sg, m=m, j=j: e.tensor_tensor(out=g0[:, m, j * 512:(j + 1) * 512], in0=tsb[sg][0][:],
                                                                       in1=tsb[sg][1][:], op=ALU.add),
                     reads=[("tsb", sg, 0), ("tsb", sg, 1)], writes=[("mg", m, j)])

        if upto == 'D':
            raise _Stop()
        P.barrier()
        dump("mg", g0[:])
        o = oE
        xr = [sb(o, [128, D], F32), sb(o + 4096, [128, D], F32)]; o += 8192
        hbE = [sb(o, [128, D], BF16), sb(o + 2048, [128, D], BF16)]; o += 4096
        gvE = sb(o, [128, D], F32); o += 4096
        sqjE = sb(o, [128, D], F32); o += 4096
        assert o <= SCR_END
        P.dma("sp", gvE[:], norm_ffn_g.partition_broadcast(128), writes=["gv"])
        trE = {}
        for t in range(NB + 1):
            if t >= 1:
                trE[t - 1]()
            if t >= NB:
                continue
            b = t
            j = b // 4
            P.dma("sp", xr[b % 2][:], x[b * 128:(b + 1) * 128, :], writes=[("xr", b % 2)])
            for ch in range(2):
                bk = 2 * (b % 2) + ch
                for m in range(8):
                    mm(ps[bk][:], g0[:, m, b * 128:(b + 1) * 128], Wout[:, m, ch * 512:(ch + 1) * 512], m == 0, m == 7,
                       [("mg", m, j), ("Wout", ch)], [("ps", bk)])
                P.op("dve", lambda e, b=b, ch=ch, bk=bk: e.tensor_tensor(
                    out=x1[:, b, ch * 512:(ch + 1) * 512], in0=ps[bk][:], in1=xr[b % 2][:, ch * 512:(ch + 1) * 512], op=ALU.add),
                    reads=[("ps", bk), ("xr", b % 2)], writes=[("x1", b, ch)])
            trE[b] = norm_block(1, b, x1[:, b, :], gvE[:], hbE[b % 2][:], sqjE[:], 4 + 2 * (b % 2),
                                h2T[:, :, b * 128:(b + 1) * 128], ("x1", b, 0), ("hbE", b % 2), ("h2T", b),
                                extra_tok=[("x1", b, 1)], defer=True)

        if upto == 'E':
            raise _Stop()
        P.barrier()
        dump("x1", x1[:])
        NG = 11
        actT = sb(SCR, [128, NG, S], BF16)
        Wd = sb(SCR + 45056, [128, NG, D], BF16)
        o = SCR + 67584
        Ag = [sb(o, [128, 512], F32), sb(o + 2048, [128, 512], F32)]; o += 4096
        Al = [sb(o, [128, 512], F32), sb(o + 2048, [128, 512], F32)]; o += 4096
        assert o <= SCR_END
        o = G0
        NW = 3
        wup = [[sb(o + (s * 2 + i) * 2048, [128, 8, 128], BF16) for i in range(2)] for s in range(NW)]; o += NW * 4096
        NR = 3
        Rg = [sb(o + s * 2080, [128, 514], F32) for s in range(NR)]; o += NR * 2080
        Rl = [sb(o + s * 2080, [128, 514], F32) for s in range(NR)]; o += NR * 2080
        Gg = [sb(o, [128, 512], F32), sb(o + 2048, [128, 512], F32)]; o += 4096
        assert o <= G0 + 32 * KB
        rstep = 0
        _KF = int(os.environ.get('KDBG_F', '9'))

        def loadF(fc):
            P.dma("pool", wup[fc % NW][0][:], wsrc(w_up, fc * 128, (fc + 1) * 128), writes=[("wup", fc % NW, 0)])
            P.dma("pool", wup[fc % NW][1][:], wsrc(w_up, DFF + fc * 128, DFF + (fc + 1) * 128), writes=[("wup", fc % NW, 1)])

        loadF(0)
        loadF(1)
        for grp in range(2):
            for q in range(2):
                P.dma("pool", Wd[:, :, q * 512:(q + 1) * 512],
                      w_down[grp * NG * 128:(grp + 1) * NG * 128, q * 512:(q + 1) * 512].rearrange("(c p) n -> p c n", p=128),
                      writes=[("Wd", q)])
            for fl in range(NG):
                fc = grp * NG + fl
                ws_ = fc % NW
                if fc + 2 < NFC:
                    loadF(fc + 2)
                for j in range(NT):
                    rs = rstep % NR
                    rp = (rstep - 1) % NR
                    sa = rstep % 2
                    pb = 2 * (rstep % 2)
                    rstep += 1
                    for i in range(2):
                        for kc in range(8):
                            mm(ps[pb + i][:], wup[ws_][i][:, kc, :], h2T[:, kc, j * 512:(j + 1) * 512], kc == 0, kc == 7,
                               [("h2T", 4 * j + u) for u in range(4)] + [("wup", ws_, i)], [("ps", pb + i)])
                    for i, (R, A, ci) in enumerate(((Rg, Ag, fc), (Rl, Al, NFC + fc))):
                        rt = ("R", i, rs)
                        P.op("act", lambda e, R=R, i=i, pb=pb, rs=rs: e.copy(out=R[rs][:, 2:514], in_=ps[pb + i][:]),
                             reads=[("ps", pb + i)], writes=[rt])
                        rn = (rs + 1) % NR
                        if j == 0:
                            P.op("dve", lambda e, R=R, rs=rs: e.memset(R[rs][:, 0:2], 0.0), writes=[("Rh", i, rs)])
                        if j < NT - 1:
                            P.op("act", lambda e, R=R, i=i, pb=pb, rn=rn: e.copy(out=R[rn][:, 0:2], in_=ps[pb + i][:, 510:512]),
                                 reads=[("ps", pb + i)], writes=[("Rh", i, rn)])
                        at = ("A", i, sa)
                        if _KF < 2:
                            continue
                        if i == 0:
                            P.op("act", lambda e, A=A, pb=pb, sa=sa, ci=ci: e.activation(
                                out=A[sa][:], in_=ps[pb][:], func=AF.Identity, scale=cw[:, ci, 2:3], bias=cb[:, ci:ci + 1]),
                                reads=[("ps", pb), "cw", "cb"], writes=[at])
                        else:
                            P.op("dve", lambda e, A=A, R=R, rs=rs, sa=sa, ci=ci: e.tensor_scalar(
                                out=A[sa][:], in0=R[rs][:, 2:514], scalar1=cw[:, ci, 2:3], scalar2=cb[:, ci:ci + 1],
                                op0=ALU.mult, op1=ALU.add), reads=[rt, "cw", "cb"], writes=[at])
                        _f2 = os.environ.get('KDBG_F2', '')
                        if _f2 == 'a':
                            continue
                        w1s, w0s = (slice(1, 513), slice(0, 512)) if _f2 != 'al' else (slice(2, 514), slice(2, 514))
                        P.op("dve", lambda e, A=A, R=R, rs=rs, sa=sa, ci=ci, w1s=w1s: e.scalar_tensor_tensor(
                            out=A[sa][:], in0=R[rs][:, w1s], scalar=cw[:, ci, 1:2], in1=A[sa][:],
                            op0=ALU.mult, op1=ALU.add), reads=[rt, ("Rh", i, rs), at, "cw"], writes=[at])
                        P.op("dve", lambda e, A=A, R=R, rs=rs, sa=sa, ci=ci, w0s=w0s: e.scalar_tensor_tensor(
                            out=A[sa][:], in0=R[rs][:, w0s], scalar=cw[:, ci, 0:1], in1=A[sa][:],
                            op0=ALU.mult, op1=ALU.add), reads=[rt, ("Rh", i, rs), at, "cw"], writes=[at])
                    if _KF < 3:
                        continue
                    P.op("act", lambda e, sa=sa: e.activation(out=Gg[sa][:], in_=Ag[sa][:], func=AF.Gelu_apprx_tanh),
                         reads=[("A", 0, sa)], writes=[("Gg", sa)])
                    P.op("pool", lambda e, sa=sa, fl=fl, j=j: e.tensor_tensor(
                        out=actT[:, fl, j * 512:(j + 1) * 512], in0=Gg[sa][:], in1=Al[sa][:], op=ALU.mult),
                        reads=[("Gg", sa), ("A", 1, sa)], writes=[("actT", fl, j)])
            for b in range(NB if _KF >= 4 else 0):
                j = b // 4
                for ch in range(2):
                    bk = 4 + 2 * (b % 2) + ch
                    for fl in range(NG):
                        mm(ps[bk][:], actT[:, fl, b * 128:(b + 1) * 128], Wd[:, fl, ch * 512:(ch + 1) * 512],
                           fl == 0, fl == NG - 1, [("actT", fl, j), ("Wd", ch)], [("ps", bk)])
                    P.op("dve", lambda e, b=b, ch=ch, bk=bk: e.tensor_tensor(
                        out=x1[:, b, ch * 512:(ch + 1) * 512], in0=ps[bk][:], in1=x1[:, b, ch * 512:(ch + 1) * 512], op=ALU.add),
                        reads=[("ps", bk), ("x1", b, ch)], writes=[("x1", b, ch)])

        if upto == 'F':
            raise _Stop()
        P.barrier()
        dump("x2", x1[:])
        o = SCR
        Wg = sb(o, [128, 8, D], BF16); o += 16384
        Wp = sb(o, [128, 2, D], BF16); o += 4096
        pst = [sb(o, [128, PLE], F32), sb(o + 1024, [128, PLE], F32)]; o += 2048
        pb16 = [sb(o, [128, PLE], BF16), sb(o + 512, [128, PLE], BF16)]; o += 1024
        pTb = [sb(o, [128, 2, 128], BF16), sb(o + 512, [128, 2, 128], BF16)]; o += 1024
        hbG = [sb(o, [128, D], BF16), sb(o + 2048, [128, D], BF16)]; o += 4096
        h3Tb = [sb(o, [128, 8, 128], BF16), sb(o + 2048, [128, 8, 128], BF16)]; o += 4096
        sgt = [[sb(o + (s * 2 + i) * 2048, [128, 512], F32) for i in range(2)] for s in range(2)]; o += 8192
        tt = [sb(o, [128, D], F32), sb(o + 4096, [128, D], F32)]; o += 8192
        ot = [sb(o, [128, D], F32), sb(o + 4096, [128, D], F32)]; o += 8192
        gv3 = sb(o, [128, D], F32); o += 4096
        gvf = sb(o, [128, D], F32); o += 4096
        sqjG = sb(o, [128, D], F32); o += 4096
        assert o <= SCR_END
        P.dma("pool", Wg[:, :, 0:512], wsrc(w_pg, 0, 512), writes=[("Wg", 0)])
        P.dma("pool", Wg[:, :, 512:1024], wsrc(w_pg, 512, 1024), writes=[("Wg", 1)])
        P.dma("pool", Wp[:], wsrc(w_ple, 0, D), writes=["Wp"])
        P.dma("sp", gv3[:], norm_ple_g.partition_broadcast(128), writes=["gv"])
        P.dma("sp", gvf[:], norm_final_g.partition_broadcast(128), writes=["gvf"])

        trG = {}

        def G0_(b):
            s = b % 2
            P.dma("sp", pst[s][:], pin[b * 128:(b + 1) * 128, :], writes=[("pst", s)])
            P.op("act", lambda e: e.copy(out=pb16[s][:], in_=pst[s][:]), reads=[("pst", s)], writes=[("pb16", s)])
            trG[b] = norm_block(2, b, x1[:, b, :], gv3[:], hbG[s][:], sqjG[:], 0, h3Tb[s][:],
                                ("x1", b, 0), ("hbG", s), ("h3Tb", s), extra_tok=[("x1", b, 1)], defer=True)

        def G0b_(b):
            s = b % 2
            tr8(pb16[s], ("pb16", s), 4, pTb[s][:], ("pTb", s), nch=2)
            trG[b]()

        def G1_(b):
            s = b % 2
            for ch in range(2):
                bk = 2 + ch
                for kc in range(8):
                    mm(ps[bk][:], h3Tb[s][:, kc, :], Wg[:, kc, ch * 512:(ch + 1) * 512], kc == 0, kc == 7,
                       [("h3Tb", s), ("Wg", ch)], [("ps", bk)])
                P.op("act", lambda e, ch=ch, bk=bk: e.activation(out=sgt[s][ch][:], in_=ps[bk][:], func=AF.Sigmoid),
                     reads=[("ps", bk)], writes=[("sgt", s, ch)])
                bk2 = 6 + ch
                for kc in range(2):
                    mm(ps[bk2][:], pTb[s][:, kc, :], Wp[:, kc, ch * 512:(ch + 1) * 512], kc == 0, kc == 1,
                       [("pTb", s), "Wp"], [("ps", bk2)])
                P.op("dve", lambda e, ch=ch, bk2=bk2: e.tensor_tensor(
                    out=tt[s][:, ch * 512:(ch + 1) * 512], in0=ps[bk2][:], in1=sgt[s][ch][:], op=ALU.mult),
                    reads=[("ps", bk2), ("sgt", s, ch)], writes=[("tt", s, ch)])
                P.op("dve", lambda e, ch=ch: e.tensor_tensor(
                    out=tt[s][:, ch * 512:(ch + 1) * 512], in0=tt[s][:, ch * 512:(ch + 1) * 512],
                    in1=x1[:, b, ch * 512:(ch + 1) * 512], op=ALU.add),
                    reads=[("tt", s, ch), ("x1", b, ch)], writes=[("tt", s, ch)])
            ssc = stat[:, 3, 0, b:b + 1]
            msc = stat[:, 3, 1, b:b + 1]
            rsc = stat[:, 3, 2, b:b + 1]
            st = ("stat", 3, b)
            P.op("act", lambda e: e.activation(out=sqjG[:], in_=tt[s][:], func=AF.Square, accum_out=ssc),
                 reads=[("tt", s, 0), ("tt", s, 1), "stat"], writes=["sqj", st])
            P.op("dve", lambda e: e.tensor_scalar(out=msc, in0=ssc, scalar1=1.0 / D, scalar2=EPS, op0=ALU.mult, op1=ALU.add),
                 reads=[st], writes=[st])
            P.op("pool", lambda e: e.tensor_tensor(out=rsc, in0=msc, in1=nh[:, 0:1], op=ALU.pow), reads=[st, "nh"], writes=[st])
            P.op("dve", lambda e: e.scalar_tensor_tensor(out=ot[s][:], in0=tt[s][:], scalar=rsc, in1=gvf[:],
                                                          op0=ALU.mult, op1=ALU.mult),
                 reads=[("tt", s, 0), ("tt", s, 1), st, "gvf"], writes=[("ot", s)])
            P.dma("sp", out[b * 128:(b + 1) * 128, :], ot[s][:], reads=[("ot", s)], writes=[("out", b)])

        for t in range(NB + 2):
            if 0 <= t - 2 < NB:
                G1_(t - 2)
            if 0 <= t - 1 < NB:
                G0b_(t - 1)
            if t < NB:
                G0_(t)
        P.op("sp", lambda e: e.nop(), reads=[("out", b) for b in range(NB)])
        P.barrier()

    except _Stop:
        P.barrier()

    emit(nc, P)
    es.close()
    return nc


_CACHE = {}


def _prep_shared(inp):
    f = np.float32
    d = {}
    d["norm_mix_g"] = np.ascontiguousarray(inp["norm_mix_g"][0], f)
    d["w_in"] = np.ascontiguousarray(inp["w_in"][0], f)
    d["b_f"] = np.ascontiguousarray(inp["b_f"][0], f)
    d["gmlp_ln_g"] = np.ascontiguousarray(inp["gmlp_ln_g"][0], f)
    d["gmlp_ln_b"] = np.ascontiguousarray(inp["gmlp_ln_b"][0], f)
    d["ws_t"] = np.ascontiguousarray(np.transpose(inp["gmlp_w_s"][0], (2, 0, 1)), f)
    d["bs_t"] = np.ascontiguousarray(inp["gmlp_b_s"][0].T, f)
    d["w_branch_a"] = np.ascontiguousarray(inp["w_branch_a"][0], f)
    d["w_branch_b"] = np.ascontiguousarray(inp["w_branch_b"][0], f)
    d["w_out"] = np.ascontiguousarray(inp["w_out"][0], f)
    d["norm_ffn_g"] = np.ascontiguousarray(inp["norm_ffn_g"][0], f)
    d["w_up"] = np.ascontiguousarray(inp["w_up"][0], f)
    cwv = np.asarray(inp["conv_w"][0], f)
    d["cw_t"] = np.ascontiguousarray(cwv.T.reshape(44, 128, 3).transpose(1, 0, 2), f)
    d["cb_t"] = np.ascontiguousarray(np.asarray(inp["conv_b"][0], f).reshape(44, 128).T, f)
    d["w_down"] = np.ascontiguousarray(inp["w_down"][0], f)
    d["norm_ple_g"] = np.ascontiguousarray(inp["norm_ple_g"][0], f)
    d["w_ple"] = np.ascontiguousarray(inp["w_ple"][0], f)
    d["w_ple_gate"] = np.ascontiguousarray(inp["w_ple_gate"][0], f)
    d["norm_final_g"] = np.ascontiguousarray(inp["norm_final_g"], f)
    d["c_ident"] = np.eye(128, dtype=f)
    r = np.arange(128)
    d["c_tri"] = (r[:, None] <= r[None, :]).astype(f)
    d["c_sel"] = np.ones((128, 128), f)
    return d


def kernel(**inputs):
    inp = {k: np.asarray(v) for k, v in inputs.items()}
    n = 8
    if "nc" not in _CACHE:
        _CACHE["nc"] = build_nc()
    nc = _CACHE["nc"]
    shared = _prep_shared(inp)
    x = np.asarray(inp["x"], np.float32)
    p = np.asarray(inp["p"], np.float32)[0]
    in_maps = []
    for i in range(n):
        m = dict(shared)
        m["x"] = np.ascontiguousarray(x[i])
        m["p"] = np.ascontiguousarray(p[i])
        in_maps.append(m)
    res = run_bass_kernel_spmd(nc, in_maps, core_ids=list(range(n)))
    return np.stack([np.asarray(r["out"], np.float32) for r in res.results], axis=0)
```

```python
import contextlib
import numpy as np
import concourse.bass as bass
import concourse.mybir as mybir
from concourse.bass_utils import run_bass_kernel_spmd

F32 = mybir.dt.float32
BF16 = mybir.dt.bfloat16
AF = mybir.ActivationFunctionType
ALU = mybir.AluOpType

ENGS = ("pe", "act", "dve", "pool", "sp")

S = 2048
D = 1024
NB = 16
NT = 4
DFF = 2816
NFC = 22
IN_COLS = 7184
O_Q = 2048
O_K = 3072
O_V = 4096
O_F = 5120
O_G = 5136
PLE = 256
EPS = 1e-6
KB = 1024
BASE = 16512
import os
_STG = int(os.environ.get('KDBG_STG', '6'))
_NBA = int(os.environ.get('KDBG_NBA', '16'))


class _Op:
    __slots__ = ("fn", "deps", "dma", "flag", "semval", "dsem", "dcount", "waits", "prewait")

    def __init__(self, fn, deps, dma):
        self.fn = fn
        self.deps = deps
        self.dma = dma
        self.flag = False
        self.semval = None
        self.dsem = None
        self.dcount = None
        self.waits = []
        self.prewait = None


class Prog:
    def __init__(self, nc, n_dma_sems=32):
        self.nc = nc
        self.streams = {e: [] for e in ENGS}
        self.last_writer = {}
        self.readers = {}
        self.n_dma_sems = n_dma_sems

    def op(self, eng, fn, reads=(), writes=(), dma=False, extra=None):
        st = self.streams[eng]
        me = (eng, len(st))
        deps = {}
        for t in reads:
            w = self.last_writer.get(t)
            if w is not None:
                deps[w] = True
        for t in writes:
            w = self.last_writer.get(t)
            if w is not None and w not in deps:
                deps[w] = False
            for r in self.readers.get(t, ()):
                if r not in deps:
                    deps[r] = False
        if extra:
            for d in extra:
                deps[d] = True
        deps.pop(me, None)
        for t in writes:
            self.last_writer[t] = me
            self.readers[t] = []
        for t in reads:
            if self.last_writer.get(t) != me:
                self.readers.setdefault(t, []).append(me)
        st.append(_Op(fn, deps, dma))
        return me

    def dma(self, eng, out, in_, reads=(), writes=()):
        return self.op(eng, lambda e: e.dma_start(out=out, in_=in_), reads, writes, dma=True)

    def barrier(self):
        lasts = []
        for e in ENGS:
            st = self.streams[e]
            for i in range(len(st) - 1, -1, -1):
                if not st[i].dma:
                    lasts.append((e, i))
                    break
            n = 0
            for i in range(len(st) - 1, -1, -1):
                if st[i].dma:
                    lasts.append((e, i))
                    n += 1
                    if n >= 16:
                        break
        for e in ENGS:
            self.op(e, lambda g: g.nop(), extra=[d for d in lasts])

    def resolve(self):
        streams = self.streams
        dma_engs = [e for e in ENGS if any(o.dma for o in streams[e])]
        pools = {}
        if dma_engs:
            per = max(2, self.n_dma_sems // len(dma_engs))
            for e in dma_engs:
                pools[e] = min(per, 16)
        self.dma_pool_sizes = pools
        for e in dma_engs:
            k = 0
            counts = [0] * pools[e]
            lastop = [None] * pools[e]
            for i, o in enumerate(streams[e]):
                if not o.dma:
                    continue
                s = k % pools[e]
                k += 1
                if lastop[s] is not None:
                    o.prewait = ((e, s), counts[s])
                counts[s] += 16
                o.dsem = (e, s)
                o.dcount = counts[s]
                lastop[s] = i
        for e in ENGS:
            seen = {f: -1 for f in ENGS}
            seen_d = {}
            for i, o in enumerate(streams[e]):
                if o.prewait is not None:
                    key, cnt = o.prewait
                    if seen_d.get(key, 0) >= cnt:
                        o.prewait = None
                    else:
                        seen_d[key] = cnt
                for (f, n), is_raw in sorted(o.deps.items(), key=lambda kv: (kv[0][0], -kv[0][1])):
                    p = streams[f][n]
                    if p.dma:
                        if seen_d.get(p.dsem, 0) >= p.dcount:
                            continue
                        seen_d[p.dsem] = p.dcount
                        o.waits.append(("d", p.dsem, p.dcount))
                        continue
                    if f == e and not o.dma:
                        if not (is_raw and e != "pe"):
                            continue
                    if n <= seen[f]:
                        continue
                    seen[f] = n
                    p.flag = True
                    o.waits.append(("c", f, n))
        for e in ENGS:
            c = 0
            for o in streams[e]:
                if o.flag:
                    c += 1
                    o.semval = c


def emit(nc, prog):
    prog.resolve()
    with contextlib.ExitStack() as es:
        csem = {e: es.enter_context(nc.semaphore("c_" + e)) for e in ENGS}
        dsem = {}
        for e, n in prog.dma_pool_sizes.items():
            for s in range(n):
                dsem[(e, s)] = es.enter_context(nc.semaphore("d_%s_%d" % (e, s)))
        block = es.enter_context(nc.Block())

        def run(ename, eng):
            for o in prog.streams[ename]:
                if o.prewait is not None:
                    key, cnt = o.prewait
                    eng.wait_ge(dsem[key], cnt)
                for w in o.waits:
                    if w[0] == "d":
                        eng.wait_ge(dsem[w[1]], w[2])
                    else:
                        eng.wait_ge(csem[w[1]], prog.streams[w[1]][w[2]].semval)
                ins = o.fn(eng)
                if o.dma:
                    ins.then_inc(dsem[o.dsem], 16)
                elif o.flag:
                    ins.then_inc(csem[ename], 1)

        @block.tensor
        def _(eng):
            run("pe", eng)

        @block.scalar
        def _(eng):
            run("act", eng)

        @block.vector
        def _(eng):
            run("dve", eng)

        @block.gpsimd
        def _(eng):
            run("pool", eng)

        @block.sync
        def _(eng):
            run("sp", eng)


class _Stop(Exception):
    pass


def build_nc(debug=False, upto=None):
    nc = bass.Bass("TRN2", target_bir_lowering=False)

    def din(name, shape):
        return nc.dram_tensor(name, list(shape), F32, kind="ExternalInput").ap()

    x = din("x", [S, D])
    pin = din("p", [S, PLE])
    norm_mix_g = din("norm_mix_g", [D])
    w_in = din("w_in", [D, IN_COLS])
    b_f = din("b_f", [16])
    ln_g = din("gmlp_ln_g", [D])
    ln_b = din("gmlp_ln_b", [D])
    ws_t = din("ws_t", [128, 8, 128])
    bs_t = din("bs_t", [128, 8])
    w_a = din("w_branch_a", [D, D])
    w_b = din("w_branch_b", [D, D])
    w_out = din("w_out", [D, D])
    norm_ffn_g = din("norm_ffn_g", [D])
    w_up = din("w_up", [D, 2 * DFF])
    cw_t = din("cw_t", [128, 44, 3])
    cb_t = din("cb_t", [128, 44])
    w_down = din("w_down", [DFF, D])
    norm_ple_g = din("norm_ple_g", [D])
    w_ple = din("w_ple", [PLE, D])
    w_pg = din("w_ple_gate", [D, D])
    norm_final_g = din("norm_final_g", [D])
    c_ident = din("c_ident", [128, 128])
    c_tri = din("c_tri", [128, 128])
    c_sel = din("c_sel", [128, 128])
    out = nc.dram_tensor("out", [S, D], F32, kind="ExternalOutput").ap()
    dbg = {}
    if debug:
        for nm in ("hT", "bT", "aT", "mg"):
            dbg[nm] = nc.dram_tensor("dbg_" + nm, [128, 8, S], BF16, kind="ExternalOutput").ap()
        for nm in ("x1", "x2"):
            dbg[nm] = nc.dram_tensor("dbg_" + nm, [128, NB, D], F32, kind="ExternalOutput").ap()

    def dump(nm, src):
        if debug:
            P.dma("sp", dbg[nm], src)
            P.barrier()

    cnt = [0]

    def sb(off, shape, dt):
        cnt[0] += 1
        return nc.alloc_sbuf_tensor_at("t%d" % cnt[0], list(shape), dt, offset=off)

    C0 = BASE
    ident = sb(C0 + 0, [128, 128], BF16)
    onesf = sb(C0 + 256, [128, 128], F32)
    trif = sb(C0 + 768, [128, 128], F32)
    self_ = sb(C0 + 1280, [128, 128], F32)
    maskb = sb(C0 + 1792, [128, 128], BF16)
    cw = sb(C0 + 2048, [128, 44, 3], F32)
    cb = sb(C0 + 2592, [128, 44], F32)
    bs = sb(C0 + 2784, [128, 8], F32)
    bfb = sb(C0 + 2816, [128, 16], F32)
    nh = sb(C0 + 2880, [128, 16], F32)
    stat = sb(C0 + 2944, [128, 4, 3, 16], F32)
    lnst = sb(C0 + 3712, [128, 16, 6], F32)
    G0 = C0 + 4 * KB
    HT = C0 + 36 * KB
    BT = C0 + 68 * KB
    AT = C0 + 100 * KB
    SCR = C0 + 132 * KB
    SCR_END = 229344

    g0 = sb(G0, [128, 8, S], BF16)
    hT = sb(HT, [128, 8, S], BF16)
    bT = sb(BT, [128, 8, S], BF16)
    aT = sb(AT, [128, 8, S], BF16)
    x1 = sb(HT, [128, NB, D], F32)
    h2T = sb(AT, [128, 8, S], BF16)

    es = contextlib.ExitStack()
    ps = [es.enter_context(nc.psum_tensor("ps%d" % i, [128, 512], F32)) for i in range(8)]

    def psb(i):
        return ps[i][:].bitcast(BF16).rearrange("p (c n) -> p c n", c=8)

    P = Prog(nc)

    def mm(o, lhsT, rhs, start, stop, reads, writes, skip=False):
        if skip:
            P.op("pe", lambda e: e.matmul(o, lhsT=lhsT, rhs=rhs, start=start, stop=stop,
                                           skip_group_check=True), reads, writes)
        else:
            P.op("pe", lambda e: e.matmul(o, lhsT=lhsT, rhs=rhs, start=start, stop=stop), reads, writes)

    def wsrc(w, c0, c1):
        return w[:, c0:c1].rearrange("(c p) n -> p c n", p=128)

    try:
        P.dma("pool", ident[:], c_ident, writes=["ident"])
        P.dma("sp", trif[:], c_tri, writes=["trif"])
        P.dma("sp", self_[:], c_sel, writes=["self"])
        P.dma("pool", maskb[:], c_tri, writes=["maskb"])
        P.dma("sp", cw[:], cw_t, writes=["cw"])
        P.dma("sp", cb[:], cb_t, writes=["cb"])
        P.dma("sp", bs[:], bs_t, writes=["bs"])
        P.dma("sp", bfb[:], b_f.partition_broadcast(128), writes=["bfb"])
        P.op("pool", lambda e: e.memset(onesf[:], 1.0), writes=["onesf"])
        P.op("pool", lambda e: e.memset(nh[:], -0.5), writes=["nh"])
        P.op("pool", lambda e: e.memset(stat[:], 0.0), writes=["stat"])
        P.op("pool", lambda e: e.memset(lnst[:], 0.0), writes=["lnst"])

        if upto == 'c':
            raise _Stop()
        def tr8(src, src_tok, bank, dstT, dst_tok, nch=8):
            for c in range(nch):
                bk = bank + c // 4
                P.op("pe", lambda e, c=c, bk=bk: e.matmul(ps[bk][:, (c % 4) * 128:(c % 4 + 1) * 128],
                                                          lhsT=src[:, c * 128:(c + 1) * 128], rhs=ident[:],
                                                          start=True, stop=True),
                     reads=[src_tok, "ident"], writes=[("ps", bk)])
            for hf in range((nch + 3) // 4):
                n = min(4, nch - 4 * hf)
                pvv = ps[bank + hf][:, 0:n * 128].rearrange("p (c n) -> p c n", c=n)
                P.op("act", lambda e, hf=hf, n=n, pvv=pvv: e.copy(out=dstT[:, 4 * hf:4 * hf + n, :], in_=pvv),
                     reads=[("ps", bank + hf)], writes=[dst_tok])

        def norm_block(ni, b, src, gv, hb, sqj, bank, dstT, src_tok, hb_tok, dst_tok, extra_tok=(), defer=False):
            ssc = stat[:, ni, 0, b:b + 1]
            msc = stat[:, ni, 1, b:b + 1]
            rsc = stat[:, ni, 2, b:b + 1]
            st = ("stat", ni, b)
            P.op("act", lambda e: e.activation(out=sqj, in_=src, func=AF.Square, accum_out=ssc),
                 reads=[src_tok, "stat"] + list(extra_tok), writes=["sqj", st])
            if _STG < 2:
                return
            P.op("dve", lambda e: e.tensor_scalar(out=msc, in0=ssc, scalar1=1.0 / D, scalar2=EPS,
                                                   op0=ALU.mult, op1=ALU.add), reads=[st], writes=[st])
            if _STG < 3:
                return
            P.op("pool", lambda e: e.tensor_tensor(out=rsc, in0=msc, in1=nh[:, 0:1], op=ALU.pow),
                 reads=[st, "nh"], writes=[st])
            if _STG < 4:
                return
            P.op("dve", lambda e: e.scalar_tensor_tensor(out=hb, in0=src, scalar=rsc, in1=gv,
                                                          op0=ALU.mult, op1=ALU.mult),
                 reads=[src_tok, st, "gv"], writes=[hb_tok])
            if _STG < 5:
                return
            if defer:
                return lambda: tr8(hb, hb_tok, bank, dstT, dst_tok)
            tr8(hb, hb_tok, bank, dstT, dst_tok)

        xt = [sb(SCR + 0, [128, D], F32), sb(SCR + 4096, [128, D], F32)]
        hbA = [sb(SCR + 8192, [128, D], BF16), sb(SCR + 10240, [128, D], BF16)]
        gvA = sb(SCR + 12288, [128, D], F32)
        sqjA = sb(SCR + 16384, [128, D], F32)
        P.dma("sp", gvA[:], norm_mix_g.partition_broadcast(128), writes=["gv"])
        Vs = sb(G0, [128, 4, NB, 192], BF16)
        Wv = sb(G0 + 24576, [128, 8, 512], BF16)
        oB = SCR + 20480
        wf = sb(oB, [128, 8, 16], BF16)
        wq = [sb(oB + 29760, [128, 8, 128], BF16), sb(oB + 29760 + 2048, [128, 8, 128], BF16)]
        wk = [sb(oB + 33856, [128, 8, 128], BF16), sb(oB + 33856 + 2048, [128, 8, 128], BF16)]
        P.dma("pool", wf[:], wsrc(w_in, O_F, O_F + 16), writes=["wf"])
        P.dma("pool", Wv[:], wsrc(w_in, O_V, O_V + 512), writes=["Wv"])
        P.dma("pool", wq[0][:], wsrc(w_in, O_Q, O_Q + 128), writes=[("wq", 0)])
        P.dma("pool", wk[0][:], wsrc(w_in, O_K, O_K + 128), writes=[("wk", 0)])
        if upto == 'p':
            raise _Stop()
        trA = {}
        for t in range(NB + 1):
            if t >= 1:
                trA[t - 1]()
            if t < NB:
                b = t
                P.dma("sp", xt[b % 2][:], x[b * 128:(b + 1) * 128, :], writes=[("xt", b % 2)])
                trA[b] = norm_block(0, b, xt[b % 2][:], gvA[:], hbA[b % 2][:], sqjA[:], 2 * (b % 2),
                                    hT[:, :, b * 128:(b + 1) * 128], ("xt", b % 2), ("hbA", b % 2), ("hT", b), defer=True)

        if upto == 'A':
            raise _Stop()
        o = SCR + 20480
        o += 256
        Lf = sb(o, [128, NB, 16], F32); o += 1024
        Sacc = sb(o, [128, NB + 1, 16], F32); o += 1088
        Cc = sb(o, [128, NB, 16], F32); o += 1024
        Mm = sb(o, [128, NB, 16], F32); o += 1024
        xf = sb(o, [128, 2, 16], F32); o += 128
        ef = sb(o, [128, 2, 16], F32); o += 128
        o = (o + 63) // 64 * 64
        NPAIR = NB * (NB + 1) // 2
        biasT = sb(o, [128, NPAIR, 16], F32); o += NPAIR * 64
        qT = [sb(o, [128, S], BF16), sb(o + 4096, [128, S], BF16)]; o += 8192
        kT = [sb(o, [128, S], BF16), sb(o + 4096, [128, S], BF16)]; o += 8192
        assert o == oB + 29760, (o, oB)
        o += 8192
        NPT = 8
        Pt = [sb(o + i * 1024, [128, 512], BF16) for i in range(NPT)]; o += NPT * 1024
        rl = [sb(o, [128, 512], F32), sb(o + 2048, [128, 512], F32)]; o += 4096
        bc = [sb(o, [128, 512], F32), sb(o + 2048, [128, 512], F32)]; o += 4096
        assert o <= SCR_END, o

        def pidx(u, kb):
            return u * (u + 1) + kb

        oA = AT
        tri_b = sb(oA, [128, 128], BF16); oA += 256
        sel_b = sb(oA, [128, 128], BF16); oA += 256
        ones_b = sb(oA, [128, 128], BF16); oA += 256
        Lp = [sb(oA + i * 512, [128, NB, 16], BF16) for i in range(3)]; oA += 1536
        Sp = [sb(oA + i * 544, [128, NB + 1, 16], BF16) for i in range(3)]; oA += 1632
        RL = [sb(oA + i * 1024, [128, NB, 16], F32) for i in range(2)]; oA += 2048
        RS = [sb(oA + i * 1088, [128, NB + 1, 16], F32) for i in range(2)]; oA += 2176
        rlp = [[sb(oA + (f_ * 2 + i) * 1024, [128, 512], BF16) for i in range(2)] for f_ in range(2)]; oA += 4096
        rlr = [sb(oA + f_ * 2048, [128, 512], F32) for f_ in range(2)]; oA += 4096
        P.dma("pool", tri_b[:], c_tri, writes=["tri_b"])
        P.dma("pool", sel_b[:], c_sel, writes=["sel_b"])
        P.op("pool", lambda e: e.memset(ones_b[:], 1.0), writes=["ones_b"])
        for i in range(3):
            P.op("pool", lambda e, i=i: e.memset(Sp[i][:, 0, :], 0.0), writes=[("Sp", 0, i)])

        def split(src, src_tok, pieces, tmps, key):
            cur, cur_tok = src, src_tok
            for i, pc in enumerate(pieces):
                P.op("dve", lambda e, pc=pc, cur=cur: e.tensor_copy(out=pc, in_=cur), reads=[cur_tok], writes=[key + (i,)])
                if i < len(pieces) - 1:
                    P.op("dve", lambda e, pc=pc, cur=cur, t=tmps[i]: e.tensor_tensor(out=t, in0=cur, in1=pc, op=ALU.subtract),
                         reads=[cur_tok, key + (i,)], writes=[key + ("r", i)])
                    cur, cur_tok = tmps[i], key + ("r", i)

        if not os.environ.get('KDBG_NOMS'):
            P.op("dve", lambda e: e.memset(Vs[:, :, :, 65:128], 0.0), writes=["Vs_c0"])
            P.op("dve", lambda e: e.memset(Vs[:, :, :, 64:65], 1.0), writes=["Vs_c"])
        P.op("pool", lambda e: e.memset(Sacc[:, 0, :], 0.0), writes=[("Sacc", 0)])

        for b in range(NB):
            for kc in range(8):
                mm(ps[2][:, 0:16], hT[:, kc, b * 128:(b + 1) * 128], wf[:, kc, :], kc == 0, kc == 7,
                   [("hT", b), "wf"], [("ps", 2)])
            P.op("dve", lambda e, b=b: e.tensor_tensor(out=xf[:, b % 2, :], in0=ps[2][:, 0:16], in1=bfb[:], op=ALU.add),
                 reads=[("ps", 2), "bfb"], writes=[("xf", b % 2)])
            P.op("act", lambda e, b=b: e.activation(out=ef[:, b % 2, :], in_=xf[:, b % 2, :], func=AF.Exp, scale=-1.0),
                 reads=[("xf", b % 2)], writes=[("ef", b % 2)])
            P.op("act", lambda e, b=b: e.activation(out=Lf[:, b, :], in_=ef[:, b % 2, :], func=AF.Ln, bias=onesf[:, 0:1]),
                 reads=[("ef", b % 2), "onesf"], writes=[("Lf", b)])
            P.op("dve", lambda e, b=b: e.tensor_tensor(out=Sacc[:, b + 1, :], in0=Sacc[:, b, :], in1=Lf[:, b, :], op=ALU.add),
                 reads=[("Sacc", b), ("Lf", b)], writes=[("Sacc", b + 1)])
            split(Lf[:, b, :], ("Lf", b), [Lp[i][:, b, :] for i in range(3)], [RL[i][:, b, :] for i in range(2)], ("Lp", b))
            split(Sacc[:, b + 1, :], ("Sacc", b + 1), [Sp[i][:, b + 1, :] for i in range(3)],
                  [RS[i][:, b + 1, :] for i in range(2)], ("Sp", b + 1))
        for b in range(NB):
            for (cols, mat, mtok) in ((slice(0, 16), tri_b, "tri_b"), (slice(16, 32), sel_b, "sel_b")):
                for i in range(3):
                    mm(ps[3][:, cols], mat[:], Lp[i][:, b, :], i == 0, False, [mtok, ("Lp", b, i)], [("ps", 3)])
                for i in range(3):
                    mm(ps[3][:, cols], ones_b[:], Sp[i][:, b, :], False, i == 2, ["ones_b", ("Sp", b, i)], [("ps", 3)])
            P.op("dve", lambda e, b=b: e.tensor_copy(out=Cc[:, b, :], in_=ps[3][:, 0:16]), reads=[("ps", 3)], writes=[("Cc", b)])
            P.op("dve", lambda e, b=b: e.tensor_copy(out=Mm[:, b, :], in_=ps[3][:, 16:32]), reads=[("ps", 3)], writes=[("Mm", b)])
        for u in range(NB // 2):
            for kb in range(2 * u + 2):
                P.op("pool", lambda e, u=u, kb=kb: e.tensor_tensor(out=biasT[:, pidx(u, kb), :], in0=Cc[:, kb, :],
                                                                   in1=Mm[:, 2 * u, :], op=ALU.subtract),
                     reads=[("Cc", kb), ("Mm", 2 * u)], writes=[("biasT", u, kb)])

        _KB = int(os.environ.get('KDBG_B', '9'))
        if _KB < 2:
            raise _Stop()
        uid = [0]
        ugc = [0]
        for gh in range(2):
            for b in range(NB):
                bk = int(os.environ.get('KDBG_VB', '4')) + (b % 2)
                for kc in range(8):
                    mm(ps[bk][:], hT[:, kc, b * 128:(b + 1) * 128], Wv[:, kc, :], kc == 0, kc == 7,
                       [("hT", b), "Wv"], [("ps", bk)])
                pv = ps[bk][:].rearrange("p (a n) -> p a n", a=4)
                _vc = int(os.environ.get('KDBG_VC', '2'))
                if _vc >= 1:
                    P.op("act", lambda e, b=b, pv=pv: e.copy(out=Vs[:, :, b, 0:64], in_=pv[:, :, 0:64]),
                         reads=[("ps", bk)], writes=[("Vs", b)])
                if _vc >= 2:
                    P.op("act", lambda e, b=b, pv=pv: e.copy(out=Vs[:, :, b, 128:192], in_=pv[:, :, 64:128]),
                         reads=[("ps", bk)], writes=[("VsB", b)])
            if _KB < 3:
                raise _Stop()
            if gh == 0:
                P.dma("pool", Wv[:], wsrc(w_in, O_V + 512, O_V + 1024), writes=["Wv"])
            for pr in range(4):
                c = gh * 4 + pr
                sl = c % 2
                for j in range(NT):
                    for (wt, dst, tok, bk) in ((wq[sl], qT[sl], "qT", 0), (wk[sl], kT[sl], "kT", 1)):
                        for kc in range(8):
                            mm(ps[bk][:], wt[:, kc, :], hT[:, kc, j * 512:(j + 1) * 512], kc == 0, kc == 7,
                               [("hT", 4 * j), ("hT", 4 * j + 1), ("hT", 4 * j + 2), ("hT", 4 * j + 3),
                                ("wq" if tok == "qT" else "wk", sl)], [("ps", bk)])
                        P.op("dve", lambda e, dst=dst, bk=bk, j=j: e.tensor_copy(out=dst[:, j * 512:(j + 1) * 512], in_=ps[bk][:]),
                             reads=[("ps", bk)], writes=[(tok, sl, j)])
                if _KB < 4:
                    raise _Stop()
                if c + 1 < 8:
                    c1 = c + 1
                    P.dma("pool", wq[c1 % 2][:], wsrc(w_in, O_Q + c1 * 128, O_Q + (c1 + 1) * 128), writes=[("wq", c1 % 2)])
                    P.dma("pool", wk[c1 % 2][:], wsrc(w_in, O_K + c1 * 128, O_K + (c1 + 1) * 128), writes=[("wk", c1 % 2)])
                units = [(hh, j, kb) for hh in range(2) for j in range(NT) for kb in range(4 * j + 4)]
                LA = 4
                tinfo = {}
                pend = []

                def qk_exp(hh, j, kb, ug):
                    h = 2 * c + hh
                    r0 = 64 * hh
                    i = kb - 4 * j
                    q0 = max(0, i) * 128
                    sbk = ug % 5
                    pt = Pt[ug % NPT]
                    ptok = ("Pt", ug % NPT)
                    mm(ps[sbk][:, q0:512], kT[sl][r0:r0 + 64, kb * 128:(kb + 1) * 128],
                       qT[sl][r0:r0 + 64, j * 512 + q0:(j + 1) * 512], True, True,
                       [("kT", sl, kb // 4), ("qT", sl, j)], [("ps", sbk)])
                    for g in range(q0 // 256, 2):
                        u = 2 * j + g
                        ca, cz = max(q0, 256 * g), 256 * (g + 1)
                        P.op("act", lambda e, sbk=sbk, ca=ca, cz=cz, pt=pt, u=u, kb=kb, h=h: e.activation(
                            out=pt[:, ca:cz], in_=ps[sbk][:, ca:cz],
                            func=AF.Exp, bias=biasT[:, pidx(u, kb), h:h + 1], scale=0.125),
                            reads=[("ps", sbk), ("biasT", u, kb)], writes=[ptok])
                    if i >= 0:
                        P.op("dve", lambda e, pt=pt, i=i: e.tensor_tensor(
                            out=pt[:, i * 128:(i + 1) * 128], in0=pt[:, i * 128:(i + 1) * 128],
                            in1=maskb[:], op=ALU.mult), reads=[ptok, "maskb"], writes=[ptok])

                def pv(hh, j, kb, ug, step):
                    r0 = 64 * hh
                    lrow = 64 if hh == 0 else 0
                    vsl = slice(0, 65) if hh == 0 else slice(64, 192)
                    nk = 4 * j + 4
                    if kb == 0:
                        tinfo[(hh, j)] = (5 + (uid[0] % 2), uid[0] % 2)
                        uid[0] += 1
                    ob, fin = tinfo[(hh, j)]
                    q0 = max(0, kb - 4 * j) * 128
                    pt = Pt[ug % NPT]
                    ptok = ("Pt", ug % NPT)
                    mm(ps[ob][0:(65 if hh == 0 else 128), q0:512], Vs[:, pr, kb, vsl], pt[:, q0:512], kb == 0, kb == nk - 1,
                       [("Vs", kb), ("VsB", kb), "Vs_c", "Vs_c0", ptok], [("ps", ob)], skip=True)
                    if kb != nk - 1:
                        return
                    P.op("dve", lambda e: e.reciprocal(out=rl[fin][lrow:lrow + 1, :], in_=ps[ob][lrow:lrow + 1, :]),
                         reads=[("ps", ob)], writes=[("rl", fin)])
                    split(rl[fin][lrow:lrow + 1, :], ("rl", fin), [rlp[fin][i][lrow:lrow + 1, :] for i in range(2)],
                          [rlr[fin][lrow:lrow + 1, :]], ("rlp", fin))

                    def fin2():
                        for i in range(2):
                            mm(ps[7][:], ones_b[lrow:lrow + 1, :], rlp[fin][i][lrow:lrow + 1, :], i == 0, i == 1,
                               ["ones_b", ("rlp", fin, i)], [("ps", 7)])
                        P.op("act", lambda e: e.copy(out=bc[fin][r0:r0 + 64, :], in_=ps[7][r0:r0 + 64, :]),
                             reads=[("ps", 7)], writes=[("bc", fin)])
                        P.op("dve", lambda e, c=c: e.tensor_tensor(
                            out=bT[r0:r0 + 64, c, j * 512:(j + 1) * 512], in0=ps[ob][r0:r0 + 64, :],
                            in1=bc[fin][r0:r0 + 64, :], op=ALU.mult),
                            reads=[("ps", ob), ("bc", fin)], writes=[("bT", c, j, hh)])
                    nxt = 4 * (j + 1) + 4 if j + 1 < NT else (4 if hh == 0 else 0)
                    pend.append((step + min(7, max(nxt, 1)), fin2))

                ug0 = ugc[0]
                GB = 4
                for tb in range(0, len(units) + LA, GB):
                    for t in range(tb, tb + GB):
                        if t < len(units):
                            qk_exp(*units[t], ug0 + t)
                    for t in range(tb, tb + GB):
                        if 0 <= t - LA < len(units):
                            pv(*units[t - LA], ug0 + t - LA, t)
                    while pend and pend[0][0] <= tb + GB - 1:
                        pend.pop(0)[1]()
                while pend:
                    pend.pop(0)[1]()
                ugc[0] += len(units)

        if upto == 'B':
            raise _Stop()
        P.barrier()
        Wuv = sb(G0, [128, 8, 2048], BF16)
        o = SCR
        ub = [sb(o + i * 2048, [128, D], BF16) for i in range(4)]; o += 8192
        v32 = [sb(o, [128, D], F32), sb(o + 4096, [128, D], F32)]; o += 8192
        vln = [sb(o, [128, D], BF16), sb(o + 2048, [128, D], BF16)]; o += 4096
        ab = [sb(o, [128, D], BF16), sb(o + 2048, [128, D], BF16)]; o += 4096
        lng = sb(o, [128, D], F32); o += 4096
        lnb = sb(o, [128, D], F32); o += 4096
        WsT = sb(o, [128, 8, 128], BF16); o += 2048
        sqjC = sb(o, [128, D], F32); o += 4096
        for q in range(4):
            P.dma("pool", Wuv[:, :, q * 512:(q + 1) * 512], wsrc(w_in, q * 512, (q + 1) * 512), writes=[("Wuv", q)])
        P.dma("sp", lng[:], ln_g.partition_broadcast(128), writes=["lng"])
        P.dma("sp", lnb[:], ln_b.partition_broadcast(128), writes=["lnb"])
        P.dma("pool", WsT[:], ws_t, writes=["WsT"])
        P.op("pool", lambda e: e.memset(WsT[64:128, :, 0:64], 0.0), reads=["WsT"], writes=["WsT"])

        def C0_(b):
            for ct in range(4):
                bk = ct
                for kc in range(8):
                    mm(ps[bk][:], hT[:, kc, b * 128:(b + 1) * 128], Wuv[:, kc, ct * 512:(ct + 1) * 512], kc == 0, kc == 7,
                       [("hT", b), ("Wuv", ct)], [("ps", bk)])
                if ct < 2:
                    P.op("act", lambda e, b=b, ct=ct, bk=bk: e.activation(
                        out=ub[b % 4][:, ct * 512:(ct + 1) * 512], in_=ps[bk][:], func=AF.Gelu_apprx_tanh),
                        reads=[("ps", bk)], writes=[("ub", b % 4, ct)])
                else:
                    P.op("act", lambda e, b=b, ct=ct, bk=bk: e.activation(
                        out=v32[b % 2][:, (ct - 2) * 512:(ct - 1) * 512], in_=ps[bk][:], func=AF.Gelu_apprx_tanh,
                        accum_out=lnst[:, b, ct - 2:ct - 1]),
                        reads=[("ps", bk), "lnst"], writes=[("v32", b % 2, ct - 2), ("lnst", b, ct - 2)])

        def C1_(b):
            s = b % 2
            L = lambda i: lnst[:, b, i:i + 1]
            lt = ("lnstb", b)
            P.op("act", lambda e: e.activation(out=sqjC[:], in_=v32[s][:], func=AF.Square, accum_out=L(2)),
                 reads=[("v32", s, 0), ("v32", s, 1), "lnst"], writes=["sqjC", lt])
            P.op("dve", lambda e: e.tensor_tensor(out=L(0), in0=L(0), in1=L(1), op=ALU.add),
                 reads=[("lnst", b, 0), ("lnst", b, 1)], writes=[("lnst", b, 0)])
            P.op("dve", lambda e: e.tensor_scalar(out=L(0), in0=L(0), scalar1=1.0 / D, scalar2=0.0, op0=ALU.mult, op1=ALU.add),
                 reads=[("lnst", b, 0)], writes=[("lnst", b, 0)])
            P.op("dve", lambda e: e.tensor_tensor(out=L(1), in0=L(0), in1=L(0), op=ALU.mult),
                 reads=[("lnst", b, 0)], writes=[("lnst", b, 1)])
            P.op("dve", lambda e: e.scalar_tensor_tensor(out=L(3), in0=L(2), scalar=1.0 / D, in1=L(1),
                                                          op0=ALU.mult, op1=ALU.subtract),
                 reads=[lt, ("lnst", b, 1)], writes=[lt])
            P.op("dve", lambda e: e.tensor_scalar(out=L(3), in0=L(3), scalar1=EPS, scalar2=0.0, op0=ALU.add, op1=ALU.add),
                 reads=[lt], writes=[lt])
            P.op("pool", lambda e: e.tensor_tensor(out=L(4), in0=L(3), in1=nh[:, 0:1], op=ALU.pow),
                 reads=[lt, "nh"], writes=[lt])
            P.op("dve", lambda e: e.scalar_tensor_tensor(out=L(5), in0=L(0), scalar=-1.0, in1=L(4),
                                                          op0=ALU.mult, op1=ALU.mult),
                 reads=[lt, ("lnst", b, 0)], writes=[lt])
            P.op("dve", lambda e: e.tensor_scalar(out=v32[s][:], in0=v32[s][:], scalar1=L(4), scalar2=L(5),
                                                   op0=ALU.mult, op1=ALU.add),
                 reads=[lt, ("v32", s, 0), ("v32", s, 1)], writes=[("v32", s, 0), ("v32", s, 1)])
            P.op("dve", lambda e: e.tensor_tensor(out=v32[s][:], in0=v32[s][:], in1=lng[:], op=ALU.mult),
                 reads=[("v32", s, 0), ("v32", s, 1), "lng"], writes=[("v32", s, 0), ("v32", s, 1)])
            P.op("dve", lambda e: e.tensor_tensor(out=vln[s][:], in0=v32[s][:], in1=lnb[:], op=ALU.add),
                 reads=[("v32", s, 0), ("v32", s, 1), "lnb"], writes=[("vln", s)])

        def C1b_(b):
            s = b % 2
            for g in range(8):
                bk = 4 + g // 4
                mm(ps[bk][:, (g % 4) * 128:(g % 4 + 1) * 128], WsT[:, g, :], vln[s][:, g * 128:(g + 1) * 128], True, True,
                   ["WsT", ("vln", s)], [("ps", bk)])

        def C2_(b):
            s = b % 2
            for g in range(8):
                bk = 4 + g // 4
                P.op("dve", lambda e, g=g, bk=bk: e.scalar_tensor_tensor(
                    out=ab[s][:, g * 128:(g + 1) * 128], in0=ps[bk][:, (g % 4) * 128:(g % 4 + 1) * 128],
                    scalar=bs[:, g:g + 1], in1=ub[b % 4][:, g * 128:(g + 1) * 128], op0=ALU.add, op1=ALU.mult),
                    reads=[("ps", bk), "bs", ("ub", b % 4, g // 4)], writes=[("ab", s)])
            tr8(ab[s], ("ab", s), 6, aT[:, :, b * 128:(b + 1) * 128], ("aT", b))

        for t in range(NB + 3):
            if 0 <= t - 3 < NB:
                C2_(t - 3)
            if 0 <= t - 2 < NB:
                C1b_(t - 2)
            if 0 <= t - 1 < NB:
                C1_(t - 1)
            if t < NB:
                C0_(t)

        if upto == 'C':
            raise _Stop()
        P.barrier()
        dump("hT", hT[:])
        dump("bT", bT[:])
        dump("aT", aT[:])
        o = SCR
        wsm = [[sb(o + (s * 4 + i) * 2048, [128, 8, 128], BF16) for i in range(4)] for s in range(2)]; o += 16384
        gsb = [[sb(o + (s * 2 + i) * 2048, [128, 512], F32) for i in range(2)] for s in range(2)]; o += 8192
        tsb = [[sb(o + (s * 2 + i) * 2048, [128, 512], F32) for i in range(2)] for s in range(2)]; o += 8192
        Wout = sb(o, [128, 8, D], BF16); o += 16384
        oE = o
        P.dma("pool", Wout[:, :, 0:512], wsrc(w_out, 0, 512), writes=[("Wout", 0)])
        P.dma("pool", Wout[:, :, 512:1024], wsrc(w_out, 512, 1024), writes=[("Wout", 1)])
        step = 0

        def loadD(m):
            srcs = (wsrc(w_in, O_G + m * 128, O_G + (m + 1) * 128),
                    wsrc(w_in, O_G + D + m * 128, O_G + D + (m + 1) * 128),
                    wsrc(w_a, m * 128, (m + 1) * 128), wsrc(w_b, m * 128, (m + 1) * 128))
            for i in range(4):
                P.dma("pool", wsm[m % 2][i][:], srcs[i], writes=[("wsm", m % 2, i)])

        loadD(0)
        for m in range(8):
            s = m % 2
            if m + 1 < 8:
                loadD(m + 1)
            for j in range(NT):
                pb = 4 * (step % 2)
                sg = step % 2
                step += 1
                acts = (hT, hT, aT, bT)
                for i in range(4):
                    for kc in range(8):
                        if i < 2:
                            rd = [("hT", 4 * j + u) for u in range(4)]
                        elif i == 2:
                            rd = [("aT", 4 * j + u) for u in range(4)]
                        else:
                            rd = [("bT", kc, j, 0), ("bT", kc, j, 1)]
                        mm(ps[pb + i][:], wsm[s][i][:, kc, :], acts[i][:, kc, j * 512:(j + 1) * 512], kc == 0, kc == 7,
                           rd + [("wsm", s, i)], [("ps", pb + i)])
                for i in range(2):
                    P.op("act", lambda e, i=i, pb=pb, sg=sg: e.activation(out=gsb[sg][i][:], in_=ps[pb + i][:], func=AF.Sigmoid),
                         reads=[("ps", pb + i)], writes=[("gsb", sg, i)])
                for i in range(2):
                    P.op("dve", lambda e, i=i, pb=pb, sg=sg: e.tensor_tensor(out=tsb[sg][i][:], in0=ps[pb + 2 + i][:],
                                                                           in1=gsb[sg][i][:], op=ALU.mult),
                         reads=[("ps", pb + 2 + i), ("gsb", sg, i)], writes=[("tsb", sg, i)])
                P.op("pool", lambda e, sg=sg, m=m, j=j: e.tensor_tensor(out=g0[:, m, j * 512:(j + 1) * 512], in0=tsb[sg][0][:],
                                                                       in1=tsb[sg][1][:], op=ALU.add),
                     reads=[("tsb", sg, 0), ("tsb", sg, 1)], writes=[("mg", m, j)])

        if upto == 'D':
            raise _Stop()
        P.barrier()
        dump("mg", g0[:])
        o = oE
        xr = [sb(o, [128, D], F32), sb(o + 4096, [128, D], F32)]; o += 8192
        hbE = [sb(o, [128, D], BF16), sb(o + 2048, [128, D], BF16)]; o += 4096
        gvE = sb(o, [128, D], F32); o += 4096
        sqjE = sb(o, [128, D], F32); o += 4096
        assert o <= SCR_END
        P.dma("sp", gvE[:], norm_ffn_g.partition_broadcast(128), writes=["gv"])
        trE = {}
        for t in range(NB + 1):
            if t >= 1:
                trE[t - 1]()
            if t >= NB:
                continue
            b = t
            j = b // 4
            P.dma("sp", xr[b % 2][:], x[b * 128:(b + 1) * 128, :], writes=[("xr", b % 2)])
            for ch in range(2):
                bk = 2 * (b % 2) + ch
                for m in range(8):
                    mm(ps[bk][:], g0[:, m, b * 128:(b + 1) * 128], Wout[:, m, ch * 512:(ch + 1) * 512], m == 0, m == 7,
                       [("mg", m, j), ("Wout", ch)], [("ps", bk)])
                P.op("dve", lambda e, b=b, ch=ch, bk=bk: e.tensor_tensor(
                    out=x1[:, b, ch * 512:(ch + 1) * 512], in0=ps[bk][:], in1=xr[b % 2][:, ch * 512:(ch + 1) * 512], op=ALU.add),
                    reads=[("ps", bk), ("xr", b % 2)], writes=[("x1", b, ch)])
            trE[b] = norm_block(1, b, x1[:, b, :], gvE[:], hbE[b % 2][:], sqjE[:], 4 + 2 * (b % 2),
                                h2T[:, :, b * 128:(b + 1) * 128], ("x1", b, 0), ("hbE", b % 2), ("h2T", b),
                                extra_tok=[("x1", b, 1)], defer=True)

        if upto == 'E':
            raise _Stop()
        P.barrier()
        dump("x1", x1[:])
        NG = 11
        actT = sb(SCR, [128, NG, S], BF16)
        Wd = sb(SCR + 45056, [128, NG, D], BF16)
        o = SCR + 67584
        Ag = [sb(o, [128, 512], F32), sb(o + 2048, [128, 512], F32)]; o += 4096
        Al = [sb(o, [128, 512], F32), sb(o + 2048, [128, 512], F32)]; o += 4096
        assert o <= SCR_END
        o = G0
        NW = 3
        wup = [[sb(o + (s * 2 + i) * 2048, [128, 8, 128], BF16) for i in range(2)] for s in range(NW)]; o += NW * 4096
        NR = 3
        Rg = [sb(o + s * 2080, [128, 514], F32) for s in range(NR)]; o += NR * 2080
        Rl = [sb(o + s * 2080, [128, 514], F32) for s in range(NR)]; o += NR * 2080
        Gg = [sb(o, [128, 512], F32), sb(o + 2048, [128, 512], F32)]; o += 4096
        assert o <= G0 + 32 * KB
        rstep = 0
        _KF = int(os.environ.get('KDBG_F', '9'))

        def loadF(fc):
            P.dma("pool", wup[fc % NW][0][:], wsrc(w_up, fc * 128, (fc + 1) * 128), writes=[("wup", fc % NW, 0)])
            P.dma("pool", wup[fc % NW][1][:], wsrc(w_up, DFF + fc * 128, DFF + (fc + 1) * 128), writes=[("wup", fc % NW, 1)])

        loadF(0)
        loadF(1)
        for grp in range(2):
            for q in range(2):
                P.dma("pool", Wd[:, :, q * 512:(q + 1) * 512],
                      w_down[grp * NG * 128:(grp + 1) * NG * 128, q * 512:(q + 1) * 512].rearrange("(c p) n -> p c n", p=128),
                      writes=[("Wd", q)])
            for fl in range(NG):
                fc = grp * NG + fl
                ws_ = fc % NW
                if fc + 2 < NFC:
                    loadF(fc + 2)
                for j in range(NT):
                    rs = rstep % NR
                    rp = (rstep - 1) % NR
                    sa = rstep % 2
                    pb = 2 * (rstep % 2)
                    rstep += 1
                    for i in range(2):
                        for kc in range(8):
                            mm(ps[pb + i][:], wup[ws_][i][:, kc, :], h2T[:, kc, j * 512:(j + 1) * 512], kc == 0, kc == 7,
                               [("h2T", 4 * j + u) for u in range(4)] + [("wup", ws_, i)], [("ps", pb + i)])
                    for i, (R, A, ci) in enumerate(((Rg, Ag, fc), (Rl, Al, NFC + fc))):
                        rt = ("R", i, rs)
                        P.op("act", lambda e, R=R, i=i, pb=pb, rs=rs: e.copy(out=R[rs][:, 2:514], in_=ps[pb + i][:]),
                             reads=[("ps", pb + i)], writes=[rt])
                        rn = (rs + 1) % NR
                        if j == 0:
                            P.op("dve", lambda e, R=R, rs=rs: e.memset(R[rs][:, 0:2], 0.0), writes=[("Rh", i, rs)])
                        if j < NT - 1:
                            P.op("act", lambda e, R=R, i=i, pb=pb, rn=rn: e.copy(out=R[rn][:, 0:2], in_=ps[pb + i][:, 510:512]),
                                 reads=[("ps", pb + i)], writes=[("Rh", i, rn)])
                        at = ("A", i, sa)
                        if _KF < 2:
                            continue
                        if i == 0:
                            P.op("act", lambda e, A=A, pb=pb, sa=sa, ci=ci: e.activation(
                                out=A[sa][:], in_=ps[pb][:], func=AF.Identity, scale=cw[:, ci, 2:3], bias=cb[:, ci:ci + 1]),
                                reads=[("ps", pb), "cw", "cb"], writes=[at])
                        else:
                            P.op("dve", lambda e, A=A, R=R, rs=rs, sa=sa, ci=ci: e.tensor_scalar(
                                out=A[sa][:], in0=R[rs][:, 2:514], scalar1=cw[:, ci, 2:3], scalar2=cb[:, ci:ci + 1],
                                op0=ALU.mult, op1=ALU.add), reads=[rt, "cw", "cb"], writes=[at])
                        _f2 = os.environ.get('KDBG_F2', '')
                        if _f2 == 'a':
                            continue
                        w1s, w0s = (slice(1, 513), slice(0, 512)) if _f2 != 'al' else (slice(2, 514), slice(2, 514))
                        P.op("dve", lambda e, A=A, R=R, rs=rs, sa=sa, ci=ci, w1s=w1s: e.scalar_tensor_tensor(
                            out=A[sa][:], in0=R[rs][:, w1s], scalar=cw[:, ci, 1:2], in1=A[sa][:],
                            op0=ALU.mult, op1=ALU.add), reads=[rt, ("Rh", i, rs), at, "cw"], writes=[at])
                        P.op("dve", lambda e, A=A, R=R, rs=rs, sa=sa, ci=ci, w0s=w0s: e.scalar_tensor_tensor(
                            out=A[sa][:], in0=R[rs][:, w0s], scalar=cw[:, ci, 0:1], in1=A[sa][:],
                            op0=ALU.mult, op1=ALU.add), reads=[rt, ("Rh", i, rs), at, "cw"], writes=[at])
                    if _KF < 3:
                        continue
                    P.op("act", lambda e, sa=sa: e.activation(out=Gg[sa][:], in_=Ag[sa][:], func=AF.Gelu_apprx_tanh),
                         reads=[("A", 0, sa)], writes=[("Gg", sa)])
                    P.op("pool", lambda e, sa=sa, fl=fl, j=j: e.tensor_tensor(
                        out=actT[:, fl, j * 512:(j + 1) * 512], in0=Gg[sa][:], in1=Al[sa][:], op=ALU.mult),
                        reads=[("Gg", sa), ("A", 1, sa)], writes=[("actT", fl, j)])
            for b in range(NB if _KF >= 4 else 0):
                j = b // 4
                for ch in range(2):
                    bk = 4 + 2 * (b % 2) + ch
                    for fl in range(NG):
                        mm(ps[bk][:], actT[:, fl, b * 128:(b + 1) * 128], Wd[:, fl, ch * 512:(ch + 1) * 512],
                           fl == 0, fl == NG - 1, [("actT", fl, j), ("Wd", ch)], [("ps", bk)])
                    P.op("dve", lambda e, b=b, ch=ch, bk=bk: e.tensor_tensor(
                        out=x1[:, b, ch * 512:(ch + 1) * 512], in0=ps[bk][:], in1=x1[:, b, ch * 512:(ch + 1) * 512], op=ALU.add),
                        reads=[("ps", bk), ("x1", b, ch)], writes=[("x1", b, ch)])

        if upto == 'F':
            raise _Stop()
        P.barrier()
        dump("x2", x1[:])
        o = SCR
        Wg = sb(o, [128, 8, D], BF16); o += 16384
        Wp = sb(o, [128, 2, D], BF16); o += 4096
        pst = [sb(o, [128, PLE], F32), sb(o + 1024, [128, PLE], F32)]; o += 2048
        pb16 = [sb(o, [128, PLE], BF16), sb(o + 512, [128, PLE], BF16)]; o += 1024
        pTb = [sb(o, [128, 2, 128], BF16), sb(o + 512, [128, 2, 128], BF16)]; o += 1024
        hbG = [sb(o, [128, D], BF16), sb(o + 2048, [128, D], BF16)]; o += 4096
        h3Tb = [sb(o, [128, 8, 128], BF16), sb(o + 2048, [128, 8, 128], BF16)]; o += 4096
        sgt = [[sb(o + (s * 2 + i) * 2048, [128, 512], F32) for i in range(2)] for s in range(2)]; o += 8192
        tt = [sb(o, [128, D], F32), sb(o + 4096, [128, D], F32)]; o += 8192
        ot = [sb(o, [128, D], F32), sb(o + 4096, [128, D], F32)]; o += 8192
        gv3 = sb(o, [128, D], F32); o += 4096
        gvf = sb(o, [128, D], F32); o += 4096
        sqjG = sb(o, [128, D], F32); o += 4096
        assert o <= SCR_END
        P.dma("pool", Wg[:, :, 0:512], wsrc(w_pg, 0, 512), writes=[("Wg", 0)])
        P.dma("pool", Wg[:, :, 512:1024], wsrc(w_pg, 512, 1024), writes=[("Wg", 1)])
        P.dma("pool", Wp[:], wsrc(w_ple, 0, D), writes=["Wp"])
        P.dma("sp", gv3[:], norm_ple_g.partition_broadcast(128), writes=["gv"])
        P.dma("sp", gvf[:], norm_final_g.partition_broadcast(128), writes=["gvf"])

        trG = {}

        def G0_(b):
            s = b % 2
            P.dma("sp", pst[s][:], pin[b * 128:(b + 1) * 128, :], writes=[("pst", s)])
            P.op("act", lambda e: e.copy(out=pb16[s][:], in_=pst[s][:]), reads=[("pst", s)], writes=[("pb16", s)])
            trG[b] = norm_block(2, b, x1[:, b, :], gv3[:], hbG[s][:], sqjG[:], 0, h3Tb[s][:],
                                ("x1", b, 0), ("hbG", s), ("h3Tb", s), extra_tok=[("x1", b, 1)], defer=True)

        def G0b_(b):
            s = b % 2
            tr8(pb16[s], ("pb16", s), 4, pTb[s][:], ("pTb", s), nch=2)
            trG[b]()

        def G1_(b):
            s = b % 2
            for ch in range(2):
                bk = 2 + ch
                for kc in range(8):
                    mm(ps[bk][:], h3Tb[s][:, kc, :], Wg[:, kc, ch * 512:(ch + 1) * 512], kc == 0, kc == 7,
                       [("h3Tb", s), ("Wg", ch)], [("ps", bk)])
                P.op("act", lambda e, ch=ch, bk=bk: e.activation(out=sgt[s][ch][:], in_=ps[bk][:], func=AF.Sigmoid),
                     reads=[("ps", bk)], writes=[("sgt", s, ch)])
                bk2 = 6 + ch
                for kc in range(2):
                    mm(ps[bk2][:], pTb[s][:, kc, :], Wp[:, kc, ch * 512:(ch + 1) * 512], kc == 0, kc == 1,
                       [("pTb", s), "Wp"], [("ps", bk2)])
                P.op("dve", lambda e, ch=ch, bk2=bk2: e.tensor_tensor(
                    out=tt[s][:, ch * 512:(ch + 1) * 512], in0=ps[bk2][:], in1=sgt[s][ch][:], op=ALU.mult),
                    reads=[("ps", bk2), ("sgt", s, ch)], writes=[("tt", s, ch)])
                P.op("dve", lambda e, ch=ch: e.tensor_tensor(
                    out=tt[s][:, ch * 512:(ch + 1) * 512], in0=tt[s][:, ch * 512:(ch + 1) * 512],
                    in1=x1[:, b, ch * 512:(ch + 1) * 512], op=ALU.add),
                    reads=[("tt", s, ch), ("x1", b, ch)], writes=[("tt", s, ch)])
            ssc = stat[:, 3, 0, b:b + 1]
            msc = stat[:, 3, 1, b:b + 1]
            rsc = stat[:, 3, 2, b:b + 1]
            st = ("stat", 3, b)
            P.op("act", lambda e: e.activation(out=sqjG[:], in_=tt[s][:], func=AF.Square, accum_out=ssc),
                 reads=[("tt", s, 0), ("tt", s, 1), "stat"], writes=["sqj", st])
            P.op("dve", lambda e: e.tensor_scalar(out=msc, in0=ssc, scalar1=1.0 / D, scalar2=EPS, op0=ALU.mult, op1=ALU.add),
                 reads=[st], writes=[st])
            P.op("pool", lambda e: e.tensor_tensor(out=rsc, in0=msc, in1=nh[:, 0:1], op=ALU.pow), reads=[st, "nh"], writes=[st])
            P.op("dve", lambda e: e.scalar_tensor_tensor(out=ot[s][:], in0=tt[s][:], scalar=rsc, in1=gvf[:],
                                                          op0=ALU.mult, op1=ALU.mult),
                 reads=[("tt", s, 0), ("tt", s, 1), st, "gvf"], writes=[("ot", s)])
            P.dma("sp", out[b * 128:(b + 1) * 128, :], ot[s][:], reads=[("ot", s)], writes=[("out", b)])

        for t in range(NB + 2):
            if 0 <= t - 2 < NB:
                G1_(t - 2)
            if 0 <= t - 1 < NB:
                G0b_(t - 1)
            if t < NB:
                G0_(t)
        P.op("sp", lambda e: e.nop(), reads=[("out", b) for b in range(NB)])
        P.barrier()

    except _Stop:
        P.barrier()

    emit(nc, P)
    es.close()
    return nc


_CACHE = {}


def _prep_shared(inp):
    f = np.float32
    d = {}
    d["norm_mix_g"] = np.ascontiguousarray(inp["norm_mix_g"][0], f)
    d["w_in"] = np.ascontiguousarray(inp["w_in"][0], f)
    d["b_f"] = np.ascontiguousarray(inp["b_f"][0], f)
    d["gmlp_ln_g"] = np.ascontiguousarray(inp["gmlp_ln_g"][0], f)
    d["gmlp_ln_b"] = np.ascontiguousarray(inp["gmlp_ln_b"][0], f)
    d["ws_t"] = np.ascontiguousarray(np.transpose(inp["gmlp_w_s"][0], (2, 0, 1)), f)
    d["bs_t"] = np.ascontiguousarray(inp["gmlp_b_s"][0].T, f)
    d["w_branch_a"] = np.ascontiguousarray(inp["w_branch_a"][0], f)
    d["w_branch_b"] = np.ascontiguousarray(inp["w_branch_b"][0], f)
    d["w_out"] = np.ascontiguousarray(inp["w_out"][0], f)
    d["norm_ffn_g"] = np.ascontiguousarray(inp["norm_ffn_g"][0], f)
    d["w_up"] = np.ascontiguousarray(inp["w_up"][0], f)
    cwv = np.asarray(inp["conv_w"][0], f)
    d["cw_t"] = np.ascontiguousarray(cwv.T.reshape(44, 128, 3).transpose(1, 0, 2), f)
    d["cb_t"] = np.ascontiguousarray(np.asarray(inp["conv_b"][0], f).reshape(44, 128).T, f)
    d["w_down"] = np.ascontiguousarray(inp["w_down"][0], f)
    d["norm_ple_g"] = np.ascontiguousarray(inp["norm_ple_g"][0], f)
    d["w_ple"] = np.ascontiguousarray(inp["w_ple"][0], f)
    d["w_ple_gate"] = np.ascontiguousarray(inp["w_ple_gate"][0], f)
    d["norm_final_g"] = np.ascontiguousarray(inp["norm_final_g"], f)
    d["c_ident"] = np.eye(128, dtype=f)
    r = np.arange(128)
    d["c_tri"] = (r[:, None] <= r[None, :]).astype(f)
    d["c_sel"] = np.ones((128, 128), f)
    return d


def kernel(**inputs):
    inp = {k: np.asarray(v) for k, v in inputs.items()}
    n = 8
    if "nc" not in _CACHE:
        _CACHE["nc"] = build_nc()
    nc = _CACHE["nc"]
    shared = _prep_shared(inp)
    x = np.asarray(inp["x"], np.float32)
    p = np.asarray(inp["p"], np.float32)[0]
    in_maps = []
    for i in range(n):
        m = dict(shared)
        m["x"] = np.ascontiguousarray(x[i])
        m["p"] = np.ascontiguousarray(p[i])
        in_maps.append(m)
    res = run_bass_kernel_spmd(nc, in_maps, core_ids=list(range(n)))
    return np.stack([np.asarray(r["out"], np.float32) for r in res.results], axis=0)
```

```python
import contextlib
import numpy as np
import concourse.bass as bass
import concourse.mybir as mybir
from concourse.bass_utils import run_bass_kernel_spmd

F32 = mybir.dt.float32
BF16 = mybir.dt.bfloat16
AF = mybir.ActivationFunctionType
ALU = mybir.AluOpType

ENGS = ("pe", "act", "dve", "pool", "sp")

S = 2048
D = 1024
NB = 16
NT = 4
DFF = 2816
NFC = 22
IN_COLS = 7184
O_Q = 2048
O_K = 3072
O_V = 4096
O_F = 5120
O_G = 5136
PLE = 256
EPS = 1e-6
KB = 1024
BASE = 16512
import os
_STG = int(os.environ.get('KDBG_STG', '6'))
_NBA = int(os.environ.get('KDBG_NBA', '16'))


class _Op:
    __slots__ = ("fn", "deps", "dma", "flag", "semval", "dsem", "dcount", "waits", "prewait")

    def __init__(self, fn, deps, dma):
        self.fn = fn
        self.deps = deps
        self.dma = dma
        self.flag = False
        self.semval = None
        self.dsem = None
        self.dcount = None
        self.waits = []
        self.prewait = None


class Prog:
    def __init__(self, nc, n_dma_sems=32):
        self.nc = nc
        self.streams = {e: [] for e in ENGS}
        self.last_writer = {}
        self.readers = {}
        self.n_dma_sems = n_dma_sems

    def op(self, eng, fn, reads=(), writes=(), dma=False, extra=None):
        st = self.streams[eng]
        me = (eng, len(st))
        deps = {}
        for t in reads:
            w = self.last_writer.get(t)
            if w is not None:
                deps[w] = True
        for t in writes:
            w = self.last_writer.get(t)
            if w is not None and w not in deps:
                deps[w] = False
            for r in self.readers.get(t, ()):
                if r not in deps:
                    deps[r] = False
        if extra:
            for d in extra:
                deps[d] = True
        deps.pop(me, None)
        for t in writes:
            self.last_writer[t] = me
            self.readers[t] = []
        for t in reads:
            if self.last_writer.get(t) != me:
                self.readers.setdefault(t, []).append(me)
        st.append(_Op(fn, deps, dma))
        return me

    def dma(self, eng, out, in_, reads=(), writes=()):
        return self.op(eng, lambda e: e.dma_start(out=out, in_=in_), reads, writes, dma=True)

    def barrier(self):
        lasts = []
        for e in ENGS:
            st = self.streams[e]
            for i in range(len(st) - 1, -1, -1):
                if not st[i].dma:
                    lasts.append((e, i))
                    break
            n = 0
            for i in range(len(st) - 1, -1, -1):
                if st[i].dma:
                    lasts.append((e, i))
                    n += 1
                    if n >= 16:
                        break
        for e in ENGS:
            self.op(e, lambda g: g.nop(), extra=[d for d in lasts])

    def resolve(self):
        streams = self.streams
        dma_engs = [e for e in ENGS if any(o.dma for o in streams[e])]
        pools = {}
        if dma_engs:
            per = max(2, self.n_dma_sems // len(dma_engs))
            for e in dma_engs:
                pools[e] = min(per, 16)
        self.dma_pool_sizes = pools
        for e in dma_engs:
            k = 0
            counts = [0] * pools[e]
            lastop = [None] * pools[e]
            for i, o in enumerate(streams[e]):
                if not o.dma:
                    continue
                s = k % pools[e]
                k += 1
                if lastop[s] is not None:
                    o.prewait = ((e, s), counts[s])
                counts[s] += 16
                o.dsem = (e, s)
                o.dcount = counts[s]
                lastop[s] = i
        for e in ENGS:
            seen = {f: -1 for f in ENGS}
            seen_d = {}
            for i, o in enumerate(streams[e]):
                if o.prewait is not None:
                    key, cnt = o.prewait
                    if seen_d.get(key, 0) >= cnt:
                        o.prewait = None
                    else:
                        seen_d[key] = cnt
                for (f, n), is_raw in sorted(o.deps.items(), key=lambda kv: (kv[0][0], -kv[0][1])):
                    p = streams[f][n]
                    if p.dma:
                        if seen_d.get(p.dsem, 0) >= p.dcount:
                            continue
                        seen_d[p.dsem] = p.dcount
                        o.waits.append(("d", p.dsem, p.dcount))
                        continue
                    if f == e and not o.dma:
                        if not (is_raw and e != "pe"):
                            continue
                    if n <= seen[f]:
                        continue
                    seen[f] = n
                    p.flag = True
                    o.waits.append(("c", f, n))
        for e in ENGS:
            c = 0
            for o in streams[e]:
                if o.flag:
                    c += 1
                    o.semval = c


def emit(nc, prog):
    prog.resolve()
    with contextlib.ExitStack() as es:
        csem = {e: es.enter_context(nc.semaphore("c_" + e)) for e in ENGS}
        dsem = {}
        for e, n in prog.dma_pool_sizes.items():
            for s in range(n):
                dsem[(e, s)] = es.enter_context(nc.semaphore("d_%s_%d" % (e, s)))
        block = es.enter_context(nc.Block())

        def run(ename, eng):
            for o in prog.streams[ename]:
                if o.prewait is not None:
                    key, cnt = o.prewait
                    eng.wait_ge(dsem[key], cnt)
                for w in o.waits:
                    if w[0] == "d":
                        eng.wait_ge(dsem[w[1]], w[2])
                    else:
                        eng.wait_ge(csem[w[1]], prog.streams[w[1]][w[2]].semval)
                ins = o.fn(eng)
                if o.dma:
                    ins.then_inc(dsem[o.dsem], 16)
                elif o.flag:
                    ins.then_inc(csem[ename], 1)

        @block.tensor
        def _(eng):
            run("pe", eng)

        @block.scalar
        def _(eng):
            run("act", eng)

        @block.vector
        def _(eng):
            run("dve", eng)

        @block.gpsimd
        def _(eng):
            run("pool", eng)

        @block.sync
        def _(eng):
            run("sp", eng)


class _Stop(Exception):
    pass


def build_nc(debug=False, upto=None):
    nc = bass.Bass("TRN2", target_bir_lowering=False)

    def din(name, shape):
        return nc.dram_tensor(name, list(shape), F32, kind="ExternalInput").ap()

    x = din("x", [S, D])
    pin = din("p", [S, PLE])
    norm_mix_g = din("norm_mix_g", [D])
    w_in = din("w_in", [D, IN_COLS])
    b_f = din("b_f", [16])
    ln_g = din("gmlp_ln_g", [D])
    ln_b = din("gmlp_ln_b", [D])
    ws_t = din("ws_t", [128, 8, 128])
    bs_t = din("bs_t", [128, 8])
    w_a = din("w_branch_a", [D, D])
    w_b = din("w_branch_b", [D, D])
    w_out = din("w_out", [D, D])
    norm_ffn_g = din("norm_ffn_g", [D])
    w_up = din("w_up", [D, 2 * DFF])
    cw_t = din("cw_t", [128, 44, 3])
    cb_t = din("cb_t", [128, 44])
    w_down = din("w_down", [DFF, D])
    norm_ple_g = din("norm_ple_g", [D])
    w_ple = din("w_ple", [PLE, D])
    w_pg = din("w_ple_gate", [D, D])
    norm_final_g = din("norm_final_g", [D])
    c_ident = din("c_ident", [128, 128])
    c_tri = din("c_tri", [128, 128])
    c_sel = din("c_sel", [128, 128])
    out = nc.dram_tensor("out", [S, D], F32, kind="ExternalOutput").ap()
    dbg = {}
    if debug:
        for nm in ("hT", "bT", "aT", "mg"):
            dbg[nm] = nc.dram_tensor("dbg_" + nm, [128, 8, S], BF16, kind="ExternalOutput").ap()
        for nm in ("x1", "x2"):
            dbg[nm] = nc.dram_tensor("dbg_" + nm, [128, NB, D], F32, kind="ExternalOutput").ap()

    def dump(nm, src):
        if debug:
            P.dma("sp", dbg[nm], src)
            P.barrier()

    cnt = [0]

    def sb(off, shape, dt):
        cnt[0] += 1
        return nc.alloc_sbuf_tensor_at("t%d" % cnt[0], list(shape), dt, offset=off)

    C0 = BASE
    ident = sb(C0 + 0, [128, 128], BF16)
    onesf = sb(C0 + 256, [128, 128], F32)
    trif = sb(C0 + 768, [128, 128], F32)
    self_ = sb(C0 + 1280, [128, 128], F32)
    maskb = sb(C0 + 1792, [128, 128], BF16)
    cw = sb(C0 + 2048, [128, 44, 3], F32)
    cb = sb(C0 + 2592, [128, 44], F32)
    bs = sb(C0 + 2784, [128, 8], F32)
    bfb = sb(C0 + 2816, [128, 16], F32)
    nh = sb(C0 + 2880, [128, 16], F32)
    stat = sb(C0 + 2944, [128, 4, 3, 16], F32)
    lnst = sb(C0 + 3712, [128, 16, 6], F32)
    G0 = C0 + 4 * KB
    HT = C0 + 36 * KB
    BT = C0 + 68 * KB
    AT = C0 + 100 * KB
    SCR = C0 + 132 * KB
    SCR_END = 229344

    g0 = sb(G0, [128, 8, S], BF16)
    hT = sb(HT, [128, 8, S], BF16)
    bT = sb(BT, [128, 8, S], BF16)
    aT = sb(AT, [128, 8, S], BF16)
    x1 = sb(HT, [128, NB, D], F32)
    h2T = sb(AT, [128, 8, S], BF16)

    es = contextlib.ExitStack()
    ps = [es.enter_context(nc.psum_tensor("ps%d" % i, [128, 512], F32)) for i in range(8)]

    def psb(i):
        return ps[i][:].bitcast(BF16).rearrange("p (c n) -> p c n", c=8)

    P = Prog(nc)

    def mm(o, lhsT, rhs, start, stop, reads, writes, skip=False):
        if skip:
            P.op("pe", lambda e: e.matmul(o, lhsT=lhsT, rhs=rhs, start=start, stop=stop,
                                           skip_group_check=True), reads, writes)
        else:
            P.op("pe", lambda e: e.matmul(o, lhsT=lhsT, rhs=rhs, start=start, stop=stop), reads, writes)

    def wsrc(w, c0, c1):
        return w[:, c0:c1].rearrange("(c p) n -> p c n", p=128)

    try:
        P.dma("pool", ident[:], c_ident, writes=["ident"])
        P.dma("sp", trif[:], c_tri, writes=["trif"])
        P.dma("sp", self_[:], c_sel, writes=["self"])
        P.dma("pool", maskb[:], c_tri, writes=["maskb"])
        P.dma("sp", cw[:], cw_t, writes=["cw"])
        P.dma("sp", cb[:], cb_t, writes=["cb"])
        P.dma("sp", bs[:], bs_t, writes=["bs"])
        P.dma("sp", bfb[:], b_f.partition_broadcast(128), writes=["bfb"])
        P.op("pool", lambda e: e.memset(onesf[:], 1.0), writes=["onesf"])
        P.op("pool", lambda e: e.memset(nh[:], -0.5), writes=["nh"])
        P.op("pool", lambda e: e.memset(stat[:], 0.0), writes=["stat"])
        P.op("pool", lambda e: e.memset(lnst[:], 0.0), writes=["lnst"])

        if upto == 'c':
            raise _Stop()
        def tr8(src, src_tok, bank, dstT, dst_tok, nch=8):
            for c in range(nch):
                bk = bank + c // 4
                P.op("pe", lambda e, c=c, bk=bk: e.matmul(ps[bk][:, (c % 4) * 128:(c % 4 + 1) * 128],
                                                          lhsT=src[:, c * 128:(c + 1) * 128], rhs=ident[:],
                                                          start=True, stop=True),
                     reads=[src_tok, "ident"], writes=[("ps", bk)])
            for hf in range((nch + 3) // 4):
                n = min(4, nch - 4 * hf)
                pvv = ps[bank + hf][:, 0:n * 128].rearrange("p (c n) -> p c n", c=n)
                P.op("act", lambda e, hf=hf, n=n, pvv=pvv: e.copy(out=dstT[:, 4 * hf:4 * hf + n, :], in_=pvv),
                     reads=[("ps", bank + hf)], writes=[dst_tok])

        def norm_block(ni, b, src, gv, hb, sqj, bank, dstT, src_tok, hb_tok, dst_tok, extra_tok=(), defer=False):
            ssc = stat[:, ni, 0, b:b + 1]
            msc = stat[:, ni, 1, b:b + 1]
            rsc = stat[:, ni, 2, b:b + 1]
            st = ("stat", ni, b)
            P.op("act", lambda e: e.activation(out=hb, in_=src, func=AF.Square, accum_out=ssc),
                 reads=[src_tok, "stat"] + list(extra_tok), writes=[hb_tok, st])
            if _STG < 2:
                return
            P.op("dve", lambda e: e.tensor_scalar(out=msc, in0=ssc, scalar1=1.0 / D, scalar2=EPS,
                                                   op0=ALU.mult, op1=ALU.add), reads=[st], writes=[st])
            if _STG < 3:
                return
            P.op("pool", lambda e: e.tensor_tensor(out=rsc, in0=msc, in1=nh[:, 0:1], op=ALU.pow),
                 reads=[st, "nh"], writes=[st])
            if _STG < 4:
                return
            P.op("dve", lambda e: e.scalar_tensor_tensor(out=hb, in0=src, scalar=rsc, in1=gv,
                                                          op0=ALU.mult, op1=ALU.mult),
                 reads=[src_tok, st, "gv"], writes=[hb_tok])
            if _STG < 5:
                return
            if defer:
                return lambda: tr8(hb, hb_tok, bank, dstT, dst_tok)
            tr8(hb, hb_tok, bank, dstT, dst_tok)

        xt = [sb(BT + i * 4096, [128, D], F32) for i in range(4)]
        hbA = [sb(BT + 16384 + i * 2048, [128, D], BF16) for i in range(4)]
        gvA = sb(BT + 24576, [128, D], F32)
        sqjA = sb(BT + 28672, [128, D], F32)
        P.dma("sp", gvA[:], norm_mix_g.partition_broadcast(128), writes=["gv"])
        Vs = sb(G0, [128, 4, NB, 192], BF16)
        Wv = sb(G0 + 24576, [128, 8, 512], BF16)
        oB = SCR + 20480
        wf = sb(oB, [128, 8, 16], BF16)
        wq = [sb(oB + 29760, [128, 8, 128], BF16), sb(oB + 29760 + 2048, [128, 8, 128], BF16)]
        wk = [sb(oB + 33856, [128, 8, 128], BF16), sb(oB + 33856 + 2048, [128, 8, 128], BF16)]
        P.dma("pool", wf[:], wsrc(w_in, O_F, O_F + 16), writes=["wf"])
        P.dma("pool", Wv[:], wsrc(w_in, O_V, O_V + 512), writes=["Wv"])
        P.dma("pool", wq[0][:], wsrc(w_in, O_Q, O_Q + 128), writes=[("wq", 0)])
        P.dma("pool", wk[0][:], wsrc(w_in, O_K, O_K + 128), writes=[("wk", 0)])
        if upto == 'p':
            raise _Stop()
        trA = {}
        for t in range(NB + 2):
            if 0 <= t - 2 < NB:
                trA[t - 2]()
            if t < NB:
                b = t
                P.dma("sp", xt[b % 4][:], x[b * 128:(b + 1) * 128, :], writes=[("xt", b % 4)])
                trA[b] = norm_block(0, b, xt[b % 4][:], gvA[:], hbA[b % 4][:], sqjA[:], 2 * (b % 4),
                                    hT[:, :, b * 128:(b + 1) * 128], ("xt", b % 4), ("hbA", b % 4), ("hT", b), defer=True)

        if upto == 'A':
            raise _Stop()
        o = SCR + 20480
        o += 256
        Lf = sb(o, [128, NB, 16], F32); o += 1024
        Sacc = sb(o, [128, NB + 1, 16], F32); o += 1088
        Cc = sb(o, [128, NB, 16], F32); o += 1024
        Mm = sb(o, [128, NB, 16], F32); o += 1024
        xf = sb(o, [128, 2, 16], F32); o += 128
        ef = sb(o, [128, 2, 16], F32); o += 128
        o = (o + 63) // 64 * 64
        NPAIR = NB * (NB + 1) // 2
        biasT = sb(o, [128, NPAIR, 16], F32); o += NPAIR * 64
        qT = [sb(o, [128, S], BF16), sb(o + 4096, [128, S], BF16)]; o += 8192
        kT = [sb(o, [128, S], BF16), sb(o + 4096, [128, S], BF16)]; o += 8192
        assert o == oB + 29760, (o, oB)
        o += 8192
        NPT = 8
        Pt = [sb(o + i * 1024, [128, 512], BF16) for i in range(NPT)]; o += NPT * 1024
        rl = [sb(o, [128, 512], F32), sb(o + 2048, [128, 512], F32)]; o += 4096
        bc = [sb(o, [128, 512], F32), sb(o + 2048, [128, 512], F32)]; o += 4096
        assert o <= SCR_END, o

        def pidx(u, kb):
            return u * (u + 1) + kb

        oA = AT
        tri_b = sb(oA, [128, 128], BF16); oA += 256
        sel_b = sb(oA, [128, 128], BF16); oA += 256
        ones_b = sb(oA, [128, 128], BF16); oA += 256
        Lp = [sb(oA + i * 512, [128, NB, 16], BF16) for i in range(3)]; oA += 1536
        Sp = [sb(oA + i * 544, [128, NB + 1, 16], BF16) for i in range(3)]; oA += 1632
        RL = [sb(oA + i * 1024, [128, NB, 16], F32) for i in range(2)]; oA += 2048
        RS = [sb(oA + i * 1088, [128, NB + 1, 16], F32) for i in range(2)]; oA += 2176
        rlp = [[sb(oA + (f_ * 2 + i) * 1024, [128, 512], BF16) for i in range(2)] for f_ in range(2)]; oA += 4096
        rlr = [sb(oA + f_ * 2048, [128, 512], F32) for f_ in range(2)]; oA += 4096
        P.dma("pool", tri_b[:], c_tri, writes=["tri_b"])
        P.dma("pool", sel_b[:], c_sel, writes=["sel_b"])
        P.op("pool", lambda e: e.memset(ones_b[:], 1.0), writes=["ones_b"])
        for i in range(3):
            P.op("pool", lambda e, i=i: e.memset(Sp[i][:, 0, :], 0.0), writes=[("Sp", 0, i)])

        def split(src, src_tok, pieces, tmps, key):
            cur, cur_tok = src, src_tok
            for i, pc in enumerate(pieces):
                P.op("dve", lambda e, pc=pc, cur=cur: e.tensor_copy(out=pc, in_=cur), reads=[cur_tok], writes=[key + (i,)])
                if i < len(pieces) - 1:
                    P.op("dve", lambda e, pc=pc, cur=cur, t=tmps[i]: e.tensor_tensor(out=t, in0=cur, in1=pc, op=ALU.subtract),
                         reads=[cur_tok, key + (i,)], writes=[key + ("r", i)])
                    cur, cur_tok = tmps[i], key + ("r", i)

        if not os.environ.get('KDBG_NOMS'):
            P.op("dve", lambda e: e.memset(Vs[:, :, :, 65:128], 0.0), writes=["Vs_c0"])
            P.op("dve", lambda e: e.memset(Vs[:, :, :, 64:65], 1.0), writes=["Vs_c"])
        P.op("pool", lambda e: e.memset(Sacc[:, 0, :], 0.0), writes=[("Sacc", 0)])

        for b in range(NB):
            for kc in range(8):
                mm(ps[2][:, 0:16], hT[:, kc, b * 128:(b + 1) * 128], wf[:, kc, :], kc == 0, kc == 7,
                   [("hT", b), "wf"], [("ps", 2)])
            P.op("dve", lambda e, b=b: e.tensor_tensor(out=xf[:, b % 2, :], in0=ps[2][:, 0:16], in1=bfb[:], op=ALU.add),
                 reads=[("ps", 2), "bfb"], writes=[("xf", b % 2)])
            P.op("act", lambda e, b=b: e.activation(out=ef[:, b % 2, :], in_=xf[:, b % 2, :], func=AF.Exp, scale=-1.0),
                 reads=[("xf", b % 2)], writes=[("ef", b % 2)])
            P.op("act", lambda e, b=b: e.activation(out=Lf[:, b, :], in_=ef[:, b % 2, :], func=AF.Ln, bias=onesf[:, 0:1]),
                 reads=[("ef", b % 2), "onesf"], writes=[("Lf", b)])
            P.op("dve", lambda e, b=b: e.tensor_tensor(out=Sacc[:, b + 1, :], in0=Sacc[:, b, :], in1=Lf[:, b, :], op=ALU.add),
                 reads=[("Sacc", b), ("Lf", b)], writes=[("Sacc", b + 1)])
            split(Lf[:, b, :], ("Lf", b), [Lp[i][:, b, :] for i in range(3)], [RL[i][:, b, :] for i in range(2)], ("Lp", b))
            split(Sacc[:, b + 1, :], ("Sacc", b + 1), [Sp[i][:, b + 1, :] for i in range(3)],
                  [RS[i][:, b + 1, :] for i in range(2)], ("Sp", b + 1))
        for b in range(NB):
            for (cols, mat, mtok) in ((slice(0, 16), tri_b, "tri_b"), (slice(16, 32), sel_b, "sel_b")):
                for i in range(3):
                    mm(ps[3][:, cols], mat[:], Lp[i][:, b, :], i == 0, False, [mtok, ("Lp", b, i)], [("ps", 3)])
                for i in range(3):
                    mm(ps[3][:, cols], ones_b[:], Sp[i][:, b, :], False, i == 2, ["ones_b", ("Sp", b, i)], [("ps", 3)])
            P.op("dve", lambda e, b=b: e.tensor_copy(out=Cc[:, b, :], in_=ps[3][:, 0:16]), reads=[("ps", 3)], writes=[("Cc", b)])
            P.op("dve", lambda e, b=b: e.tensor_copy(out=Mm[:, b, :], in_=ps[3][:, 16:32]), reads=[("ps", 3)], writes=[("Mm", b)])
        for u in range(NB // 2):
            for kb in range(2 * u + 2):
                P.op("pool", lambda e, u=u, kb=kb: e.tensor_tensor(out=biasT[:, pidx(u, kb), :], in0=Cc[:, kb, :],
                                                                   in1=Mm[:, 2 * u, :], op=ALU.subtract),
                     reads=[("Cc", kb), ("Mm", 2 * u)], writes=[("biasT", u, kb)])

        _KB = int(os.environ.get('KDBG_B', '9'))
        if _KB < 2:
            raise _Stop()
        uid = [0]
        ugc = [0]
        for gh in range(2):
            for b in range(NB):
                bk = int(os.environ.get('KDBG_VB', '4')) + (b % 2)
                for kc in range(8):
                    mm(ps[bk][:], hT[:, kc, b * 128:(b + 1) * 128], Wv[:, kc, :], kc == 0, kc == 7,
                       [("hT", b), "Wv"], [("ps", bk)])
                pv = ps[bk][:].rearrange("p (a n) -> p a n", a=4)
                _vc = int(os.environ.get('KDBG_VC', '2'))
                if _vc >= 1:
                    P.op("act", lambda e, b=b, pv=pv: e.copy(out=Vs[:, :, b, 0:64], in_=pv[:, :, 0:64]),
                         reads=[("ps", bk)], writes=[("Vs", b)])
                if _vc >= 2:
                    P.op("act", lambda e, b=b, pv=pv: e.copy(out=Vs[:, :, b, 128:192], in_=pv[:, :, 64:128]),
                         reads=[("ps", bk)], writes=[("VsB", b)])
            if _KB < 3:
                raise _Stop()
            if gh == 0:
                P.dma("pool", Wv[:], wsrc(w_in, O_V + 512, O_V + 1024), writes=["Wv"])
            for pr in range(4):
                c = gh * 4 + pr
                sl = c % 2
                for j in range(NT):
                    for (wt, dst, tok, bk) in ((wq[sl], qT[sl], "qT", 0), (wk[sl], kT[sl], "kT", 1)):
                        for kc in range(8):
                            mm(ps[bk][:], wt[:, kc, :], hT[:, kc, j * 512:(j + 1) * 512], kc == 0, kc == 7,
                               [("hT", 4 * j), ("hT", 4 * j + 1), ("hT", 4 * j + 2), ("hT", 4 * j + 3),
                                ("wq" if tok == "qT" else "wk", sl)], [("ps", bk)])
                        P.op("dve", lambda e, dst=dst, bk=bk, j=j: e.tensor_copy(out=dst[:, j * 512:(j + 1) * 512], in_=ps[bk][:]),
                             reads=[("ps", bk)], writes=[(tok, sl, j)])
                if _KB < 4:
                    raise _Stop()
                if c + 1 < 8:
                    c1 = c + 1
                    P.dma("pool", wq[c1 % 2][:], wsrc(w_in, O_Q + c1 * 128, O_Q + (c1 + 1) * 128), writes=[("wq", c1 % 2)])
                    P.dma("pool", wk[c1 % 2][:], wsrc(w_in, O_K + c1 * 128, O_K + (c1 + 1) * 128), writes=[("wk", c1 % 2)])
                units = [(hh, j, kb) for hh in range(2) for j in range(NT) for kb in range(4 * j + 4)]
                LA = 4
                tinfo = {}
                pend = []

                def qk_exp(hh, j, kb, ug):
                    h = 2 * c + hh
                    r0 = 64 * hh
                    i = kb - 4 * j
                    q0 = max(0, i) * 128
                    sbk = ug % 5
                    pt = Pt[ug % NPT]
                    ptok = ("Pt", ug % NPT)
                    mm(ps[sbk][:, q0:512], kT[sl][r0:r0 + 64, kb * 128:(kb + 1) * 128],
                       qT[sl][r0:r0 + 64, j * 512 + q0:(j + 1) * 512], True, True,
                       [("kT", sl, kb // 4), ("qT", sl, j)], [("ps", sbk)])
                    for g in range(q0 // 256, 2):
                        u = 2 * j + g
                        ca, cz = max(q0, 256 * g), 256 * (g + 1)
                        P.op("act", lambda e, sbk=sbk, ca=ca, cz=cz, pt=pt, u=u, kb=kb, h=h: e.activation(
                            out=pt[:, ca:cz], in_=ps[sbk][:, ca:cz],
                            func=AF.Exp, bias=biasT[:, pidx(u, kb), h:h + 1], scale=0.125),
                            reads=[("ps", sbk), ("biasT", u, kb)], writes=[ptok])
                    if i >= 0:
                        P.op("dve", lambda e, pt=pt, i=i: e.tensor_tensor(
                            out=pt[:, i * 128:(i + 1) * 128], in0=pt[:, i * 128:(i + 1) * 128],
                            in1=maskb[:], op=ALU.mult), reads=[ptok, "maskb"], writes=[ptok])

                def pv(hh, j, kb, ug, step):
                    r0 = 64 * hh
                    lrow = 64 if hh == 0 else 0
                    vsl = slice(0, 65) if hh == 0 else slice(64, 192)
                    nk = 4 * j + 4
                    if kb == 0:
                        tinfo[(hh, j)] = (5 + (uid[0] % 2), uid[0] % 2)
                        uid[0] += 1
                    ob, fin = tinfo[(hh, j)]
                    q0 = max(0, kb - 4 * j) * 128
                    pt = Pt[ug % NPT]
                    ptok = ("Pt", ug % NPT)
                    mm(ps[ob][0:(65 if hh == 0 else 128), q0:512], Vs[:, pr, kb, vsl], pt[:, q0:512], kb == 0, kb == nk - 1,
                       [("Vs", kb), ("VsB", kb), "Vs_c", "Vs_c0", ptok], [("ps", ob)], skip=True)
                    if kb != nk - 1:
                        return
                    P.op("dve", lambda e: e.reciprocal(out=rl[fin][lrow:lrow + 1, :], in_=ps[ob][lrow:lrow + 1, :]),
                         reads=[("ps", ob)], writes=[("rl", fin)])
                    split(rl[fin][lrow:lrow + 1, :], ("rl", fin), [rlp[fin][i][lrow:lrow + 1, :] for i in range(2)],
                          [rlr[fin][lrow:lrow + 1, :]], ("rlp", fin))

                    def fin2():
                        for i in range(2):
                            mm(ps[7][:], ones_b[lrow:lrow + 1, :], rlp[fin][i][lrow:lrow + 1, :], i == 0, i == 1,
                               ["ones_b", ("rlp", fin, i)], [("ps", 7)])
                        P.op("act", lambda e: e.copy(out=bc[fin][r0:r0 + 64, :], in_=ps[7][r0:r0 + 64, :]),
                             reads=[("ps", 7)], writes=[("bc", fin)])
                        P.op("dve", lambda e, c=c: e.tensor_tensor(
                            out=bT[r0:r0 + 64, c, j * 512:(j + 1) * 512], in0=ps[ob][r0:r0 + 64, :],
                            in1=bc[fin][r0:r0 + 64, :], op=ALU.mult),
                            reads=[("ps", ob), ("bc", fin)], writes=[("bT", c, j, hh)])
                    nxt = 4 * (j + 1) + 4 if j + 1 < NT else (4 if hh == 0 else 0)
                    pend.append((step + min(7, max(nxt, 1)), fin2))

                ug0 = ugc[0]
                GB = 4
                for tb in range(0, len(units) + LA, GB):
                    for t in range(tb, tb + GB):
                        if t < len(units):
                            qk_exp(*units[t], ug0 + t)
                    for t in range(tb, tb + GB):
                        if 0 <= t - LA < len(units):
                            pv(*units[t - LA], ug0 + t - LA, t)
                    while pend and pend[0][0] <= tb + GB - 1:
                        pend.pop(0)[1]()
                while pend:
                    pend.pop(0)[1]()
                ugc[0] += len(units)

        if upto == 'B':
            raise _Stop()
        P.barrier()
        Wuv = sb(G0, [128, 8, 2048], BF16)
        o = SCR
        ub = [sb(o + i * 2048, [128, D], BF16) for i in range(4)]; o += 8192
        v32 = [sb(o, [128, D], F32), sb(o + 4096, [128, D], F32)]; o += 8192
        vln = [sb(o, [128, D], BF16), sb(o + 2048, [128, D], BF16)]; o += 4096
        ab = [sb(o, [128, D], BF16), sb(o + 2048, [128, D], BF16)]; o += 4096
        lng = sb(o, [128, D], F32); o += 4096
        lnb = sb(o, [128, D], F32); o += 4096
        WsT = sb(o, [128, 8, 128], BF16); o += 2048
        sqjC = sb(o, [128, D], F32); o += 4096
        for q in range(4):
            P.dma("pool", Wuv[:, :, q * 512:(q + 1) * 512], wsrc(w_in, q * 512, (q + 1) * 512), writes=[("Wuv", q)])
        P.dma("sp", lng[:], ln_g.partition_broadcast(128), writes=["lng"])
        P.dma("sp", lnb[:], ln_b.partition_broadcast(128), writes=["lnb"])
        P.dma("pool", WsT[:], ws_t, writes=["WsT"])
        P.op("pool", lambda e: e.memset(WsT[64:128, :, 0:64], 0.0), reads=["WsT"], writes=["WsT"])

        def C0_(b):
            for ct in range(4):
                bk = ct
                for kc in range(8):
                    mm(ps[bk][:], hT[:, kc, b * 128:(b + 1) * 128], Wuv[:, kc, ct * 512:(ct + 1) * 512], kc == 0, kc == 7,
                       [("hT", b), ("Wuv", ct)], [("ps", bk)])
                if ct < 2:
                    P.op("act", lambda e, b=b, ct=ct, bk=bk: e.activation(
                        out=ub[b % 4][:, ct * 512:(ct + 1) * 512], in_=ps[bk][:], func=AF.Gelu_apprx_tanh),
                        reads=[("ps", bk)], writes=[("ub", b % 4, ct)])
                else:
                    P.op("act", lambda e, b=b, ct=ct, bk=bk: e.activation(
                        out=v32[b % 2][:, (ct - 2) * 512:(ct - 1) * 512], in_=ps[bk][:], func=AF.Gelu_apprx_tanh,
                        accum_out=lnst[:, b, ct - 2:ct - 1]),
                        reads=[("ps", bk), "lnst"], writes=[("v32", b % 2, ct - 2), ("lnst", b, ct - 2)])

        def C1_(b):
            s = b % 2
            L = lambda i: lnst[:, b, i:i + 1]
            lt = ("lnstb", b)
            P.op("act", lambda e: e.activation(out=sqjC[:], in_=v32[s][:], func=AF.Square, accum_out=L(2)),
                 reads=[("v32", s, 0), ("v32", s, 1), "lnst"], writes=["sqjC", lt])
            P.op("dve", lambda e: e.tensor_tensor(out=L(0), in0=L(0), in1=L(1), op=ALU.add),
                 reads=[("lnst", b, 0), ("lnst", b, 1)], writes=[("lnst", b, 0)])
            P.op("dve", lambda e: e.tensor_scalar(out=L(0), in0=L(0), scalar1=1.0 / D, scalar2=0.0, op0=ALU.mult, op1=ALU.add),
                 reads=[("lnst", b, 0)], writes=[("lnst", b, 0)])
            P.op("dve", lambda e: e.tensor_tensor(out=L(1), in0=L(0), in1=L(0), op=ALU.mult),
                 reads=[("lnst", b, 0)], writes=[("lnst", b, 1)])
            P.op("dve", lambda e: e.scalar_tensor_tensor(out=L(3), in0=L(2), scalar=1.0 / D, in1=L(1),
                                                          op0=ALU.mult, op1=ALU.subtract),
                 reads=[lt, ("lnst", b, 1)], writes=[lt])
            P.op("dve", lambda e: e.tensor_scalar(out=L(3), in0=L(3), scalar1=EPS, scalar2=0.0, op0=ALU.add, op1=ALU.add),
                 reads=[lt], writes=[lt])
            P.op("pool", lambda e: e.tensor_tensor(out=L(4), in0=L(3), in1=nh[:, 0:1], op=ALU.pow),
                 reads=[lt, "nh"], writes=[lt])
            P.op("dve", lambda e: e.scalar_tensor_tensor(out=L(5), in0=L(0), scalar=-1.0, in1=L(4),
                                                          op0=ALU.mult, op1=ALU.mult),
                 reads=[lt, ("lnst", b, 0)], writes=[lt])
            P.op("dve", lambda e: e.tensor_scalar(out=v32[s][:], in0=v32[s][:], scalar1=L(4), scalar2=L(5),
                                                   op0=ALU.mult, op1=ALU.add),
                 reads=[lt, ("v32", s, 0), ("v32", s, 1)], writes=[("v32", s, 0), ("v32", s, 1)])
            P.op("dve", lambda e: e.tensor_tensor(out=v32[s][:], in0=v32[s][:], in1=lng[:], op=ALU.mult),
                 reads=[("v32", s, 0), ("v32", s, 1), "lng"], writes=[("v32", s, 0), ("v32", s, 1)])
            P.op("dve", lambda e: e.tensor_tensor(out=vln[s][:], in0=v32[s][:], in1=lnb[:], op=ALU.add),
                 reads=[("v32", s, 0), ("v32", s, 1), "lnb"], writes=[("vln", s)])

        def C1b_(b):
            s = b % 2
            for g in range(8):
                bk = 4 + g // 4
                mm(ps[bk][:, (g % 4) * 128:(g % 4 + 1) * 128], WsT[:, g, :], vln[s][:, g * 128:(g + 1) * 128], True, True,
                   ["WsT", ("vln", s)], [("ps", bk)])

        def C2_(b):
            s = b % 2
            for g in range(8):
                bk = 4 + g // 4
                P.op("dve", lambda e, g=g, bk=bk: e.scalar_tensor_tensor(
                    out=ab[s][:, g * 128:(g + 1) * 128], in0=ps[bk][:, (g % 4) * 128:(g % 4 + 1) * 128],
                    scalar=bs[:, g:g + 1], in1=ub[b % 4][:, g * 128:(g + 1) * 128], op0=ALU.add, op1=ALU.mult),
                    reads=[("ps", bk), "bs", ("ub", b % 4, g // 4)], writes=[("ab", s)])
            tr8(ab[s], ("ab", s), 6, aT[:, :, b * 128:(b + 1) * 128], ("aT", b))

        for t in range(NB + 3):
            if 0 <= t - 3 < NB:
                C2_(t - 3)
            if 0 <= t - 2 < NB:
                C1b_(t - 2)
            if 0 <= t - 1 < NB:
                C1_(t - 1)
            if t < NB:
                C0_(t)

        if upto == 'C':
            raise _Stop()
        P.barrier()
        dump("hT", hT[:])
        dump("bT", bT[:])
        dump("aT", aT[:])
        o = SCR
        wsm = [[sb(o + (s * 4 + i) * 2048, [128, 8, 128], BF16) for i in range(4)] for s in range(2)]; o += 16384
        gsb = [[sb(o + (s * 2 + i) * 2048, [128, 512], F32) for i in range(2)] for s in range(2)]; o += 8192
        tsb = [[sb(o + (s * 2 + i) * 2048, [128, 512], F32) for i in range(2)] for s in range(2)]; o += 8192
        Wout = sb(o, [128, 8, D], BF16); o += 16384
        oE = o
        P.dma("pool", Wout[:, :, 0:512], wsrc(w_out, 0, 512), writes=[("Wout", 0)])
        P.dma("pool", Wout[:, :, 512:1024], wsrc(w_out, 512, 1024), writes=[("Wout", 1)])
        step = 0

        def loadD(m):
            srcs = (wsrc(w_in, O_G + m * 128, O_G + (m + 1) * 128),
                    wsrc(w_in, O_G + D + m * 128, O_G + D + (m + 1) * 128),
                    wsrc(w_a, m * 128, (m + 1) * 128), wsrc(w_b, m * 128, (m + 1) * 128))
            for i in range(4):
                P.dma("pool", wsm[m % 2][i][:], srcs[i], writes=[("wsm", m % 2, i)])

        loadD(0)
        for m in range(8):
            s = m % 2
            if m + 1 < 8:
                loadD(m + 1)
            for j in range(NT):
                pb = 4 * (step % 2)
                sg = step % 2
                step += 1
                acts = (hT, hT, aT, bT)
                for i in range(4):
                    for kc in range(8):
                        if i < 2:
                            rd = [("hT", 4 * j + u) for u in range(4)]
                        elif i == 2:
                            rd = [("aT", 4 * j + u) for u in range(4)]
                        else:
                            rd = [("bT", kc, j, 0), ("bT", kc, j, 1)]
                        mm(ps[pb + i][:], wsm[s][i][:, kc, :], acts[i][:, kc, j * 512:(j + 1) * 512], kc == 0, kc == 7,
                           rd + [("wsm", s, i)], [("ps", pb + i)])
                for i in range(2):
                    P.op("act", lambda e, i=i, pb=pb, sg=sg: e.activation(out=gsb[sg][i][:], in_=ps[pb + i][:], func=AF.Sigmoid),
                         reads=[("ps", pb + i)], writes=[("gsb", sg, i)])
                for i in range(2):
                    P.op("dve", lambda e, i=i, pb=pb, sg=sg: e.tensor_tensor(out=tsb[sg][i][:], in0=ps[pb + 2 + i][:],
                                                                           in1=gsb[sg][i][:], op=ALU.mult),
                         reads=[("ps", pb + 2 + i), ("gsb", sg, i)], writes=[("tsb", sg, i)])
                P.op("pool", lambda e, sg=sg, m=m, j=j: e.tensor_tensor(out=g0[:, m, j * 512:(j + 1) * 512], in0=tsb[sg][0][:],
                                                                       in1=tsb[sg][1][:], op=ALU.add),
                     reads=[("tsb", sg, 0), ("tsb", sg, 1)], writes=[("mg", m, j)])

        if upto == 'D':
            raise _Stop()
        P.barrier()
        dump("mg", g0[:])
        o = oE
        xr = [sb(o + i * 4096, [128, D], F32) for i in range(3)]; o += 12288
        hbE = [sb(o + i * 2048, [128, D], BF16) for i in range(3)]; o += 6144
        gvE = sb(o, [128, D], F32); o += 4096
        sqjE = sb(o, [128, D], F32); o += 4096
        assert o <= SCR_END
        P.dma("sp", gvE[:], norm_ffn_g.partition_broadcast(128), writes=["gv"])
        trE = {}
        for t in range(NB + 2):
            if 0 <= t - 2 < NB:
                trE[t - 2]()
            if t >= NB:
                continue
            b = t
            j = b // 4
            P.dma("sp", xr[b % 3][:], x[b * 128:(b + 1) * 128, :], writes=[("xr", b % 3)])
            for ch in range(2):
                bk = 2 * (b % 2) + ch
                for m in range(8):
                    mm(ps[bk][:], g0[:, m, b * 128:(b + 1) * 128], Wout[:, m, ch * 512:(ch + 1) * 512], m == 0, m == 7,
                       [("mg", m, j), ("Wout", ch)], [("ps", bk)])
                P.op("dve", lambda e, b=b, ch=ch, bk=bk: e.tensor_tensor(
                    out=x1[:, b, ch * 512:(ch + 1) * 512], in0=ps[bk][:], in1=xr[b % 3][:, ch * 512:(ch + 1) * 512], op=ALU.add),
                    reads=[("ps", bk), ("xr", b % 3)], writes=[("x1", b, ch)])
            trE[b] = norm_block(1, b, x1[:, b, :], gvE[:], hbE[b % 3][:], sqjE[:], 4 + 2 * (b % 2),
                                h2T[:, :, b * 128:(b + 1) * 128], ("x1", b, 0), ("hbE", b % 3), ("h2T", b),
                                extra_tok=[("x1", b, 1)], defer=True)

        if upto == 'E':
            raise _Stop()
        P.barrier()
        dump("x1", x1[:])
        NG = 11
        actT = sb(SCR, [128, NG, S], BF16)
        Wd = sb(SCR + 45056, [128, NG, D], BF16)
        o = SCR + 67584
        Ag = [sb(o, [128, 512], F32), sb(o + 2048, [128, 512], F32)]; o += 4096
        Al = [sb(o, [128, 512], F32), sb(o + 2048, [128, 512], F32)]; o += 4096
        assert o <= SCR_END
        o = G0
        NW = 3
        wup = [[sb(o + (s * 2 + i) * 2048, [128, 8, 128], BF16) for i in range(2)] for s in range(NW)]; o += NW * 4096
        NR = 3
        Rg = [sb(o + s * 2080, [128, 514], F32) for s in range(NR)]; o += NR * 2080
        Rl = [sb(o + s * 2080, [128, 514], F32) for s in range(NR)]; o += NR * 2080
        Gg = [sb(o, [128, 512], F32), sb(o + 2048, [128, 512], F32)]; o += 4096
        assert o <= G0 + 32 * KB
        rstep = 0
        _KF = int(os.environ.get('KDBG_F', '9'))

        def loadF(fc):
            P.dma("pool", wup[fc % NW][0][:], wsrc(w_up, fc * 128, (fc + 1) * 128), writes=[("wup", fc % NW, 0)])
            P.dma("pool", wup[fc % NW][1][:], wsrc(w_up, DFF + fc * 128, DFF + (fc + 1) * 128), writes=[("wup", fc % NW, 1)])

        loadF(0)
        loadF(1)
        for grp in range(2):
            for q in range(2):
                P.dma("pool", Wd[:, :, q * 512:(q + 1) * 512],
                      w_down[grp * NG * 128:(grp + 1) * NG * 128, q * 512:(q + 1) * 512].rearrange("(c p) n -> p c n", p=128),
                      writes=[("Wd", q)])
            for fl in range(NG):
                fc = grp * NG + fl
                ws_ = fc % NW
                if fc + 2 < NFC:
                    loadF(fc + 2)
                for j in range(NT):
                    rs = rstep % NR
                    rp = (rstep - 1) % NR
                    sa = rstep % 2
                    pb = 2 * (rstep % 2)
                    rstep += 1
                    for i in range(2):
                        for kc in range(8):
                            mm(ps[pb + i][:], wup[ws_][i][:, kc, :], h2T[:, kc, j * 512:(j + 1) * 512], kc == 0, kc == 7,
                               [("h2T", 4 * j + u) for u in range(4)] + [("wup", ws_, i)], [("ps", pb + i)])
                    for i, (R, A, ci) in enumerate(((Rg, Ag, fc), (Rl, Al, NFC + fc))):
                        rt = ("R", i, rs)
                        P.op("act", lambda e, R=R, i=i, pb=pb, rs=rs: e.copy(out=R[rs][:, 2:514], in_=ps[pb + i][:]),
                             reads=[("ps", pb + i)], writes=[rt])
                        rn = (rs + 1) % NR
                        if j == 0:
                            P.op("dve", lambda e, R=R, rs=rs: e.memset(R[rs][:, 0:2], 0.0), writes=[("Rh", i, rs)])
                        if j < NT - 1:
                            P.op("act", lambda e, R=R, i=i, pb=pb, rn=rn: e.copy(out=R[rn][:, 0:2], in_=ps[pb + i][:, 510:512]),
                                 reads=[("ps", pb + i)], writes=[("Rh", i, rn)])
                        at = ("A", i, sa)
                        if _KF < 2:
                            continue
                        if i == 0:
                            P.op("act", lambda e, A=A, pb=pb, sa=sa, ci=ci: e.activation(
                                out=A[sa][:], in_=ps[pb][:], func=AF.Identity, scale=cw[:, ci, 2:3], bias=cb[:, ci:ci + 1]),
                                reads=[("ps", pb), "cw", "cb"], writes=[at])
                        else:
                            P.op("dve", lambda e, A=A, R=R, rs=rs, sa=sa, ci=ci: e.tensor_scalar(
                                out=A[sa][:], in0=R[rs][:, 2:514], scalar1=cw[:, ci, 2:3], scalar2=cb[:, ci:ci + 1],
                                op0=ALU.mult, op1=ALU.add), reads=[rt, "cw", "cb"], writes=[at])
                        _f2 = os.environ.get('KDBG_F2', '')
                        if _f2 == 'a':
                            continue
                        w1s, w0s = (slice(1, 513), slice(0, 512)) if _f2 != 'al' else (slice(2, 514), slice(2, 514))
                        P.op("dve", lambda e, A=A, R=R, rs=rs, sa=sa, ci=ci, w1s=w1s: e.scalar_tensor_tensor(
                            out=A[sa][:], in0=R[rs][:, w1s], scalar=cw[:, ci, 1:2], in1=A[sa][:],
                            op0=ALU.mult, op1=ALU.add), reads=[rt, ("Rh", i, rs), at, "cw"], writes=[at])
                        P.op("dve", lambda e, A=A, R=R, rs=rs, sa=sa, ci=ci, w0s=w0s: e.scalar_tensor_tensor(
                            out=A[sa][:], in0=R[rs][:, w0s], scalar=cw[:, ci, 0:1], in1=A[sa][:],
                            op0=ALU.mult, op1=ALU.add), reads=[rt, ("Rh", i, rs), at, "cw"], writes=[at])
                    if _KF < 3:
                        continue
                    P.op("act", lambda e, sa=sa: e.activation(out=Gg[sa][:], in_=Ag[sa][:], func=AF.Gelu_apprx_tanh),
                         reads=[("A", 0, sa)], writes=[("Gg", sa)])
                    P.op("pool", lambda e, sa=sa, fl=fl, j=j: e.tensor_tensor(
                        out=actT[:, fl, j * 512:(j + 1) * 512], in0=Gg[sa][:], in1=Al[sa][:], op=ALU.mult),
                        reads=[("Gg", sa), ("A", 1, sa)], writes=[("actT", fl, j)])
            for b in range(NB if _KF >= 4 else 0):
                j = b // 4
                for ch in range(2):
                    bk = 4 + 2 * (b % 2) + ch
                    for fl in range(NG):
                        mm(ps[bk][:], actT[:, fl, b * 128:(b + 1) * 128], Wd[:, fl, ch * 512:(ch + 1) * 512],
                           fl == 0, fl == NG - 1, [("actT", fl, j), ("Wd", ch)], [("ps", bk)])
                    P.op("dve", lambda e, b=b, ch=ch, bk=bk: e.tensor_tensor(
                        out=x1[:, b, ch * 512:(ch + 1) * 512], in0=ps[bk][:], in1=x1[:, b, ch * 512:(ch + 1) * 512], op=ALU.add),
                        reads=[("ps", bk), ("x1", b, ch)], writes=[("x1", b, ch)])

        if upto == 'F':
            raise _Stop()
        P.barrier()
        dump("x2", x1[:])
        o = SCR
        Wg = sb(o, [128, 8, D], BF16); o += 16384
        Wp = sb(o, [128, 2, D], BF16); o += 4096
        pst = [sb(o, [128, PLE], F32), sb(o + 1024, [128, PLE], F32)]; o += 2048
        pb16 = [sb(o, [128, PLE], BF16), sb(o + 512, [128, PLE], BF16)]; o += 1024
        pTb = [sb(o, [128, 2, 128], BF16), sb(o + 512, [128, 2, 128], BF16)]; o += 1024
        hbG = [sb(o, [128, D], BF16), sb(o + 2048, [128, D], BF16)]; o += 4096
        h3Tb = [sb(o, [128, 8, 128], BF16), sb(o + 2048, [128, 8, 128], BF16)]; o += 4096
        sgt = [[sb(o + (s * 2 + i) * 2048, [128, 512], F32) for i in range(2)] for s in range(2)]; o += 8192
        tt = [sb(o, [128, D], F32), sb(o + 4096, [128, D], F32)]; o += 8192
        ot = [sb(o, [128, D], F32), sb(o + 4096, [128, D], F32)]; o += 8192
        gv3 = sb(o, [128, D], F32); o += 4096
        gvf = sb(o, [128, D], F32); o += 4096
        sqjG = sb(o, [128, D], F32); o += 4096
        assert o <= SCR_END
        P.dma("pool", Wg[:, :, 0:512], wsrc(w_pg, 0, 512), writes=[("Wg", 0)])
        P.dma("pool", Wg[:, :, 512:1024], wsrc(w_pg, 512, 1024), writes=[("Wg", 1)])
        P.dma("pool", Wp[:], wsrc(w_ple, 0, D), writes=["Wp"])
        P.dma("sp", gv3[:], norm_ple_g.partition_broadcast(128), writes=["gv"])
        P.dma("sp", gvf[:], norm_final_g.partition_broadcast(128), writes=["gvf"])

        trG = {}

        def G0_(b):
            s = b % 2
            P.dma("sp", pst[s][:], pin[b * 128:(b + 1) * 128, :], writes=[("pst", s)])
            P.op("act", lambda e: e.copy(out=pb16[s][:], in_=pst[s][:]), reads=[("pst", s)], writes=[("pb16", s)])
            trG[b] = norm_block(2, b, x1[:, b, :], gv3[:], hbG[s][:], sqjG[:], 0, h3Tb[s][:],
                                ("x1", b, 0), ("hbG", s), ("h3Tb", s), extra_tok=[("x1", b, 1)], defer=True)

        def G0b_(b):
            s = b % 2
            tr8(pb16[s], ("pb16", s), 4, pTb[s][:], ("pTb", s), nch=2)
            trG[b]()

        def G1_(b):
            s = b % 2
            for ch in range(2):
                bk = 2 + ch
                for kc in range(8):
                    mm(ps[bk][:], h3Tb[s][:, kc, :], Wg[:, kc, ch * 512:(ch + 1) * 512], kc == 0, kc == 7,
                       [("h3Tb", s), ("Wg", ch)], [("ps", bk)])
                P.op("act", lambda e, ch=ch, bk=bk: e.activation(out=sgt[s][ch][:], in_=ps[bk][:], func=AF.Sigmoid),
                     reads=[("ps", bk)], writes=[("sgt", s, ch)])
                bk2 = 6 + ch
                for kc in range(2):
                    mm(ps[bk2][:], pTb[s][:, kc, :], Wp[:, kc, ch * 512:(ch + 1) * 512], kc == 0, kc == 1,
                       [("pTb", s), "Wp"], [("ps", bk2)])
                P.op("dve", lambda e, ch=ch, bk2=bk2: e.tensor_tensor(
                    out=tt[s][:, ch * 512:(ch + 1) * 512], in0=ps[bk2][:], in1=sgt[s][ch][:], op=ALU.mult),
                    reads=[("ps", bk2), ("sgt", s, ch)], writes=[("tt", s, ch)])
                P.op("dve", lambda e, ch=ch: e.tensor_tensor(
                    out=tt[s][:, ch * 512:(ch + 1) * 512], in0=tt[s][:, ch * 512:(ch + 1) * 512],
                    in1=x1[:, b, ch * 512:(ch + 1) * 512], op=ALU.add),
                    reads=[("tt", s, ch), ("x1", b, ch)], writes=[("tt", s, ch)])
            ssc = stat[:, 3, 0, b:b + 1]
            msc = stat[:, 3, 1, b:b + 1]
            rsc = stat[:, 3, 2, b:b + 1]
            st = ("stat", 3, b)
            P.op("act", lambda e: e.activation(out=sqjG[:], in_=tt[s][:], func=AF.Square, accum_out=ssc),
                 reads=[("tt", s, 0), ("tt", s, 1), "stat"], writes=["sqj", st])
            P.op("dve", lambda e: e.tensor_scalar(out=msc, in0=ssc, scalar1=1.0 / D, scalar2=EPS, op0=ALU.mult, op1=ALU.add),
                 reads=[st], writes=[st])
            P.op("pool", lambda e: e.tensor_tensor(out=rsc, in0=msc, in1=nh[:, 0:1], op=ALU.pow), reads=[st, "nh"], writes=[st])
            P.op("dve", lambda e: e.scalar_tensor_tensor(out=ot[s][:], in0=tt[s][:], scalar=rsc, in1=gvf[:],
                                                          op0=ALU.mult, op1=ALU.mult),
                 reads=[("tt", s, 0), ("tt", s, 1), st, "gvf"], writes=[("ot", s)])
            P.dma("sp", out[b * 128:(b + 1) * 128, :], ot[s][:], reads=[("ot", s)], writes=[("out", b)])

        for t in range(NB + 2):
            if 0 <= t - 2 < NB:
                G1_(t - 2)
            if 0 <= t - 1 < NB:
                G0b_(t - 1)
            if t < NB:
                G0_(t)
        P.op("sp", lambda e: e.nop(), reads=[("out", b) for b in range(NB)])
        P.barrier()

    except _Stop:
        P.barrier()

    emit(nc, P)
    es.close()
    return nc


_CACHE = {}


def _prep_shared(inp):
    f = np.float32
    d = {}
    d["norm_mix_g"] = np.ascontiguousarray(inp["norm_mix_g"][0], f)
    d["w_in"] = np.ascontiguousarray(inp["w_in"][0], f)
    d["b_f"] = np.ascontiguousarray(inp["b_f"][0], f)
    d["gmlp_ln_g"] = np.ascontiguousarray(inp["gmlp_ln_g"][0], f)
    d["gmlp_ln_b"] = np.ascontiguousarray(inp["gmlp_ln_b"][0], f)
    d["ws_t"] = np.ascontiguousarray(np.transpose(inp["gmlp_w_s"][0], (2, 0, 1)), f)
    d["bs_t"] = np.ascontiguousarray(inp["gmlp_b_s"][0].T, f)
    d["w_branch_a"] = np.ascontiguousarray(inp["w_branch_a"][0], f)
    d["w_branch_b"] = np.ascontiguousarray(inp["w_branch_b"][0], f)
    d["w_out"] = np.ascontiguousarray(inp["w_out"][0], f)
    d["norm_ffn_g"] = np.ascontiguousarray(inp["norm_ffn_g"][0], f)
    d["w_up"] = np.ascontiguousarray(inp["w_up"][0], f)
    cwv = np.asarray(inp["conv_w"][0], f)
    d["cw_t"] = np.ascontiguousarray(cwv.T.reshape(44, 128, 3).transpose(1, 0, 2), f)
    d["cb_t"] = np.ascontiguousarray(np.asarray(inp["conv_b"][0], f).reshape(44, 128).T, f)
    d["w_down"] = np.ascontiguousarray(inp["w_down"][0], f)
    d["norm_ple_g"] = np.ascontiguousarray(inp["norm_ple_g"][0], f)
    d["w_ple"] = np.ascontiguousarray(inp["w_ple"][0], f)
    d["w_ple_gate"] = np.ascontiguousarray(inp["w_ple_gate"][0], f)
    d["norm_final_g"] = np.ascontiguousarray(inp["norm_final_g"], f)
    d["c_ident"] = np.eye(128, dtype=f)
    r = np.arange(128)
    d["c_tri"] = (r[:, None] <= r[None, :]).astype(f)
    d["c_sel"] = np.ones((128, 128), f)
    return d


def kernel(**inputs):
    inp = {k: np.asarray(v) for k, v in inputs.items()}
    n = 8
    if "nc" not in _CACHE:
        _CACHE["nc"] = build_nc()
    nc = _CACHE["nc"]
    shared = _prep_shared(inp)
    x = np.asarray(inp["x"], np.float32)
    p = np.asarray(inp["p"], np.float32)[0]
    in_maps = []
    for i in range(n):
        m = dict(shared)
        m["x"] = np.ascontiguousarray(x[i])
        m["p"] = np.ascontiguousarray(p[i])
        in_maps.append(m)
    res = run_bass_kernel_spmd(nc, in_maps, core_ids=list(range(n)))
    return np.stack([np.asarray(r["out"], np.float32) for r in res.results], axis=0)
```

```python
import contextlib
import numpy as np
import concourse.bass as bass
import concourse.mybir as mybir
from concourse.bass_utils import run_bass_kernel_spmd

F32 = mybir.dt.float32
BF16 = mybir.dt.bfloat16
AF = mybir.ActivationFunctionType
ALU = mybir.AluOpType

ENGS = ("pe", "act", "dve", "pool", "sp")

S = 2048
D = 1024
NB = 16
NT = 4
DFF = 2816
NFC = 22
IN_COLS = 7184
O_Q = 2048
O_K = 3072
O_V = 4096
O_F = 5120
O_G = 5136
PLE = 256
EPS = 1e-6
KB = 1024
BASE = 16512
import os
_STG = int(os.environ.get('KDBG_STG', '6'))
_NBA = int(os.environ.get('KDBG_NBA', '16'))


class _Op:
    __slots__ = ("fn", "deps", "dma", "flag", "semval", "dsem", "dcount", "waits", "prewait")

    def __init__(self, fn, deps, dma):
        self.fn = fn
        self.deps = deps
        self.dma = dma
        self.flag = False
        self.semval = None
        self.dsem = None
        self.dcount = None
        self.waits = []
        self.prewait = None


class Prog:
    def __init__(self, nc, n_dma_sems=32):
        self.nc = nc
        self.streams = {e: [] for e in ENGS}
        self.last_writer = {}
        self.readers = {}
        self.n_dma_sems = n_dma_sems

    def op(self, eng, fn, reads=(), writes=(), dma=False, extra=None):
        st = self.streams[eng]
        me = (eng, len(st))
        deps = {}
        for t in reads:
            w = self.last_writer.get(t)
            if w is not None:
                deps[w] = True
        for t in writes:
            w = self.last_writer.get(t)
            if w is not None and w not in deps:
                deps[w] = False
            for r in self.readers.get(t, ()):
                if r not in deps:
                    deps[r] = False
        if extra:
            for d in extra:
                deps[d] = True
        deps.pop(me, None)
        for t in writes:
            self.last_writer[t] = me
            self.readers[t] = []
        for t in reads:
            if self.last_writer.get(t) != me:
                self.readers.setdefault(t, []).append(me)
        st.append(_Op(fn, deps, dma))
        return me

    def dma(self, eng, out, in_, reads=(), writes=()):
        return self.op(eng, lambda e: e.dma_start(out=out, in_=in_), reads, writes, dma=True)

    def barrier(self):
        lasts = []
        for e in ENGS:
            st = self.streams[e]
            for i in range(len(st) - 1, -1, -1):
                if not st[i].dma:
                    lasts.append((e, i))
                    break
            n = 0
            for i in range(len(st) - 1, -1, -1):
                if st[i].dma:
                    lasts.append((e, i))
                    n += 1
                    if n >= 16:
                        break
        for e in ENGS:
            self.op(e, lambda g: g.nop(), extra=[d for d in lasts])

    def resolve(self):
        streams = self.streams
        dma_engs = [e for e in ENGS if any(o.dma for o in streams[e])]
        pools = {}
        if dma_engs:
            per = max(2, self.n_dma_sems // len(dma_engs))
            for e in dma_engs:
                pools[e] = min(per, 16)
        self.dma_pool_sizes = pools
        for e in dma_engs:
            k = 0
            counts = [0] * pools[e]
            lastop = [None] * pools[e]
            for i, o in enumerate(streams[e]):
                if not o.dma:
                    continue
                s = k % pools[e]
                k += 1
                if lastop[s] is not None:
                    o.prewait = ((e, s), counts[s])
                counts[s] += 16
                o.dsem = (e, s)
                o.dcount = counts[s]
                lastop[s] = i
        for e in ENGS:
            seen = {f: -1 for f in ENGS}
            seen_d = {}
            for i, o in enumerate(streams[e]):
                if o.prewait is not None:
                    key, cnt = o.prewait
                    if seen_d.get(key, 0) >= cnt:
                        o.prewait = None
                    else:
                        seen_d[key] = cnt
                for (f, n), is_raw in sorted(o.deps.items(), key=lambda kv: (kv[0][0], -kv[0][1])):
                    p = streams[f][n]
                    if p.dma:
                        if seen_d.get(p.dsem, 0) >= p.dcount:
                            continue
                        seen_d[p.dsem] = p.dcount
                        o.waits.append(("d", p.dsem, p.dcount))
                        continue
                    if f == e and not o.dma:
                        if not (is_raw and e != "pe"):
                            continue
                    if n <= seen[f]:
                        continue
                    seen[f] = n
                    p.flag = True
                    o.waits.append(("c", f, n))
        for e in ENGS:
            c = 0
            for o in streams[e]:
                if o.flag:
                    c += 1
                    o.semval = c


def emit(nc, prog):
    prog.resolve()
    with contextlib.ExitStack() as es:
        csem = {e: es.enter_context(nc.semaphore("c_" + e)) for e in ENGS}
        dsem = {}
        for e, n in prog.dma_pool_sizes.items():
            for s in range(n):
                dsem[(e, s)] = es.enter_context(nc.semaphore("d_%s_%d" % (e, s)))
        block = es.enter_context(nc.Block())

        def run(ename, eng):
            for o in prog.streams[ename]:
                if o.prewait is not None:
                    key, cnt = o.prewait
                    eng.wait_ge(dsem[key], cnt)
                for w in o.waits:
                    if w[0] == "d":
                        eng.wait_ge(dsem[w[1]], w[2])
                    else:
                        eng.wait_ge(csem[w[1]], prog.streams[w[1]][w[2]].semval)
                ins = o.fn(eng)
                if o.dma:
                    ins.then_inc(dsem[o.dsem], 16)
                elif o.flag:
                    ins.then_inc(csem[ename], 1)

        @block.tensor
        def _(eng):
            run("pe", eng)

        @block.scalar
        def _(eng):
            run("act", eng)

        @block.vector
        def _(eng):
            run("dve", eng)

        @block.gpsimd
        def _(eng):
            run("pool", eng)

        @block.sync
        def _(eng):
            run("sp", eng)


class _Stop(Exception):
    pass


def build_nc(debug=False, upto=None):
    nc = bass.Bass("TRN2", target_bir_lowering=False)

    def din(name, shape):
        return nc.dram_tensor(name, list(shape), F32, kind="ExternalInput").ap()

    x = din("x", [S, D])
    pin = din("p", [S, PLE])
    norm_mix_g = din("norm_mix_g", [D])
    w_in = din("w_in", [D, IN_COLS])
    b_f = din("b_f", [16])
    ln_g = din("gmlp_ln_g", [D])
    ln_b = din("gmlp_ln_b", [D])
    ws_t = din("ws_t", [128, 8, 128])
    bs_t = din("bs_t", [128, 8])
    w_a = din("w_branch_a", [D, D])
    w_b = din("w_branch_b", [D, D])
    w_out = din("w_out", [D, D])
    norm_ffn_g = din("norm_ffn_g", [D])
    w_up = din("w_up", [D, 2 * DFF])
    cw_t = din("cw_t", [128, 44, 3])
    cb_t = din("cb_t", [128, 44])
    w_down = din("w_down", [DFF, D])
    norm_ple_g = din("norm_ple_g", [D])
    w_ple = din("w_ple", [PLE, D])
    w_pg = din("w_ple_gate", [D, D])
    norm_final_g = din("norm_final_g", [D])
    c_ident = din("c_ident", [128, 128])
    c_tri = din("c_tri", [128, 128])
    c_sel = din("c_sel", [128, 128])
    out = nc.dram_tensor("out", [S, D], F32, kind="ExternalOutput").ap()
    dbg = {}
    if debug:
        for nm in ("hT", "bT", "aT", "mg"):
            dbg[nm] = nc.dram_tensor("dbg_" + nm, [128, 8, S], BF16, kind="ExternalOutput").ap()
        for nm in ("x1", "x2"):
            dbg[nm] = nc.dram_tensor("dbg_" + nm, [128, NB, D], F32, kind="ExternalOutput").ap()

    def dump(nm, src):
        if debug:
            P.dma("sp", dbg[nm], src)
            P.barrier()

    cnt = [0]

    def sb(off, shape, dt):
        cnt[0] += 1
        return nc.alloc_sbuf_tensor_at("t%d" % cnt[0], list(shape), dt, offset=off)

    C0 = BASE
    ident = sb(C0 + 0, [128, 128], BF16)
    onesf = sb(C0 + 256, [128, 128], F32)
    trif = sb(C0 + 768, [128, 128], F32)
    self_ = sb(C0 + 1280, [128, 128], F32)
    maskb = sb(C0 + 1792, [128, 128], BF16)
    cw = sb(C0 + 2048, [128, 44, 3], F32)
    cb = sb(C0 + 2592, [128, 44], F32)
    bs = sb(C0 + 2784, [128, 8], F32)
    bfb = sb(C0 + 2816, [128, 16], F32)
    nh = sb(C0 + 2880, [128, 16], F32)
    stat = sb(C0 + 2944, [128, 4, 3, 16], F32)
    lnst = sb(C0 + 3712, [128, 16, 6], F32)
    G0 = C0 + 4 * KB
    HT = C0 + 36 * KB
    BT = C0 + 68 * KB
    AT = C0 + 100 * KB
    SCR = C0 + 132 * KB
    SCR_END = 229344

    g0 = sb(G0, [128, 8, S], BF16)
    hT = sb(HT, [128, 8, S], BF16)
    bT = sb(BT, [128, 8, S], BF16)
    aT = sb(AT, [128, 8, S], BF16)
    x1 = sb(HT, [128, NB, D], F32)
    h2T = sb(AT, [128, 8, S], BF16)

    es = contextlib.ExitStack()
    ps = [es.enter_context(nc.psum_tensor("ps%d" % i, [128, 512], F32)) for i in range(8)]

    def psb(i):
        return ps[i][:].bitcast(BF16).rearrange("p (c n) -> p c n", c=8)

    P = Prog(nc)

    def mm(o, lhsT, rhs, start, stop, reads, writes, skip=False):
        if skip:
            P.op("pe", lambda e: e.matmul(o, lhsT=lhsT, rhs=rhs, start=start, stop=stop,
                                           skip_group_check=True), reads, writes)
        else:
            P.op("pe", lambda e: e.matmul(o, lhsT=lhsT, rhs=rhs, start=start, stop=stop), reads, writes)

    def wsrc(w, c0, c1):
        return w[:, c0:c1].rearrange("(c p) n -> p c n", p=128)

    try:
        P.dma("pool", ident[:], c_ident, writes=["ident"])
        P.dma("sp", trif[:], c_tri, writes=["trif"])
        P.dma("sp", self_[:], c_sel, writes=["self"])
        P.dma("pool", maskb[:], c_tri, writes=["maskb"])
        P.dma("sp", cw[:], cw_t, writes=["cw"])
        P.dma("sp", cb[:], cb_t, writes=["cb"])
        P.dma("sp", bs[:], bs_t, writes=["bs"])
        P.dma("sp", bfb[:], b_f.partition_broadcast(128), writes=["bfb"])
        P.op("pool", lambda e: e.memset(onesf[:], 1.0), writes=["onesf"])
        P.op("pool", lambda e: e.memset(nh[:], -0.5), writes=["nh"])
        P.op("pool", lambda e: e.memset(stat[:], 0.0), writes=["stat"])
        P.op("pool", lambda e: e.memset(lnst[:], 0.0), writes=["lnst"])

        if upto == 'c':
            raise _Stop()
        def tr8(src, src_tok, bank, dstT, dst_tok, nch=8):
            for c in range(nch):
                bk = bank + c // 4
                P.op("pe", lambda e, c=c, bk=bk: e.matmul(ps[bk][:, (c % 4) * 128:(c % 4 + 1) * 128],
                                                          lhsT=src[:, c * 128:(c + 1) * 128], rhs=ident[:],
                                                          start=True, stop=True),
                     reads=[src_tok, "ident"], writes=[("ps", bk)])
            for hf in range((nch + 3) // 4):
                n = min(4, nch - 4 * hf)
                pvv = ps[bank + hf][:, 0:n * 128].rearrange("p (c n) -> p c n", c=n)
                P.op("act", lambda e, hf=hf, n=n, pvv=pvv: e.copy(out=dstT[:, 4 * hf:4 * hf + n, :], in_=pvv),
                     reads=[("ps", bank + hf)], writes=[dst_tok])

        def norm_block(ni, b, src, gv, hb, sqj, bank, dstT, src_tok, hb_tok, dst_tok, extra_tok=(), defer=False):
            ssc = stat[:, ni, 0, b:b + 1]
            msc = stat[:, ni, 1, b:b + 1]
            rsc = stat[:, ni, 2, b:b + 1]
            st = ("stat", ni, b)
            P.op("act", lambda e: e.activation(out=hb, in_=src, func=AF.Square, accum_out=ssc),
                 reads=[src_tok, "stat"] + list(extra_tok), writes=[hb_tok, st])
            if _STG < 2:
                return
            P.op("dve", lambda e: e.tensor_scalar(out=msc, in0=ssc, scalar1=1.0 / D, scalar2=EPS,
                                                   op0=ALU.mult, op1=ALU.add), reads=[st], writes=[st])
            if _STG < 3:
                return
            P.op("pool", lambda e: e.tensor_tensor(out=rsc, in0=msc, in1=nh[:, 0:1], op=ALU.pow),
                 reads=[st, "nh"], writes=[st])
            if _STG < 4:
                return
            P.op("dve", lambda e: e.scalar_tensor_tensor(out=hb, in0=src, scalar=rsc, in1=gv,
                                                          op0=ALU.mult, op1=ALU.mult),
                 reads=[src_tok, st, "gv"], writes=[hb_tok])
            if _STG < 5:
                return
            if defer:
                return lambda: tr8(hb, hb_tok, bank, dstT, dst_tok)
            tr8(hb, hb_tok, bank, dstT, dst_tok)

        xt = [sb(BT + i * 4096, [128, D], F32) for i in range(4)]
        hbA = [sb(BT + 16384 + i * 2048, [128, D], BF16) for i in range(4)]
        gvA = sb(BT + 24576, [128, D], F32)
        sqjA = sb(BT + 28672, [128, D], F32)
        P.dma("sp", gvA[:], norm_mix_g.partition_broadcast(128), writes=["gv"])
        Vs = sb(G0, [128, 4, NB, 192], BF16)
        Wv = sb(G0 + 24576, [128, 8, 512], BF16)
        oB = SCR + 20480
        wf = sb(oB, [128, 8, 16], BF16)
        wq = [sb(oB + 29760, [128, 8, 128], BF16), sb(oB + 29760 + 2048, [128, 8, 128], BF16)]
        wk = [sb(oB + 33856, [128, 8, 128], BF16), sb(oB + 33856 + 2048, [128, 8, 128], BF16)]
        P.dma("pool", wf[:], wsrc(w_in, O_F, O_F + 16), writes=["wf"])
        P.dma("pool", Wv[:], wsrc(w_in, O_V, O_V + 512), writes=["Wv"])
        P.dma("pool", wq[0][:], wsrc(w_in, O_Q, O_Q + 128), writes=[("wq", 0)])
        P.dma("pool", wk[0][:], wsrc(w_in, O_K, O_K + 128), writes=[("wk", 0)])
        if upto == 'p':
            raise _Stop()
        trA = {}
        for t in range(NB + 2):
            if 0 <= t - 2 < NB:
                trA[t - 2]()
            if t < NB:
                b = t
                P.dma("sp", xt[b % 4][:], x[b * 128:(b + 1) * 128, :], writes=[("xt", b % 4)])
                trA[b] = norm_block(0, b, xt[b % 4][:], gvA[:], hbA[b % 4][:], sqjA[:], 2 * (b % 4),
                                    hT[:, :, b * 128:(b + 1) * 128], ("xt", b % 4), ("hbA", b % 4), ("hT", b), defer=True)

        if upto == 'A':
            raise _Stop()
        o = SCR + 20480
        o += 256
        Lf = sb(o, [128, NB, 16], F32); o += 1024
        Sacc = sb(o, [128, NB + 1, 16], F32); o += 1088
        Cc = sb(o, [128, NB, 16], F32); o += 1024
        Mm = sb(o, [128, NB, 16], F32); o += 1024
        xf = sb(o, [128, 2, 16], F32); o += 128
        ef = sb(o, [128, 2, 16], F32); o += 128
        o = (o + 63) // 64 * 64
        NPAIR = NB * (NB + 1) // 2
        biasT = sb(o, [128, NPAIR, 16], F32); o += NPAIR * 64
        qT = [sb(o, [128, S], BF16), sb(o + 4096, [128, S], BF16)]; o += 8192
        kT = [sb(o, [128, S], BF16), sb(o + 4096, [128, S], BF16)]; o += 8192
        assert o == oB + 29760, (o, oB)
        o += 8192
        NPT = 8
        Pt = [sb(o + i * 1024, [128, 512], BF16) for i in range(NPT)]; o += NPT * 1024
        rl = [sb(o, [128, 512], F32), sb(o + 2048, [128, 512], F32)]; o += 4096
        bc = [sb(o, [128, 512], F32), sb(o + 2048, [128, 512], F32)]; o += 4096
        assert o <= SCR_END, o

        def pidx(u, kb):
            return u * (u + 1) + kb

        oA = AT
        tri_b = sb(oA, [128, 128], BF16); oA += 256
        sel_b = sb(oA, [128, 128], BF16); oA += 256
        ones_b = sb(oA, [128, 128], BF16); oA += 256
        Lp = [sb(oA + i * 512, [128, NB, 16], BF16) for i in range(3)]; oA += 1536
        Sp = [sb(oA + i * 544, [128, NB + 1, 16], BF16) for i in range(3)]; oA += 1632
        RL = [sb(oA + i * 1024, [128, NB, 16], F32) for i in range(2)]; oA += 2048
        RS = [sb(oA + i * 1088, [128, NB + 1, 16], F32) for i in range(2)]; oA += 2176
        rlp = [[sb(oA + (f_ * 2 + i) * 1024, [128, 512], BF16) for i in range(2)] for f_ in range(2)]; oA += 4096
        rlr = [sb(oA + f_ * 2048, [128, 512], F32) for f_ in range(2)]; oA += 4096
        P.dma("pool", tri_b[:], c_tri, writes=["tri_b"])
        P.dma("pool", sel_b[:], c_sel, writes=["sel_b"])
        P.op("pool", lambda e: e.memset(ones_b[:], 1.0), writes=["ones_b"])
        for i in range(3):
            P.op("pool", lambda e, i=i: e.memset(Sp[i][:, 0, :], 0.0), writes=[("Sp", 0, i)])

        def split(src, src_tok, pieces, tmps, key):
            cur, cur_tok = src, src_tok
            for i, pc in enumerate(pieces):
                P.op("dve", lambda e, pc=pc, cur=cur: e.tensor_copy(out=pc, in_=cur), reads=[cur_tok], writes=[key + (i,)])
                if i < len(pieces) - 1:
                    P.op("dve", lambda e, pc=pc, cur=cur, t=tmps[i]: e.tensor_tensor(out=t, in0=cur, in1=pc, op=ALU.subtract),
                         reads=[cur_tok, key + (i,)], writes=[key + ("r", i)])
                    cur, cur_tok = tmps[i], key + ("r", i)

        if not os.environ.get('KDBG_NOMS'):
            P.op("dve", lambda e: e.memset(Vs[:, :, :, 65:128], 0.0), writes=["Vs_c0"])
            P.op("dve", lambda e: e.memset(Vs[:, :, :, 64:65], 1.0), writes=["Vs_c"])
        P.op("pool", lambda e: e.memset(Sacc[:, 0, :], 0.0), writes=[("Sacc", 0)])

        for b in range(NB):
            for kc in range(8):
                mm(ps[2][:, 0:16], hT[:, kc, b * 128:(b + 1) * 128], wf[:, kc, :], kc == 0, kc == 7,
                   [("hT", b), "wf"], [("ps", 2)])
            P.op("dve", lambda e, b=b: e.tensor_tensor(out=xf[:, b % 2, :], in0=ps[2][:, 0:16], in1=bfb[:], op=ALU.add),
                 reads=[("ps", 2), "bfb"], writes=[("xf", b % 2)])
            P.op("act", lambda e, b=b: e.activation(out=ef[:, b % 2, :], in_=xf[:, b % 2, :], func=AF.Exp, scale=-1.0),
                 reads=[("xf", b % 2)], writes=[("ef", b % 2)])
            P.op("act", lambda e, b=b: e.activation(out=Lf[:, b, :], in_=ef[:, b % 2, :], func=AF.Ln, bias=onesf[:, 0:1]),
                 reads=[("ef", b % 2), "onesf"], writes=[("Lf", b)])
            P.op("dve", lambda e, b=b: e.tensor_tensor(out=Sacc[:, b + 1, :], in0=Sacc[:, b, :], in1=Lf[:, b, :], op=ALU.add),
                 reads=[("Sacc", b), ("Lf", b)], writes=[("Sacc", b + 1)])
            split(Lf[:, b, :], ("Lf", b), [Lp[i][:, b, :] for i in range(3)], [RL[i][:, b, :] for i in range(2)], ("Lp", b))
            split(Sacc[:, b + 1, :], ("Sacc", b + 1), [Sp[i][:, b + 1, :] for i in range(3)],
                  [RS[i][:, b + 1, :] for i in range(2)], ("Sp", b + 1))
        for b in range(NB):
            for (cols, mat, mtok) in ((slice(0, 16), tri_b, "tri_b"), (slice(16, 32), sel_b, "sel_b")):
                for i in range(3):
                    mm(ps[3][:, cols], mat[:], Lp[i][:, b, :], i == 0, False, [mtok, ("Lp", b, i)], [("ps", 3)])
                for i in range(3):
                    mm(ps[3][:, cols], ones_b[:], Sp[i][:, b, :], False, i == 2, ["ones_b", ("Sp", b, i)], [("ps", 3)])
            P.op("dve", lambda e, b=b: e.tensor_copy(out=Cc[:, b, :], in_=ps[3][:, 0:16]), reads=[("ps", 3)], writes=[("Cc", b)])
            P.op("dve", lambda e, b=b: e.tensor_copy(out=Mm[:, b, :], in_=ps[3][:, 16:32]), reads=[("ps", 3)], writes=[("Mm", b)])
        for u in range(NB // 2):
            for kb in range(2 * u + 2):
                P.op("pool", lambda e, u=u, kb=kb: e.tensor_tensor(out=biasT[:, pidx(u, kb), :], in0=Cc[:, kb, :],
                                                                   in1=Mm[:, 2 * u, :], op=ALU.subtract),
                     reads=[("Cc", kb), ("Mm", 2 * u)], writes=[("biasT", u, kb)])

        _KB = int(os.environ.get('KDBG_B', '9'))
        if _KB < 2:
            raise _Stop()
        uid = [0]
        ugc = [0]
        for gh in range(2):
            for b in range(NB):
                bk = int(os.environ.get('KDBG_VB', '4')) + (b % 2)
                for kc in range(8):
                    mm(ps[bk][:], hT[:, kc, b * 128:(b + 1) * 128], Wv[:, kc, :], kc == 0, kc == 7,
                       [("hT", b), "Wv"], [("ps", bk)])
                pv = ps[bk][:].rearrange("p (a n) -> p a n", a=4)
                _vc = int(os.environ.get('KDBG_VC', '2'))
                if _vc >= 1:
                    P.op("act", lambda e, b=b, pv=pv: e.copy(out=Vs[:, :, b, 0:64], in_=pv[:, :, 0:64]),
                         reads=[("ps", bk)], writes=[("Vs", b)])
                if _vc >= 2:
                    P.op("act", lambda e, b=b, pv=pv: e.copy(out=Vs[:, :, b, 128:192], in_=pv[:, :, 64:128]),
                         reads=[("ps", bk)], writes=[("VsB", b)])
            if _KB < 3:
                raise _Stop()
            if gh == 0:
                P.dma("pool", Wv[:], wsrc(w_in, O_V + 512, O_V + 1024), writes=["Wv"])
            for pr in range(4):
                c = gh * 4 + pr
                sl = c % 2
                for j in range(NT):
                    for (wt, dst, tok, bk) in ((wq[sl], qT[sl], "qT", 0), (wk[sl], kT[sl], "kT", 1)):
                        for kc in range(8):
                            mm(ps[bk][:], wt[:, kc, :], hT[:, kc, j * 512:(j + 1) * 512], kc == 0, kc == 7,
                               [("hT", 4 * j), ("hT", 4 * j + 1), ("hT", 4 * j + 2), ("hT", 4 * j + 3),
                                ("wq" if tok == "qT" else "wk", sl)], [("ps", bk)])
                        P.op("dve", lambda e, dst=dst, bk=bk, j=j: e.tensor_copy(out=dst[:, j * 512:(j + 1) * 512], in_=ps[bk][:]),
                             reads=[("ps", bk)], writes=[(tok, sl, j)])
                if _KB < 4:
                    raise _Stop()
                if c + 1 < 8:
                    c1 = c + 1
                    P.dma("pool", wq[c1 % 2][:], wsrc(w_in, O_Q + c1 * 128, O_Q + (c1 + 1) * 128), writes=[("wq", c1 % 2)])
                    P.dma("pool", wk[c1 % 2][:], wsrc(w_in, O_K + c1 * 128, O_K + (c1 + 1) * 128), writes=[("wk", c1 % 2)])
                units = [(hh, j, kb) for hh in range(2) for j in range(NT) for kb in range(4 * j + 4)]
                LA = 4
                tinfo = {}
                pend = []

                def qk_exp(hh, j, kb, ug):
                    h = 2 * c + hh
                    r0 = 64 * hh
                    i = kb - 4 * j
                    q0 = max(0, i) * 128
                    sbk = ug % 5
                    pt = Pt[ug % NPT]
                    ptok = ("Pt", ug % NPT)
                    mm(ps[sbk][:, q0:512], kT[sl][r0:r0 + 64, kb * 128:(kb + 1) * 128],
                       qT[sl][r0:r0 + 64, j * 512 + q0:(j + 1) * 512], True, True,
                       [("kT", sl, kb // 4), ("qT", sl, j)], [("ps", sbk)])
                    for g in range(q0 // 256, 2):
                        u = 2 * j + g
                        ca, cz = max(q0, 256 * g), 256 * (g + 1)
                        P.op("act", lambda e, sbk=sbk, ca=ca, cz=cz, pt=pt, u=u, kb=kb, h=h: e.activation(
                            out=pt[:, ca:cz], in_=ps[sbk][:, ca:cz],
                            func=AF.Exp, bias=biasT[:, pidx(u, kb), h:h + 1], scale=0.125),
                            reads=[("ps", sbk), ("biasT", u, kb)], writes=[ptok])
                    if i >= 0:
                        P.op("dve", lambda e, pt=pt, i=i: e.tensor_tensor(
                            out=pt[:, i * 128:(i + 1) * 128], in0=pt[:, i * 128:(i + 1) * 128],
                            in1=maskb[:], op=ALU.mult), reads=[ptok, "maskb"], writes=[ptok])

                def pv(hh, j, kb, ug, step):
                    r0 = 64 * hh
                    lrow = 64 if hh == 0 else 0
                    vsl = slice(0, 65) if hh == 0 else slice(64, 192)
                    nk = 4 * j + 4
                    if kb == 0:
                        tinfo[(hh, j)] = (5 + (uid[0] % 2), uid[0] % 2)
                        uid[0] += 1
                    ob, fin = tinfo[(hh, j)]
                    q0 = max(0, kb - 4 * j) * 128
                    pt = Pt[ug % NPT]
                    ptok = ("Pt", ug % NPT)
                    mm(ps[ob][0:(65 if hh == 0 else 128), q0:512], Vs[:, pr, kb, vsl], pt[:, q0:512], kb == 0, kb == nk - 1,
                       [("Vs", kb), ("VsB", kb), "Vs_c", "Vs_c0", ptok], [("ps", ob)], skip=True)
                    if kb != nk - 1:
                        return
                    P.op("dve", lambda e: e.reciprocal(out=rl[fin][lrow:lrow + 1, :], in_=ps[ob][lrow:lrow + 1, :]),
                         reads=[("ps", ob)], writes=[("rl", fin)])
                    split(rl[fin][lrow:lrow + 1, :], ("rl", fin), [rlp[fin][i][lrow:lrow + 1, :] for i in range(2)],
                          [rlr[fin][lrow:lrow + 1, :]], ("rlp", fin))

                    def fin2():
                        for i in range(2):
                            mm(ps[7][:], ones_b[lrow:lrow + 1, :], rlp[fin][i][lrow:lrow + 1, :], i == 0, i == 1,
                               ["ones_b", ("rlp", fin, i)], [("ps", 7)])
                        P.op("act", lambda e: e.copy(out=bc[fin][r0:r0 + 64, :], in_=ps[7][r0:r0 + 64, :]),
                             reads=[("ps", 7)], writes=[("bc", fin)])
                        P.op("dve", lambda e, c=c: e.tensor_tensor(
                            out=bT[r0:r0 + 64, c, j * 512:(j + 1) * 512], in0=ps[ob][r0:r0 + 64, :],
                            in1=bc[fin][r0:r0 + 64, :], op=ALU.mult),
                            reads=[("ps", ob), ("bc", fin)], writes=[("bT", c, j, hh)])
                    nxt = 4 * (j + 1) + 4 if j + 1 < NT else (4 if hh == 0 else 0)
                    pend.append((step + min(7, max(nxt, 1)), fin2))

                ug0 = ugc[0]
                GB = 4
                for tb in range(0, len(units) + LA, GB):
                    for t in range(tb, tb + GB):
                        if t < len(units):
                            qk_exp(*units[t], ug0 + t)
                    for t in range(tb, tb + GB):
                        if 0 <= t - LA < len(units):
                            pv(*units[t - LA], ug0 + t - LA, t)
                    while pend and pend[0][0] <= tb + GB - 1:
                        pend.pop(0)[1]()
                while pend:
                    pend.pop(0)[1]()
                ugc[0] += len(units)

        if upto == 'B':
            raise _Stop()
        P.barrier()
        Wuv = sb(G0, [128, 8, 2048], BF16)
        o = SCR
        ub = [sb(o + i * 2048, [128, D], BF16) for i in range(4)]; o += 8192
        v32 = [sb(o, [128, D], F32), sb(o + 4096, [128, D], F32)]; o += 8192
        vln = [sb(o, [128, D], BF16), sb(o + 2048, [128, D], BF16)]; o += 4096
        ab = [sb(o, [128, D], BF16), sb(o + 2048, [128, D], BF16)]; o += 4096
        lng = sb(o, [128, D], F32); o += 4096
        lnb = sb(o, [128, D], F32); o += 4096
        WsT = sb(o, [128, 8, 128], BF16); o += 2048
        sqjC = sb(o, [128, D], F32); o += 4096
        for q in range(4):
            P.dma("pool", Wuv[:, :, q * 512:(q + 1) * 512], wsrc(w_in, q * 512, (q + 1) * 512), writes=[("Wuv", q)])
        P.dma("sp", lng[:], ln_g.partition_broadcast(128), writes=["lng"])
        P.dma("sp", lnb[:], ln_b.partition_broadcast(128), writes=["lnb"])
        P.dma("pool", WsT[:], ws_t, writes=["WsT"])
        P.op("pool", lambda e: e.memset(WsT[64:128, :, 0:64], 0.0), reads=["WsT"], writes=["WsT"])

        def C0_(b):
            for ct in range(4):
                bk = ct
                for kc in range(8):
                    mm(ps[bk][:], hT[:, kc, b * 128:(b + 1) * 128], Wuv[:, kc, ct * 512:(ct + 1) * 512], kc == 0, kc == 7,
                       [("hT", b), ("Wuv", ct)], [("ps", bk)])
                if ct < 2:
                    P.op("act", lambda e, b=b, ct=ct, bk=bk: e.activation(
                        out=ub[b % 4][:, ct * 512:(ct + 1) * 512], in_=ps[bk][:], func=AF.Gelu_apprx_tanh),
                        reads=[("ps", bk)], writes=[("ub", b % 4, ct)])
                else:
                    P.op("act", lambda e, b=b, ct=ct, bk=bk: e.activation(
                        out=v32[b % 2][:, (ct - 2) * 512:(ct - 1) * 512], in_=ps[bk][:], func=AF.Gelu_apprx_tanh,
                        accum_out=lnst[:, b, ct - 2:ct - 1]),
                        reads=[("ps", bk), "lnst"], writes=[("v32", b % 2, ct - 2), ("lnst", b, ct - 2)])

        def C1_(b):
            s = b % 2
            L = lambda i: lnst[:, b, i:i + 1]
            lt = ("lnstb", b)
            P.op("act", lambda e: e.activation(out=sqjC[:], in_=v32[s][:], func=AF.Square, accum_out=L(2)),
                 reads=[("v32", s, 0), ("v32", s, 1), "lnst"], writes=["sqjC", lt])
            P.op("dve", lambda e: e.tensor_tensor(out=L(0), in0=L(0), in1=L(1), op=ALU.add),
                 reads=[("lnst", b, 0), ("lnst", b, 1)], writes=[("lnst", b, 0)])
            P.op("dve", lambda e: e.tensor_scalar(out=L(0), in0=L(0), scalar1=1.0 / D, scalar2=0.0, op0=ALU.mult, op1=ALU.add),
                 reads=[("lnst", b, 0)], writes=[("lnst", b, 0)])
            P.op("dve", lambda e: e.tensor_tensor(out=L(1), in0=L(0), in1=L(0), op=ALU.mult),
                 reads=[("lnst", b, 0)], writes=[("lnst", b, 1)])
            P.op("dve", lambda e: e.scalar_tensor_tensor(out=L(3), in0=L(2), scalar=1.0 / D, in1=L(1),
                                                          op0=ALU.mult, op1=ALU.subtract),
                 reads=[lt, ("lnst", b, 1)], writes=[lt])
            P.op("dve", lambda e: e.tensor_scalar(out=L(3), in0=L(3), scalar1=EPS, scalar2=0.0, op0=ALU.add, op1=ALU.add),
                 reads=[lt], writes=[lt])
            P.op("pool", lambda e: e.tensor_tensor(out=L(4), in0=L(3), in1=nh[:, 0:1], op=ALU.pow),
                 reads=[lt, "nh"], writes=[lt])
            P.op("dve", lambda e: e.scalar_tensor_tensor(out=L(5), in0=L(0), scalar=-1.0, in1=L(4),
                                                          op0=ALU.mult, op1=ALU.mult),
                 reads=[lt, ("lnst", b, 0)], writes=[lt])
            P.op("dve", lambda e: e.tensor_scalar(out=v32[s][:], in0=v32[s][:], scalar1=L(4), scalar2=L(5),
                                                   op0=ALU.mult, op1=ALU.add),
                 reads=[lt, ("v32", s, 0), ("v32", s, 1)], writes=[("v32", s, 0), ("v32", s, 1)])
            P.op("dve", lambda e: e.tensor_tensor(out=v32[s][:], in0=v32[s][:], in1=lng[:], op=ALU.mult),
                 reads=[("v32", s, 0), ("v32", s, 1), "lng"], writes=[("v32", s, 0), ("v32", s, 1)])
            P.op("dve", lambda e: e.tensor_tensor(out=vln[s][:], in0=v32[s][:], in1=lnb[:], op=ALU.add),
                 reads=[("v32", s, 0), ("v32", s, 1), "lnb"], writes=[("vln", s)])

        def C1b_(b):
            s = b % 2
            for g in range(8):
                bk = 4 + g // 4
                mm(ps[bk][:, (g % 4) * 128:(g % 4 + 1) * 128], WsT[:, g, :], vln[s][:, g * 128:(g + 1) * 128], True, True,
                   ["WsT", ("vln", s)], [("ps", bk)])

        def C2_(b):
            s = b % 2
            for g in range(8):
                bk = 4 + g // 4
                P.op("dve", lambda e, g=g, bk=bk: e.scalar_tensor_tensor(
                    out=ab[s][:, g * 128:(g + 1) * 128], in0=ps[bk][:, (g % 4) * 128:(g % 4 + 1) * 128],
                    scalar=bs[:, g:g + 1], in1=ub[b % 4][:, g * 128:(g + 1) * 128], op0=ALU.add, op1=ALU.mult),
                    reads=[("ps", bk), "bs", ("ub", b % 4, g // 4)], writes=[("ab", s)])
            tr8(ab[s], ("ab", s), 6, aT[:, :, b * 128:(b + 1) * 128], ("aT", b))

        for t in range(NB + 3):
            if 0 <= t - 3 < NB:
                C2_(t - 3)
            if 0 <= t - 2 < NB:
                C1b_(t - 2)
            if 0 <= t - 1 < NB:
                C1_(t - 1)
            if t < NB:
                C0_(t)

        if upto == 'C':
            raise _Stop()
        P.barrier()
        dump("hT", hT[:])
        dump("bT", bT[:])
        dump("aT", aT[:])
        o = SCR
        wsm = [[sb(o + (s * 4 + i) * 2048, [128, 8, 128], BF16) for i in range(4)] for s in range(2)]; o += 16384
        gsb = [[sb(o + (s * 2 + i) * 2048, [128, 512], F32) for i in range(2)] for s in range(2)]; o += 8192
        tsb = [[sb(o + (s * 2 + i) * 2048, [128, 512], F32) for i in range(2)] for s in range(2)]; o += 8192
        Wout = sb(o, [128, 8, D], BF16); o += 16384
        oE = o
        P.dma("pool", Wout[:, :, 0:512], wsrc(w_out, 0, 512), writes=[("Wout", 0)])
        P.dma("pool", Wout[:, :, 512:1024], wsrc(w_out, 512, 1024), writes=[("Wout", 1)])
        step = 0

        def loadD(m):
            srcs = (wsrc(w_in, O_G + m * 128, O_G + (m + 1) * 128),
                    wsrc(w_in, O_G + D + m * 128, O_G + D + (m + 1) * 128),
                    wsrc(w_a, m * 128, (m + 1) * 128), wsrc(w_b, m * 128, (m + 1) * 128))
            for i in range(4):
                P.dma("pool", wsm[m % 2][i][:], srcs[i], writes=[("wsm", m % 2, i)])

        loadD(0)
        for m in range(8):
            s = m % 2
            if m + 1 < 8:
                loadD(m + 1)
            for j in range(NT):
                pb = 4 * (step % 2)
                sg = step % 2
                step += 1
                acts = (hT, hT, aT, bT)
                for i in range(4):
                    for kc in range(8):
                        if i < 2:
                            rd = [("hT", 4 * j + u) for u in range(4)]
                        elif i == 2:
                            rd = [("aT", 4 * j + u) for u in range(4)]
                        else:
                            rd = [("bT", kc, j, 0), ("bT", kc, j, 1)]
                        mm(ps[pb + i][:], wsm[s][i][:, kc, :], acts[i][:, kc, j * 512:(j + 1) * 512], kc == 0, kc == 7,
                           rd + [("wsm", s, i)], [("ps", pb + i)])
                for i in range(2):
                    P.op("act", lambda e, i=i, pb=pb, sg=sg: e.activation(out=gsb[sg][i][:], in_=ps[pb + i][:], func=AF.Sigmoid),
                         reads=[("ps", pb + i)], writes=[("gsb", sg, i)])
                for i in range(2):
                    P.op("dve", lambda e, i=i, pb=pb, sg=sg: e.tensor_tensor(out=tsb[sg][i][:], in0=ps[pb + 2 + i][:],
                                                                           in1=gsb[sg][i][:], op=ALU.mult),
                         reads=[("ps", pb + 2 + i), ("gsb", sg, i)], writes=[("tsb", sg, i)])
                P.op("pool", lambda e, sg=sg, m=m, j=j: e.tensor_tensor(out=g0[:, m, j * 512:(j + 1) * 512], in0=tsb[sg][0][:],
                                                                       in1=tsb[sg][1][:], op=ALU.add),
                     reads=[("tsb", sg, 0), ("tsb", sg, 1)], writes=[("mg", m, j)])

        if upto == 'D':
            raise _Stop()
        P.barrier()
        dump("mg", g0[:])
        o = oE
        xr = [sb(o + i * 4096, [128, D], F32) for i in range(3)]; o += 12288
        hbE = [sb(o + i * 2048, [128, D], BF16) for i in range(3)]; o += 6144
        gvE = sb(o, [128, D], F32); o += 4096
        sqjE = sb(o, [128, D], F32); o += 4096
        assert o <= SCR_END
        P.dma("sp", gvE[:], norm_ffn_g.partition_broadcast(128), writes=["gv"])
        trE = {}
        for t in range(NB + 2):
            if 0 <= t - 2 < NB:
                trE[t - 2]()
            if t >= NB:
                continue
            b = t
            j = b // 4
            P.dma("sp", xr[b % 3][:], x[b * 128:(b + 1) * 128, :], writes=[("xr", b % 3)])
            for ch in range(2):
                bk = 2 * (b % 2) + ch
                for m in range(8):
                    mm(ps[bk][:], g0[:, m, b * 128:(b + 1) * 128], Wout[:, m, ch * 512:(ch + 1) * 512], m == 0, m == 7,
                       [("mg", m, j), ("Wout", ch)], [("ps", bk)])
                P.op("dve", lambda e, b=b, ch=ch, bk=bk: e.tensor_tensor(
                    out=x1[:, b, ch * 512:(ch + 1) * 512], in0=ps[bk][:], in1=xr[b % 3][:, ch * 512:(ch + 1) * 512], op=ALU.add),
                    reads=[("ps", bk), ("xr", b % 3)], writes=[("x1", b, ch)])
            trE[b] = norm_block(1, b, x1[:, b, :], gvE[:], hbE[b % 3][:], sqjE[:], 4 + 2 * (b % 2),
                                h2T[:, :, b * 128:(b + 1) * 128], ("x1", b, 0), ("hbE", b % 3), ("h2T", b),
                                extra_tok=[("x1", b, 1)], defer=True)

        if upto == 'E':
            raise _Stop()
        P.barrier()
        dump("x1", x1[:])
        NG = 11
        actT = sb(SCR, [128, NG, S], BF16)
        Wd = sb(SCR + 45056, [128, NG, D], BF16)
        o = SCR + 67584
        Ag = [sb(o, [128, 512], F32), sb(o + 2048, [128, 512], F32)]; o += 4096
        Al = [sb(o, [128, 512], F32), sb(o + 2048, [128, 512], F32)]; o += 4096
        assert o <= SCR_END
        o = G0
        NW = 3
        wup = [[sb(o + (s * 2 + i) * 2048, [128, 8, 128], BF16) for i in range(2)] for s in range(NW)]; o += NW * 4096
        NR = 3
        Rg = [sb(o + s * 2080, [128, 514], F32) for s in range(NR)]; o += NR * 2080
        Rl = [sb(o + s * 2080, [128, 514], F32) for s in range(NR)]; o += NR * 2080
        Gg = [sb(o, [128, 512], F32), sb(o + 2048, [128, 512], F32)]; o += 4096
        assert o <= G0 + 32 * KB
        rstep = 0
        _KF = int(os.environ.get('KDBG_F', '9'))

        def loadF(fc):
            P.dma("pool", wup[fc % NW][0][:], wsrc(w_up, fc * 128, (fc + 1) * 128), writes=[("wup", fc % NW, 0)])
            P.dma("pool", wup[fc % NW][1][:], wsrc(w_up, DFF + fc * 128, DFF + (fc + 1) * 128), writes=[("wup", fc % NW, 1)])

        loadF(0)
        loadF(1)
        for grp in range(2):
            for q in range(2):
                P.dma("pool", Wd[:, :, q * 512:(q + 1) * 512],
                      w_down[grp * NG * 128:(grp + 1) * NG * 128, q * 512:(q + 1) * 512].rearrange("(c p) n -> p c n", p=128),
                      writes=[("Wd", q)])
            for fl in range(NG):
                fc = grp * NG + fl
                ws_ = fc % NW
                if fc + 2 < NFC:
                    loadF(fc + 2)
                for j in range(NT):
                    rs = rstep % NR
                    rp = (rstep - 1) % NR
                    sa = rstep % 2
                    pb = 2 * (rstep % 2)
                    rstep += 1
                    for i in range(2):
                        for kc in range(8):
                            mm(ps[pb + i][:], wup[ws_][i][:, kc, :], h2T[:, kc, j * 512:(j + 1) * 512], kc == 0, kc == 7,
                               [("h2T", 4 * j + u) for u in range(4)] + [("wup", ws_, i)], [("ps", pb + i)])
                    for i, (R, A, ci) in enumerate(((Rg, Ag, fc), (Rl, Al, NFC + fc))):
                        rt = ("R", i, rs)
                        P.op("act", lambda e, R=R, i=i, pb=pb, rs=rs: e.copy(out=R[rs][:, 2:514], in_=ps[pb + i][:]),
                             reads=[("ps", pb + i)], writes=[rt])
                        rn = (rs + 1) % NR
                        if j == 0:
                            P.op("dve", lambda e, R=R, rs=rs: e.memset(R[rs][:, 0:2], 0.0), writes=[("Rh", i, rs)])
                        if j < NT - 1:
                            P.op("act", lambda e, R=R, i=i, pb=pb, rn=rn: e.copy(out=R[rn][:, 0:2], in_=ps[pb + i][:, 510:512]),
                                 reads=[("ps", pb + i)], writes=[("Rh", i, rn)])
                        at = ("A", i, sa)
                        if _KF < 2:
                            continue
                        if True:
                            P.op("act", lambda e, A=A, pb=pb, i=i, sa=sa, ci=ci: e.activation(
                                out=A[sa][:], in_=ps[pb + i][:], func=AF.Identity, scale=cw[:, ci, 2:3], bias=cb[:, ci:ci + 1]),
                                reads=[("ps", pb + i), "cw", "cb"], writes=[at])
                        else:
                            P.op("dve", lambda e, A=A, R=R, rs=rs, sa=sa, ci=ci: e.tensor_scalar(
                                out=A[sa][:], in0=R[rs][:, 2:514], scalar1=cw[:, ci, 2:3], scalar2=cb[:, ci:ci + 1],
                                op0=ALU.mult, op1=ALU.add), reads=[rt, "cw", "cb"], writes=[at])
                        _f2 = os.environ.get('KDBG_F2', '')
                        if _f2 == 'a':
                            continue
                        w1s, w0s = (slice(1, 513), slice(0, 512)) if _f2 != 'al' else (slice(2, 514), slice(2, 514))
                        P.op("dve", lambda e, A=A, R=R, rs=rs, sa=sa, ci=ci, w1s=w1s: e.scalar_tensor_tensor(
                            out=A[sa][:], in0=R[rs][:, w1s], scalar=cw[:, ci, 1:2], in1=A[sa][:],
                            op0=ALU.mult, op1=ALU.add), reads=[rt, ("Rh", i, rs), at, "cw"], writes=[at])
                        P.op("dve", lambda e, A=A, R=R, rs=rs, sa=sa, ci=ci, w0s=w0s: e.scalar_tensor_tensor(
                            out=A[sa][:], in0=R[rs][:, w0s], scalar=cw[:, ci, 0:1], in1=A[sa][:],
                            op0=ALU.mult, op1=ALU.add), reads=[rt, ("Rh", i, rs), at, "cw"], writes=[at])
                    if _KF < 3:
                        continue
                    P.op("act", lambda e, sa=sa: e.activation(out=Gg[sa][:], in_=Ag[sa][:], func=AF.Gelu_apprx_tanh),
                         reads=[("A", 0, sa)], writes=[("Gg", sa)])
                    P.op("pool", lambda e, sa=sa, fl=fl, j=j: e.tensor_tensor(
                        out=actT[:, fl, j * 512:(j + 1) * 512], in0=Gg[sa][:], in1=Al[sa][:], op=ALU.mult),
                        reads=[("Gg", sa), ("A", 1, sa)], writes=[("actT", fl, j)])
            for b in range(NB if _KF >= 4 else 0):
                j = b // 4
                for ch in range(2):
                    bk = 4 + 2 * (b % 2) + ch
                    for fl in range(NG):
                        mm(ps[bk][:], actT[:, fl, b * 128:(b + 1) * 128], Wd[:, fl, ch * 512:(ch + 1) * 512],
                           fl == 0, fl == NG - 1, [("actT", fl, j), ("Wd", ch)], [("ps", bk)])
                    P.op("dve", lambda e, b=b, ch=ch, bk=bk: e.tensor_tensor(
                        out=x1[:, b, ch * 512:(ch + 1) * 512], in0=ps[bk][:], in1=x1[:, b, ch * 512:(ch + 1) * 512], op=ALU.add),
                        reads=[("ps", bk), ("x1", b, ch)], writes=[("x1", b, ch)])

        if upto == 'F':
            raise _Stop()
        P.barrier()
        dump("x2", x1[:])
        o = SCR
        Wg = sb(o, [128, 8, D], BF16); o += 16384
        Wp = sb(o, [128, 2, D], BF16); o += 4096
        pst = [sb(o, [128, PLE], F32), sb(o + 1024, [128, PLE], F32)]; o += 2048
        pb16 = [sb(o, [128, PLE], BF16), sb(o + 512, [128, PLE], BF16)]; o += 1024
        pT_all = sb(o, [128, 2, S], BF16); o += 8192
        hbG = [sb(o, [128, D], BF16), sb(o + 2048, [128, D], BF16)]; o += 4096
        h3Tb = [sb(o, [128, 8, 128], BF16), sb(o + 2048, [128, 8, 128], BF16)]; o += 4096
        sgt = [[sb(o + (s * 2 + i) * 2048, [128, 512], F32) for i in range(2)] for s in range(2)]; o += 8192
        tt = [sb(o, [128, D], F32), sb(o + 4096, [128, D], F32)]; o += 8192
        ot = [sb(o, [128, D], F32), sb(o + 4096, [128, D], F32)]; o += 8192
        gv3 = sb(o, [128, D], F32); o += 4096
        gvf = sb(o, [128, D], F32); o += 4096
        assert o <= SCR_END
        P.dma("pool", Wg[:, :, 0:512], wsrc(w_pg, 0, 512), writes=[("Wg", 0)])
        P.dma("pool", Wg[:, :, 512:1024], wsrc(w_pg, 512, 1024), writes=[("Wg", 1)])
        P.dma("pool", Wp[:], wsrc(w_ple, 0, D), writes=["Wp"])
        P.dma("sp", gv3[:], norm_ple_g.partition_broadcast(128), writes=["gv"])
        P.dma("sp", gvf[:], norm_final_g.partition_broadcast(128), writes=["gvf"])

        trG = {}
        for b in range(NB):
            s = b % 2
            P.dma("sp", pst[s][:], pin[b * 128:(b + 1) * 128, :], writes=[("pst", s)])
            P.op("act", lambda e, s=s: e.copy(out=pb16[s][:], in_=pst[s][:]), reads=[("pst", s)], writes=[("pb16", s)])
            tr8(pb16[s], ("pb16", s), 4 + s, pT_all[:, :, b * 128:(b + 1) * 128], ("pT", b), nch=2)

        def G0_(b):
            s = b % 2
            trG[b] = norm_block(2, b, x1[:, b, :], gv3[:], hbG[s][:], None, 0, h3Tb[s][:],
                                ("x1", b, 0), ("hbG", s), ("h3Tb", s), extra_tok=[("x1", b, 1)], defer=True)

        def G0b_(b):
            s = b % 2
            trG[b]()

        def G1_(b):
            s = b % 2
            for ch in range(2):
                bk = 2 + 2 * s + ch
                for kc in range(8):
                    mm(ps[bk][:], h3Tb[s][:, kc, :], Wg[:, kc, ch * 512:(ch + 1) * 512], kc == 0, kc == 7,
                       [("h3Tb", s), ("Wg", ch)], [("ps", bk)])
                P.op("act", lambda e, ch=ch, bk=bk: e.activation(out=sgt[s][ch][:], in_=ps[bk][:], func=AF.Sigmoid),
                     reads=[("ps", bk)], writes=[("sgt", s, ch)])
                bk2 = 6 + ch
                for kc in range(2):
                    mm(ps[bk2][:], pT_all[:, kc, b * 128:(b + 1) * 128], Wp[:, kc, ch * 512:(ch + 1) * 512], kc == 0, kc == 1,
                       [("pT", b), "Wp"], [("ps", bk2)])
                P.op("dve", lambda e, ch=ch, bk2=bk2: e.tensor_tensor(
                    out=tt[s][:, ch * 512:(ch + 1) * 512], in0=ps[bk2][:], in1=sgt[s][ch][:], op=ALU.mult),
                    reads=[("ps", bk2), ("sgt", s, ch)], writes=[("tt", s, ch)])
                P.op("dve", lambda e, ch=ch: e.tensor_tensor(
                    out=tt[s][:, ch * 512:(ch + 1) * 512], in0=tt[s][:, ch * 512:(ch + 1) * 512],
                    in1=x1[:, b, ch * 512:(ch + 1) * 512], op=ALU.add),
                    reads=[("tt", s, ch), ("x1", b, ch)], writes=[("tt", s, ch)])
            ssc = stat[:, 3, 0, b:b + 1]
            msc = stat[:, 3, 1, b:b + 1]
            rsc = stat[:, 3, 2, b:b + 1]
            st = ("stat", 3, b)
            P.op("act", lambda e: e.activation(out=ot[s][:], in_=tt[s][:], func=AF.Square, accum_out=ssc),
                 reads=[("tt", s, 0), ("tt", s, 1), "stat"], writes=[("ot", s), st])
            P.op("dve", lambda e: e.tensor_scalar(out=msc, in0=ssc, scalar1=1.0 / D, scalar2=EPS, op0=ALU.mult, op1=ALU.add),
                 reads=[st], writes=[st])
            P.op("pool", lambda e: e.tensor_tensor(out=rsc, in0=msc, in1=nh[:, 0:1], op=ALU.pow), reads=[st, "nh"], writes=[st])
            P.op("dve", lambda e: e.scalar_tensor_tensor(out=ot[s][:], in0=tt[s][:], scalar=rsc, in1=gvf[:],
                                                          op0=ALU.mult, op1=ALU.mult),
                 reads=[("tt", s, 0), ("tt", s, 1), st, "gvf"], writes=[("ot", s)])
            P.dma("sp", out[b * 128:(b + 1) * 128, :], ot[s][:], reads=[("ot", s)], writes=[("out", b)])

        for t in range(NB + 2):
            if 0 <= t - 2 < NB:
                G1_(t - 2)
            if 0 <= t - 1 < NB:
                G0b_(t - 1)
            if t < NB:
                G0_(t)
        P.op("sp", lambda e: e.nop(), reads=[("out", b) for b in range(NB)])
        P.barrier()

    except _Stop:
        P.barrier()

    emit(nc, P)
    es.close()
    return nc


_CACHE = {}


def _prep_shared(inp):
    f = np.float32
    d = {}
    d["norm_mix_g"] = np.ascontiguousarray(inp["norm_mix_g"][0], f)
    d["w_in"] = np.ascontiguousarray(inp["w_in"][0], f)
    d["b_f"] = np.ascontiguousarray(inp["b_f"][0], f)
    d["gmlp_ln_g"] = np.ascontiguousarray(inp["gmlp_ln_g"][0], f)
    d["gmlp_ln_b"] = np.ascontiguousarray(inp["gmlp_ln_b"][0], f)
    d["ws_t"] = np.ascontiguousarray(np.transpose(inp["gmlp_w_s"][0], (2, 0, 1)), f)
    d["bs_t"] = np.ascontiguousarray(inp["gmlp_b_s"][0].T, f)
    d["w_branch_a"] = np.ascontiguousarray(inp["w_branch_a"][0], f)
    d["w_branch_b"] = np.ascontiguousarray(inp["w_branch_b"][0], f)
    d["w_out"] = np.ascontiguousarray(inp["w_out"][0], f)
    d["norm_ffn_g"] = np.ascontiguousarray(inp["norm_ffn_g"][0], f)
    d["w_up"] = np.ascontiguousarray(inp["w_up"][0], f)
    cwv = np.asarray(inp["conv_w"][0], f)
    d["cw_t"] = np.ascontiguousarray(cwv.T.reshape(44, 128, 3).transpose(1, 0, 2), f)
    d["cb_t"] = np.ascontiguousarray(np.asarray(inp["conv_b"][0], f).reshape(44, 128).T, f)
    d["w_down"] = np.ascontiguousarray(inp["w_down"][0], f)
    d["norm_ple_g"] = np.ascontiguousarray(inp["norm_ple_g"][0], f)
    d["w_ple"] = np.ascontiguousarray(inp["w_ple"][0], f)
    d["w_ple_gate"] = np.ascontiguousarray(inp["w_ple_gate"][0], f)
    d["norm_final_g"] = np.ascontiguousarray(inp["norm_final_g"], f)
    d["c_ident"] = np.eye(128, dtype=f)
    r = np.arange(128)
    d["c_tri"] = (r[:, None] <= r[None, :]).astype(f)
    d["c_sel"] = np.ones((128, 128), f)
    return d


def kernel(**inputs):
    inp = {k: np.asarray(v) for k, v in inputs.items()}
    n = 8
    if "nc" not in _CACHE:
        _CACHE["nc"] = build_nc()
    nc = _CACHE["nc"]
    shared = _prep_shared(inp)
    x = np.asarray(inp["x"], np.float32)
    p = np.asarray(inp["p"], np.float32)[0]
    in_maps = []
    for i in range(n):
        m = dict(shared)
        m["x"] = np.ascontiguousarray(x[i])
        m["p"] = np.ascontiguousarray(p[i])
        in_maps.append(m)
    res = run_bass_kernel_spmd(nc, in_maps, core_ids=list(range(n)))
    return np.stack([np.asarray(r["out"], np.float32) for r in res.results], axis=0)
```
